# Optimizing a Trainium2 kernel written in Bass

```python
import math
import jax, jax.numpy as jnp
from jax import lax
import numpy as np

D_MODEL = 2048
BATCH = 4
SEQ = 4096
DEPTH = 4

CHUNK = 64
N_MIXERS = 4
CONV_KERNEL = 31
LN_EPS = 1e-5
RWKV_HEAD_DIM = 64
RWKV_HEADS = D_MODEL // RWKV_HEAD_DIM
DECAY_LORA = 96
AAA_LORA = 96
GATE_LORA = 256
GN_EPS = 64e-5
SHORT_KERNEL = 3
FOX_HEAD_DIM = 128
FOX_HEADS = D_MODEL // FOX_HEAD_DIM
Q_BLOCK = 128
D_FF = 4 * D_MODEL
RMS_EPS = 1e-6

kernel_name = "hybrid_conformer_rwkv7_shortconv_fox_trunk"


def _n_layers_of(m):
    return (DEPTH - m + N_MIXERS - 1) // N_MIXERS


def rmsnorm(x, g):
    xf = x.astype(jnp.float32)
    y = xf * lax.rsqrt(jnp.mean(xf * xf, axis=-1, keepdims=True) + RMS_EPS)
    return (y * g.astype(jnp.float32)).astype(x.dtype)


def layernorm(x, g, b):
    xf = x.astype(jnp.float32)
    mu = jnp.mean(xf, axis=-1, keepdims=True)
    var = jnp.mean(jnp.square(xf - mu), axis=-1, keepdims=True)
    y = (xf - mu) * lax.rsqrt(var + LN_EPS)
    return (y * g.astype(jnp.float32) + b.astype(jnp.float32)).astype(x.dtype)


def causal_dwconv(x, w):
    K = w.shape[0]
    return lax.conv_general_dilated(
        x, w[:, None, :].astype(x.dtype), window_strides=(1,), padding=[(K - 1, 0)],
        dimension_numbers=('NWC', 'WIO', 'NWC'), feature_group_count=x.shape[-1])


def conformer_conv(h, w_in, b_in, dw, dw_b, ln_g, ln_b, w_out, b_out):
    u = h @ w_in + b_in
    val, gate = jnp.split(u, 2, axis=-1)
    u = val * jax.nn.sigmoid(gate)
    u = causal_dwconv(u, dw) + dw_b
    u = jax.nn.silu(layernorm(u, ln_g, ln_b))
    return u @ w_out + b_out


def rwkv7_time_mix(h, mu, w_rkv, w0, w1, w2, a0, a1, a2, g1, g2, k_k, k_a, r_k, gn_g, gn_b, w_o):
    B, T, D = h.shape
    H, N = RWKV_HEADS, RWKV_HEAD_DIM
    f32 = jnp.float32
    xx = jnp.pad(h, ((0, 0), (1, 0), (0, 0)))[:, :T] - h
    xr, xw, xk, xv, xa, xg = (h + xx * mu[i] for i in range(6))
    r = xr @ w_rkv[0]
    k = xk @ w_rkv[1]
    v = xv @ w_rkv[2]
    w = -jax.nn.softplus(-(w0 + jnp.tanh(xw @ w1) @ w2).astype(f32)) - 0.5
    a = jax.nn.sigmoid(a0 + (xa @ a1) @ a2)
    g = jax.nn.sigmoid(xg @ g1) @ g2
    kk = (k * k_k).astype(f32).reshape(B, T, H, N)
    kk = kk / jnp.maximum(jnp.sqrt(jnp.sum(kk * kk, axis=-1, keepdims=True)), 1e-12)
    k = k * (1 + (a - 1) * k_a)

    def heads(t):
        return t.astype(f32).reshape(B, T, H, N)

    r_h, k_h, v_h, a_h = heads(r), heads(k), heads(v), heads(a)
    decay = jnp.exp(-jnp.exp(w)).reshape(B, T, H, N)

    def to_chunks(t):
        return t.transpose(1, 0, 2, 3).reshape(T // CHUNK, CHUNK, B, H, N)

    xs = tuple(to_chunks(t) for t in (r_h, decay, k_h, v_h, -kk, kk * a_h))

    def frame_step(S, inp):
        r_t, w_t, k_t, v_t, a_t, b_t = inp
        sa = jnp.einsum('bhij,bhj->bhi', S, a_t)
        S = S * w_t[:, :, None, :] + sa[..., None] * b_t[:, :, None, :] + v_t[..., None] * k_t[:, :, None, :]
        y = jnp.einsum('bhij,bhj->bhi', S, r_t)
        return S, y

    def chunk_step(S, chunk):
        return lax.scan(frame_step, S, chunk)

    S0 = jnp.zeros((B, H, N, N), f32)
    _, y = lax.scan(chunk_step, S0, xs)
    y = y.reshape(T, B, H, N).transpose(1, 0, 2, 3)
    mean = jnp.mean(y, axis=-1, keepdims=True)
    var = jnp.mean(jnp.square(y - mean), axis=-1, keepdims=True)
    y = ((y - mean) * lax.rsqrt(var + GN_EPS)).reshape(B, T, D) * gn_g.astype(f32) + gn_b.astype(f32)
    bonus = (jnp.sum(r_h * k_h * r_k.astype(f32), axis=-1, keepdims=True) * v_h).reshape(B, T, D)
    y = (y + bonus).astype(h.dtype)
    return (y * g) @ w_o


def short_conv(h, w_in, conv_w, w_out):
    u = h @ w_in
    gate_b, gate_c, hv = jnp.split(u, 3, axis=-1)
    z = causal_dwconv(gate_c * hv, conv_w)
    return (gate_b * z) @ w_out


def fox_attention(h, w_qkvf, b_f, w_o):
    B, T, D = h.shape
    H, Dh = FOX_HEADS, FOX_HEAD_DIM
    f32 = jnp.float32
    u = h @ w_qkvf

    def heads(t):
        return t.reshape(B, T, H, Dh).transpose(0, 2, 1, 3)

    q, k, v = heads(u[..., :D]), heads(u[..., D:2 * D]), heads(u[..., 2 * D:3 * D])
    log_f = jax.nn.log_sigmoid((u[..., 3 * D:] + b_f).astype(f32))
    c = jnp.cumsum(log_f, axis=1).transpose(0, 2, 1)
    scale = 1.0 / math.sqrt(Dh)
    outs = []
    for blk in range(T // Q_BLOCK):
        q0 = blk * Q_BLOCK
        q1 = q0 + Q_BLOCK
        s = jnp.einsum('bhqd,bhkd->bhqk', q[:, :, q0:q1], k[:, :, :q1]).astype(f32) * scale
        s = s + c[:, :, q0:q1, None] - c[:, :, None, :q1]
        mask = (q0 + jnp.arange(Q_BLOCK))[:, None] >= jnp.arange(q1)[None, :]
        s = jnp.where(mask, s, -jnp.inf)
        p = jax.nn.softmax(s, axis=-1).astype(v.dtype)
        outs.append(jnp.einsum('bhqk,bhkd->bhqd', p, v[:, :, :q1]))
    o = jnp.concatenate(outs, axis=2).transpose(0, 2, 1, 3).reshape(B, T, D)
    return o @ w_o


def sq_relu_mlp(h, w1, w2):
    return jnp.square(jax.nn.relu(h @ w1)) @ w2


def setup_inputs(seed: int = 0) -> dict:
    key = jax.random.key(seed)
    ks = iter(jax.random.split(key, 48))
    D = D_MODEL
    nA, nB, nC, nD = (_n_layers_of(m) for m in range(N_MIXERS))
    H, N = RWKV_HEADS, RWKV_HEAD_DIM

    def nrm(shape, scale):
        return jax.random.normal(next(ks), shape, jnp.float32) * scale

    def gain(shape):
        return 1.0 + nrm(shape, 0.02)

    inp = {}
    inp["x"] = nrm((BATCH, SEQ, D), 1.0)
    inp["norm1_g"] = gain((DEPTH, D))
    inp["norm2_g"] = gain((DEPTH, D))
    inp["mlp_w1"] = nrm((DEPTH, D, D_FF), D ** -0.5)
    inp["mlp_w2"] = nrm((DEPTH, D_FF, D), D_FF ** -0.5)
    inp["cc_w_in"] = nrm((nA, D, 2 * D), D ** -0.5)
    inp["cc_b_in"] = nrm((nA, 2 * D), 0.02)
    inp["cc_dw"] = nrm((nA, CONV_KERNEL, D), CONV_KERNEL ** -0.5)
    inp["cc_dw_b"] = nrm((nA, D), 0.02)
    inp["cc_ln_g"] = gain((nA, D))
    inp["cc_ln_b"] = nrm((nA, D), 0.02)
    inp["cc_w_out"] = nrm((nA, D, D), D ** -0.5)
    inp["cc_b_out"] = nrm((nA, D), 0.02)
    inp["rw_mu"] = jax.random.uniform(next(ks), (nB, 6, D), jnp.float32)
    inp["rw_w_rkv"] = nrm((nB, 3, D, D), D ** -0.5)
    inp["rw_w0"] = jax.random.uniform(next(ks), (nB, D), jnp.float32, -6.0, -1.0)
    inp["rw_w1"] = nrm((nB, D, DECAY_LORA), D ** -0.5)
    inp["rw_w2"] = nrm((nB, DECAY_LORA, D), 0.5 * DECAY_LORA ** -0.5)
    inp["rw_a0"] = nrm((nB, D), 0.02)
    inp["rw_a1"] = nrm((nB, D, AAA_LORA), D ** -0.5)
    inp["rw_a2"] = nrm((nB, AAA_LORA, D), AAA_LORA ** -0.5)
    inp["rw_g1"] = nrm((nB, D, GATE_LORA), D ** -0.5)
    inp["rw_g2"] = nrm((nB, GATE_LORA, D), GATE_LORA ** -0.5)
    inp["rw_k_k"] = 0.85 + nrm((nB, D), 0.1)
    inp["rw_k_a"] = 1.0 + nrm((nB, D), 0.1)
    inp["rw_r_k"] = nrm((nB, H, N), 0.1)
    inp["rw_gn_g"] = gain((nB, D))
    inp["rw_gn_b"] = nrm((nB, D), 0.02)
    inp["rw_w_o"] = nrm((nB, D, D), D ** -0.5)
    inp["sc_w_in"] = nrm((nC, D, 3 * D), D ** -0.5)
    inp["sc_conv_w"] = nrm((nC, SHORT_KERNEL, D), SHORT_KERNEL ** -0.5)
    inp["sc_w_out"] = nrm((nC, D, D), D ** -0.5)
    w_qkv = nrm((nD, D, 3 * D), D ** -0.5)
    w_f = nrm((nD, D, FOX_HEADS), 0.1 * D ** -0.5)
    inp["fx_w_qkvf"] = jnp.concatenate([w_qkv, w_f], axis=-1)
    inp["fx_b_f"] = jax.random.uniform(next(ks), (nD, FOX_HEADS), jnp.float32, 1.0, 5.0)
    inp["fx_w_o"] = nrm((nD, D, D), D ** -0.5)
    inp["final_g"] = gain((D,))
    return inp


def reference(x, norm1_g, norm2_g, mlp_w1, mlp_w2,
              cc_w_in, cc_b_in, cc_dw, cc_dw_b, cc_ln_g, cc_ln_b, cc_w_out, cc_b_out,
              rw_mu, rw_w_rkv, rw_w0, rw_w1, rw_w2, rw_a0, rw_a1, rw_a2, rw_g1, rw_g2,
              rw_k_k, rw_k_a, rw_r_k, rw_gn_g, rw_gn_b, rw_w_o,
              sc_w_in, sc_conv_w, sc_w_out,
              fx_w_qkvf, fx_b_f, fx_w_o, final_g):
    for i in range(DEPTH):
        m, j = i % N_MIXERS, i // N_MIXERS
        h = rmsnorm(x, norm1_g[i])
        if m == 0:
            y = conformer_conv(h, cc_w_in[j], cc_b_in[j], cc_dw[j], cc_dw_b[j],
                               cc_ln_g[j], cc_ln_b[j], cc_w_out[j], cc_b_out[j])
        elif m == 1:
            y = rwkv7_time_mix(h, rw_mu[j], rw_w_rkv[j], rw_w0[j], rw_w1[j], rw_w2[j],
                               rw_a0[j], rw_a1[j], rw_a2[j], rw_g1[j], rw_g2[j],
                               rw_k_k[j], rw_k_a[j], rw_r_k[j], rw_gn_g[j], rw_gn_b[j], rw_w_o[j])
        elif m == 2:
            y = short_conv(h, sc_w_in[j], sc_conv_w[j], sc_w_out[j])
        else:
            y = fox_attention(h, fx_w_qkvf[j], fx_b_f[j], fx_w_o[j])
        x = x + y
        x = x + sq_relu_mlp(rmsnorm(x, norm2_g[i]), mlp_w1[i], mlp_w2[i])
    return rmsnorm(x, final_g)
```

```python
import contextlib
import numpy as np
import concourse.bass as bass
import concourse.mybir as mybir
from concourse.bass_utils import run_bass_kernel_spmd

F32 = mybir.dt.float32
BF16 = mybir.dt.bfloat16
AF = mybir.ActivationFunctionType
ALU = mybir.AluOpType
AX = mybir.AxisListType

D = 2048
DC = 16
DFF = 8192
B = 4
T = 4096
NCORES = 8


class Buf:
    __slots__ = ("name", "w", "r", "dsem")

    def __init__(self, name):
        self.name = name
        self.w = None
        self.r = {}
        self.dsem = None


class Prog:
    ENGS = ("pe", "act", "dve", "pool", "sp")

    def __init__(self, nc):
        self.nc = nc
        self.ops = {e: [] for e in self.ENGS}
        self.seen = {e: {} for e in self.ENGS}
        self.ndsem = 0
        self.dcount = []
        self.free_dsems = []
        self.dkind = []

    def _dsem(self, buf, eng="sp"):
        if buf.dsem is None:
            kind = "sw" if eng == "pool" else "hw"
            fl = [i for i in self.free_dsems if self.dkind[i] == kind]
            if fl:
                buf.dsem = fl[-1]
                self.free_dsems.remove(fl[-1])
            else:
                buf.dsem = self.ndsem
                self.ndsem += 1
                self.dcount.append(0)
                self.dkind.append(kind)
        return buf.dsem

    def barrier(self):
        last = {}
        for e in self.ENGS:
            for i in range(len(self.ops[e]) - 1, -1, -1):
                o = self.ops[e][i]
                if o["fn"] is not None and o["dma"] is None:
                    last[e] = ("eng", e, i)
                    break
        for e in self.ENGS:
            deps = [tok for e2, tok in last.items() if e2 != e]
            deps += [("dma", s_, c) for s_, c in enumerate(self.dcount) if c > 0]
            idx = len(self.ops[e])
            waits = self._waits(e, idx, deps)
            self.ops[e].append({"waits": waits, "fn": None, "sig": False, "dma": None})
        self.free_dsems = list(range(self.ndsem))

    def _waits(self, eng, idx, deps):
        waits = []
        seen = self.seen[eng]
        for d in deps:
            if d[0] == "eng":
                _, e2, i2 = d
                if e2 == eng:
                    if eng in ("pe", "sp"):
                        continue
                key = ("eng", e2)
                if seen.get(key, -1) >= i2:
                    continue
                seen[key] = i2
                self.ops[e2][i2]["sig"] = True
                waits.append(d)
            else:
                _, s, c = d
                key = ("dma", s)
                if seen.get(key, -1) >= c:
                    continue
                seen[key] = c
                waits.append(d)
        return waits

    def _deps(self, reads, writes):
        deps = []
        for b in reads:
            if b.w is not None:
                deps.append(b.w)
        for b in writes:
            if b.w is not None:
                deps.append(b.w)
            deps.extend(b.r.values())
        return deps

    def op(self, eng, fn, reads=(), writes=()):
        idx = len(self.ops[eng])
        waits = self._waits(eng, idx, self._deps(reads, writes))
        self.ops[eng].append({"waits": waits, "fn": fn, "sig": False, "dma": None})
        tok = ("eng", eng, idx)
        for b in reads:
            b.r[("eng", eng)] = tok
        for b in writes:
            b.w = tok
            b.r = {}
        return tok

    def dma(self, eng, out, in_, reads=(), writes=(), sembuf=None, **kw):
        idx = len(self.ops[eng])
        waits = self._waits(eng, idx, self._deps(reads, writes))
        sb = sembuf or (writes[0] if writes else reads[0])
        s = self._dsem(sb, eng)
        self.dcount[s] += 16
        tok = ("dma", s, self.dcount[s])
        self.ops[eng].append({"waits": waits, "fn": lambda e: e.dma_start(out=out, in_=in_, **kw),
                              "sig": False, "dma": s})
        for b in reads:
            b.r[("dma", s)] = tok
        for b in writes:
            b.w = tok
            b.r = {}
        return tok

    def finish(self, bufs):
        deps = []
        for b in bufs:
            if b.w is not None:
                deps.append(b.w)
            deps.extend(b.r.values())
        idx = len(self.ops["sp"])
        waits = self._waits("sp", idx, deps)
        self.ops["sp"].append({"waits": waits, "fn": None, "sig": False, "dma": None})

    def emit(self):
        nc = self.nc
        with contextlib.ExitStack() as st:
            esem = {e: st.enter_context(nc.semaphore("s_" + e)) for e in self.ENGS}
            dsem = [st.enter_context(nc.semaphore("d%d" % i)) for i in range(self.ndsem)]
            cnt = {}
            for e in self.ENGS:
                c = 0
                arr = []
                for o in self.ops[e]:
                    if o["sig"]:
                        c += 1
                    arr.append(c)
                cnt[e] = arr
            block = st.enter_context(nc.Block())

            def run(e, eng):
                for o in self.ops[e]:
                    for w in o["waits"]:
                        if w[0] == "eng":
                            eng.wait_ge(esem[w[1]], cnt[w[1]][w[2]])
                        else:
                            eng.wait_ge(dsem[w[1]], w[2])
                    if o["fn"] is None:
                        continue
                    ins = o["fn"](eng)
                    if o["dma"] is not None:
                        ins.then_inc(dsem[o["dma"]], 16)
                    elif o["sig"]:
                        ins.then_inc(esem[e], 1)

            @block.tensor
            def _(eng):
                run("pe", eng)

            @block.scalar
            def _(eng):
                run("act", eng)

            @block.vector
            def _(eng):
                run("dve", eng)

            @block.gpsimd
            def _(eng):
                run("pool", eng)

            @block.sync
            def _(eng):
                run("sp", eng)


def _each(rng):
    def deco(fn):
        for x in rng:
            fn(x)
        return fn
    return deco


class Ctx:
    def __init__(self, nc, st):
        self.nc = nc
        self.st = st
        self.P = Prog(nc)
        self.n = 0
        self.psum = []
        self.pi = 0

    def sb(self, shape, dt, name=None):
        self.n += 1
        arena = getattr(self, "arena", None)
        if arena is None:
            return self.st.enter_context(self.nc.sbuf_tensor("sb_" + (name or ("t%d" % self.n)), list(shape), dt))
        shape = list(shape)
        elems = 1
        for d_ in shape[1:]:
            elems *= d_
        esz = 2 if dt == BF16 else 4
        words = (elems * esz + 3) // 4
        words = (words + 7) // 8 * 8
        assert self.off + words <= self.arena_words, ("arena overflow", name, self.off, words)
        v = arena[0:shape[0], self.off:self.off + words]
        self.off += words
        if dt == BF16:
            v = v.bitcast(BF16)
        v = v[:, 0:elems]
        if len(shape) == 3:
            v = v.rearrange("p (a b) -> p a b", a=shape[1])
        elif len(shape) == 4:
            v = v.rearrange("p (a b c) -> p a b c", a=shape[1], b=shape[2])
        return v

    def init_psum(self, nbanks=8):
        for i in range(nbanks):
            t = self.st.enter_context(self.nc.psum_tensor("ps%d" % i, [128, 512], F32))
            self.psum.append((t, Buf("ps%d" % i)))

    def ps(self):
        n = getattr(self, "rot_n", len(self.psum))
        r = self.psum[self.pi % n]
        self.pi += 1
        return r


class Env:
    def __init__(self, nc, st):
        self.nc = nc
        self.st = st
        self.cx = Ctx(nc, st)
        cx = self.cx
        cx.arena_words = 53000
        cx.arena = st.enter_context(nc.sbuf_tensor("arena", [128, cx.arena_words], F32))
        cx.off = 0
        self.psd = [st.enter_context(nc.psum_tensor("psd%d" % i, [128, 1024], F32)) for i in range(4)]
        self.io = {}
        self.first = True

    def dram(self, n, s, k="ExternalInput"):
        ap = self.io[n]
        assert list(ap.shape) == list(s), (n, ap.shape, s)
        return ap

    @contextlib.contextmanager
    def scope(self):
        cx = self.cx
        if not self.first:
            cx.P.barrier()
        self.first = False
        cx.off = 0
        cx.rot_n = 8
        cx.pi = 0
        cx.psum = [(self.psd[i // 2][:, (i % 2) * 512:(i % 2 + 1) * 512], Buf("ps%d" % i)) for i in range(8)]
        yield self.st


class WStream:
    def __init__(self, cx, nslots=3, kc=16, cols=512):
        self.cx = cx
        self.slots = [(cx.sb([128, kc, cols], BF16, "wsl%d" % i), Buf("wsl%d" % i)) for i in range(nslots)]
        self.i = 0
        self.kc = kc
        self.cols = cols

    def load(self, W, k0, kp, nk, c0, ncols):
        t, b = self.slots[self.i % len(self.slots)]
        self.i += 1
        src = W[k0:k0 + nk * kp, c0:c0 + ncols].rearrange("(kc p) m -> p kc m", p=kp)
        self.cx.P.dma("pool", t[0:kp, 0:nk, 0:ncols], src, writes=[b])
        return t, b


def gemm(cx, ws, W, K, M, act, N, epi, kp=128, mgrp=3, mp=128):
    P = cx.P
    nk = K // kp
    nm = M // mp
    kblk = ws.kc
    nkb = (nk + kblk - 1) // kblk
    for m0 in range(0, nm, mgrp):
        mg = min(mgrp, nm - m0)
        pss = [cx.ps() for _ in range(mg)]
        for kb in range(nkb):
            nkc = min(kblk, nk - kb * kblk)
            wt, wb = ws.load(W, kb * kblk * kp, kp, nkc, m0 * mp, mg * mp)
            for mi in range(mg):
                pt, pb = pss[mi]
                for kc in range(nkc):
                    a_ap, a_buf = act(kb * kblk + kc)
                    first = (kb == 0 and kc == 0)
                    last = (kb == nkb - 1 and kc == nkc - 1)
                    P.op("pe", (lambda e, pt=pt, wt=wt, kc=kc, mi=mi, a_ap=a_ap, first=first, last=last:
                                e.matmul(pt[0:mp, 0:N], wt[0:kp, kc, mi * mp:(mi + 1) * mp], a_ap,
                                         start=first, stop=last)),
                         reads=[wb, a_buf], writes=[pb])
        for mi in range(mg):
            pt, pb = pss[mi]
            epi(m0 + mi, pt[0:mp, 0:N], pb)


NTOK = 2048
NT = 512
NH = 32
RMS_EPS = 1e-6
LN_EPS = 1e-5


def pack_vec(v):
    v = np.ascontiguousarray(np.asarray(v, dtype=np.float32).reshape(-1))
    return np.ascontiguousarray(v.reshape(-1, 128).T)


def build_tok(mixer, final, ntok=NTOK, env=None):
    nc = env.nc if env else bass.Bass("TRN2", target_bir_lowering=False)
    dram = env.dram if env else (lambda n, s, k="ExternalInput": nc.dram_tensor(n, list(s), F32, kind=k).ap())
    xT = dram("xT", [D, ntok])
    out = dram("out", [D, ntok], "ExternalOutput")
    w1 = dram("w1", [D, DFF])
    w2 = dram("w2", [DFF, D])
    o_eps = 48
    VB = 50
    if mixer == "cc":
        xh = dram("xh", [D, NH])
        w_in = dram("w_in", [D, 2 * D])
        w_out = dram("w_out", [D, D])
        o_bin, o_dw, o_dwb, o_lng, o_lnb, o_bout = VB, VB + 32, VB + 32 + 496, VB + 544, VB + 560, VB + 576
        o_mask = VB + 592
        NV = VB + 593
        CARRY = 30
    elif mixer == "sc":
        xh = dram("xh", [D, NH])
        w_in = dram("w_in", [D, 3 * D])
        w_out = dram("w_out", [D, D])
        o_cw = VB
        o_mask = VB + 48
        NV = VB + 49
        CARRY = 2
    else:
        mT = dram("mT", [D, ntok])
        w_out = dram("w_out", [D, D])
        NV = VB
    vec_d = dram("vec", [128, NV])

    with (env.scope() if env else contextlib.ExitStack()) as st:
        cx = env.cx if env else Ctx(nc, st)
        P = cx.P
        if env is None:
            cx.init_psum(8)
        ws = WStream(cx, nslots=2, cols=384)
        vec = cx.sb([128, NV], F32, "vec")
        vec_b = Buf("vec")
        ones = cx.sb([128, 128], BF16, "ones")
        ones_b = Buf("ones")
        xs = [(cx.sb([128, NT], F32, "x%d" % c), Buf("x%d" % c)) for c in range(DC)]
        hs = [(cx.sb([128, NT], BF16, "h%d" % c), Buf("h%d" % c)) for c in range(DC)]
        qs = [(cx.sb([128, NT], BF16, "q%d" % c), Buf("q%d" % c)) for c in range(DC)]
        BIGW = 1568
        bigs = [(cx.sb([128, BIGW], F32, "big%d" % c), Buf("big%d" % c)) for c in range(DC)]
        sq = [(cx.sb([128, NT], BF16, "sq%d" % i), Buf("sq%d" % i)) for i in range(2)]
        tmpb = [(cx.sb([128, NT], BF16, "tb%d" % i), Buf("tb%d" % i)) for i in range(2)]
        tmpf = [(cx.sb([128, NT], F32, "tf%d" % i), Buf("tf%d" % i)) for i in range(3)]
        rstd = (cx.sb([128, NT], F32, "rstd"), Buf("rstd"))
        mean = (cx.sb([128, NT], F32, "mean"), Buf("mean"))
        cnt = {"sq": 0, "tb": 0, "tf": 0, "ev": 0}

        def rot(lst, key):
            r = lst[cnt[key] % len(lst)]
            cnt[key] += 1
            return r

        def a_view(k):
            t, b = bigs[k // 4]
            return t[:, 0:1024].bitcast(BF16)[:, (k % 4) * NT:(k % 4 + 1) * NT], b

        P.dma("sp", vec[:, :], vec_d[:, :], writes=[vec_b])
        P.op("dve", lambda e: e.memset(ones[:, :], 1.0), writes=[ones_b])

        def vcol(o, n=1):
            return vec[:, o:o + n]

        def rmsnorm(src, gcol, dst, N, dst_is_x=False):
            pt, pb = cx.ps()
            for c in range(DC):
                s_t, s_b = rot(sq, "sq")
                x_ap, x_b = src[c]
                P.op("act", lambda e, s_t=s_t, x_ap=x_ap: e.activation(out=s_t[:, 0:N], in_=x_ap, func=AF.Square),
                     reads=[x_b], writes=[s_b])
                P.op("pe", lambda e, s_t=s_t, c=c: e.matmul(pt[:, 0:N], ones[:, :], s_t[:, 0:N],
                                                          start=(c == 0), stop=(c == DC - 1)),
                     reads=[s_b, ones_b], writes=[pb])
            r_t, r_b = rstd
            P.op("act", lambda e: e.activation(out=r_t[:, 0:N], in_=pt[:, 0:N], func=AF.Sqrt, scale=1.0 / D,
                                               bias=vcol(o_eps)), reads=[pb, vec_b], writes=[r_b])
            P.op("dve", lambda e: e.reciprocal(out=r_t[:, 0:N], in_=r_t[:, 0:N]), reads=[r_b], writes=[r_b])
            for c in range(DC):
                x_ap, x_b = src[c]
                d_ap, d_b = dst[c]
                eng = "dve"
                P.op(eng, lambda e, x_ap=x_ap, d_ap=d_ap, c=c: e.scalar_tensor_tensor(
                    out=d_ap, in0=x_ap, scalar=vcol(gcol + c), in1=r_t[:, 0:N], op0=ALU.mult, op1=ALU.mult),
                    reads=[x_b, r_b, vec_b], writes=[d_b])

        def evac_engine():
            cnt["ev"] += 1
            return "act" if cnt["ev"] % 2 else "dve"

        def mixer_sc(N, halo):
            def gb(c): return bigs[c][0][:, 0:NT]
            def gc(c): return bigs[c][0][:, 512:512 + NT]
            def pp(c): return bigs[c][0][:, 1024:1024 + CARRY + NT]

            def epi(mc, ps, pb):
                if mc < 16:
                    if halo:
                        return
                    t, b = bigs[mc]
                    P.op("act", lambda e: e.copy(out=gb(mc)[:, 0:N], in_=ps), reads=[pb], writes=[b])
                elif mc < 32:
                    c = mc - 16
                    t, b = bigs[c]
                    P.op("act", lambda e: e.copy(out=gc(c)[:, 0:N], in_=ps), reads=[pb], writes=[b])
                else:
                    c = mc - 32
                    t, b = bigs[c]
                    if halo:
                        f_t, f_b = rot(tmpf, "tf")
                        P.op("dve", lambda e: e.tensor_tensor(out=f_t[:, 0:N], in0=gc(c)[:, 0:N], in1=ps, op=ALU.mult),
                             reads=[pb, b], writes=[f_b])
                        P.op("dve", lambda e: e.tensor_scalar(out=pp(c)[:, 0:CARRY], in0=f_t[:, N - CARRY:N],
                                                              scalar1=vcol(o_mask), scalar2=None, op0=ALU.mult),
                             reads=[f_b, vec_b], writes=[b])
                        return
                    P.op("dve", lambda e: e.tensor_tensor(out=pp(c)[:, CARRY:CARRY + N], in0=gc(c)[:, 0:N], in1=ps,
                                                          op=ALU.mult), reads=[pb, b], writes=[b])
                    z_t, z_b = rot(tmpf, "tf")
                    P.op("dve", lambda e: e.tensor_scalar(out=z_t[:, 0:N], in0=pp(c)[:, 0:N], scalar1=vcol(o_cw + c),
                                                          scalar2=None, op0=ALU.mult), reads=[b, vec_b], writes=[z_b])
                    for k in (1, 2):
                        P.op("dve", lambda e, k=k: e.scalar_tensor_tensor(
                            out=z_t[:, 0:N], in0=pp(c)[:, k:k + N], scalar=vcol(o_cw + 16 * k + c), in1=z_t[:, 0:N],
                            op0=ALU.mult, op1=ALU.add), reads=[b, vec_b, z_b], writes=[z_b])
                    q_t, q_b = qs[c]
                    P.op("dve", lambda e: e.tensor_tensor(out=q_t[:, 0:N], in0=gb(c)[:, 0:N], in1=z_t[:, 0:N],
                                                          op=ALU.mult), reads=[b, z_b], writes=[q_b])
                    P.op("act", lambda e: e.copy(out=pp(c)[:, 0:CARRY], in_=pp(c)[:, N:N + CARRY]),
                         reads=[b], writes=[b])

            gemm(cx, ws, w_in, D, 3 * D, lambda kc: (hs[kc][0][:, 0:N], hs[kc][1]), N, epi)
            if halo:
                return

            def epi_o(mc, ps, pb):
                x_t, x_b = xs[mc]
                P.op("dve", lambda e: e.tensor_tensor(out=x_t[:, 0:N], in0=x_t[:, 0:N], in1=ps, op=ALU.add),
                     reads=[pb, x_b], writes=[x_b])

            gemm(cx, ws, w_out, D, D, lambda kc: (qs[kc][0][:, 0:N], qs[kc][1]), N, epi_o)

        def mixer_cc(N, halo):
            def cv(c): return bigs[c][0][:, 0:NT]
            def uu(c): return bigs[c][0][:, 1024:1024 + CARRY + NT]
            KW = 31

            def epi(mc, ps, pb):
                if mc < 16:
                    t, b = bigs[mc]
                    P.op("act", lambda e: e.activation(out=cv(mc)[:, 0:N], in_=ps, func=AF.Identity,
                                                       bias=vcol(o_bin + mc)),
                         reads=[pb, vec_b], writes=[b])
                else:
                    c = mc - 16
                    t, b = bigs[c]
                    f_t, f_b = rot(tmpf, "tf")
                    P.op("act", lambda e: e.activation(out=f_t[:, 0:N], in_=ps, func=AF.Sigmoid,
                                                       bias=vcol(o_bin + 16 + c)),
                         reads=[pb, vec_b], writes=[f_b])
                    if halo:
                        P.op("dve", lambda e: e.tensor_tensor(out=f_t[:, 0:N], in0=cv(c)[:, 0:N], in1=f_t[:, 0:N],
                                                              op=ALU.mult), reads=[b, f_b], writes=[f_b])
                        P.op("dve", lambda e: e.tensor_scalar(out=uu(c)[:, 0:CARRY], in0=f_t[:, N - CARRY:N],
                                                              scalar1=vcol(o_mask), scalar2=None, op0=ALU.mult),
                             reads=[f_b, vec_b], writes=[b])
                        return
                    P.op("dve", lambda e: e.tensor_tensor(out=uu(c)[:, CARRY:CARRY + N], in0=cv(c)[:, 0:N],
                                                          in1=f_t[:, 0:N], op=ALU.mult), reads=[b, f_b], writes=[b])
                    eng = "dve"
                    P.op(eng, lambda e: e.tensor_scalar(out=cv(c)[:, 0:N], in0=uu(c)[:, 0:N], scalar1=vcol(o_dw + c),
                                                        scalar2=vcol(o_dwb + c), op0=ALU.mult, op1=ALU.add),
                         reads=[b, vec_b], writes=[b])
                    for k in range(1, KW):
                        P.op(eng, lambda e, k=k: e.scalar_tensor_tensor(
                            out=cv(c)[:, 0:N], in0=uu(c)[:, k:k + N], scalar=vcol(o_dw + 16 * k + c), in1=cv(c)[:, 0:N],
                            op0=ALU.mult, op1=ALU.add), reads=[b, vec_b], writes=[b])
                    P.op("act", lambda e: e.copy(out=uu(c)[:, 0:CARRY], in_=uu(c)[:, N:N + CARRY]),
                         reads=[b], writes=[b])

            gemm(cx, ws, w_in, D, 2 * D, lambda kc: (hs[kc][0][:, 0:N], hs[kc][1]), N, epi)
            if halo:
                return
            p1, p1b = cx.ps()
            p2, p2b = cx.ps()
            for c in range(DC):
                t, b = bigs[c]
                a_t, a_b = rot(tmpb, "tb")
                s_t, s_b = rot(sq, "sq")
                P.op("act", lambda e, a_t=a_t, c=c: e.copy(out=a_t[:, 0:N], in_=cv(c)[:, 0:N]), reads=[b], writes=[a_b])
                P.op("act", lambda e, s_t=s_t, c=c: e.activation(out=s_t[:, 0:N], in_=cv(c)[:, 0:N], func=AF.Square),
                     reads=[b], writes=[s_b])
                P.op("pe", lambda e, a_t=a_t, c=c: e.matmul(p1[:, 0:N], ones[:, :], a_t[:, 0:N], start=(c == 0),
                                                          stop=(c == DC - 1)), reads=[a_b, ones_b], writes=[p1b])
                P.op("pe", lambda e, s_t=s_t, c=c: e.matmul(p2[:, 0:N], ones[:, :], s_t[:, 0:N], start=(c == 0),
                                                          stop=(c == DC - 1)), reads=[s_b, ones_b], writes=[p2b])
            m_t, m_b = mean
            r_t, r_b = rstd
            P.op("dve", lambda e: e.tensor_scalar(out=m_t[:, 0:N], in0=p1[:, 0:N], scalar1=1.0 / D, scalar2=None,
                                                  op0=ALU.mult), reads=[p1b], writes=[m_b])
            f_t, f_b = rot(tmpf, "tf")
            P.op("dve", lambda e: e.tensor_tensor(out=f_t[:, 0:N], in0=m_t[:, 0:N], in1=m_t[:, 0:N], op=ALU.mult),
                 reads=[m_b], writes=[f_b])
            P.op("dve", lambda e: e.scalar_tensor_tensor(out=r_t[:, 0:N], in0=p2[:, 0:N], scalar=1.0 / D,
                                                         in1=f_t[:, 0:N], op0=ALU.mult, op1=ALU.subtract),
                 reads=[p2b, f_b], writes=[r_b])
            P.op("act", lambda e: e.activation(out=r_t[:, 0:N], in_=r_t[:, 0:N], func=AF.Sqrt,
                                               bias=vcol(o_eps + 1)), reads=[r_b, vec_b], writes=[r_b])
            P.op("dve", lambda e: e.reciprocal(out=r_t[:, 0:N], in_=r_t[:, 0:N]), reads=[r_b], writes=[r_b])
            for c in range(DC):
                t, b = bigs[c]
                g_t, g_b = rot(tmpf, "tf")
                P.op("dve", lambda e, c=c, g_t=g_t: e.tensor_tensor(out=g_t[:, 0:N], in0=cv(c)[:, 0:N], in1=m_t[:, 0:N],
                                                                  op=ALU.subtract), reads=[b, m_b], writes=[g_b])
                P.op("dve", lambda e, c=c, g_t=g_t: e.scalar_tensor_tensor(
                    out=g_t[:, 0:N], in0=g_t[:, 0:N], scalar=vcol(o_lng + c), in1=r_t[:, 0:N], op0=ALU.mult,
                    op1=ALU.mult), reads=[g_b, r_b, vec_b], writes=[g_b])
                q_t, q_b = qs[c]
                P.op("act", lambda e, c=c, g_t=g_t, q_t=q_t: e.activation(out=q_t[:, 0:N], in_=g_t[:, 0:N], func=AF.Silu,
                                                                        bias=vcol(o_lnb + c)),
                     reads=[g_b, vec_b], writes=[q_b])

            def epi_o(mc, ps, pb):
                x_t, x_b = xs[mc]
                P.op("dve", lambda e: e.scalar_tensor_tensor(out=x_t[:, 0:N], in0=ps, scalar=vcol(o_bout + mc),
                                                             in1=x_t[:, 0:N], op0=ALU.add, op1=ALU.add),
                     reads=[pb, x_b, vec_b], writes=[x_b])

            gemm(cx, ws, w_out, D, D, lambda kc: (qs[kc][0][:, 0:N], qs[kc][1]), N, epi_o)

        def mixer_post(N, t0):
            for c in range(DC):
                f_t, f_b = rot(tmpf, "tf")
                P.dma("sp", f_t[:, 0:N], mT[c * 128:(c + 1) * 128, t0:t0 + N], writes=[f_b])
                q_t, q_b = qs[c]
                P.op("act", lambda e, f_t=f_t, q_t=q_t: e.copy(out=q_t[:, 0:N], in_=f_t[:, 0:N]), reads=[f_b], writes=[q_b])

            def epi_o(mc, ps, pb):
                x_t, x_b = xs[mc]
                P.op("dve", lambda e: e.tensor_tensor(out=x_t[:, 0:N], in0=x_t[:, 0:N], in1=ps, op=ALU.add),
                     reads=[pb, x_b], writes=[x_b])

            gemm(cx, ws, w_out, D, D, lambda kc: (qs[kc][0][:, 0:N], qs[kc][1]), N, epi_o)

        def mlp(N):
            def epi1(mc, ps, pb):
                r_t, r_b = rot(tmpb, "tb")
                P.op("act", lambda e: e.activation(out=r_t[:, 0:N], in_=ps, func=AF.Relu), reads=[pb], writes=[r_b])
                a_ap, a_b = a_view(mc)
                P.op("dve", lambda e: e.tensor_tensor(out=a_ap[:, 0:N], in0=r_t[:, 0:N], in1=r_t[:, 0:N], op=ALU.mult),
                     reads=[r_b], writes=[a_b])

            gemm(cx, ws, w1, D, DFF, lambda kc: (hs[kc][0][:, 0:N], hs[kc][1]), N, epi1)

            def epi2(mc, ps, pb):
                x_t, x_b = xs[mc]
                P.op("dve", lambda e: e.tensor_tensor(out=x_t[:, 0:N], in0=x_t[:, 0:N], in1=ps, op=ALU.add),
                     reads=[pb, x_b], writes=[x_b])

            def actf(kc):
                ap, b = a_view(kc)
                return ap[:, 0:N], b

            gemm(cx, ws, w2, DFF, D, actf, N, epi2)

        if mixer in ("cc", "sc"):
            for c in range(DC):
                x_t, x_b = xs[c]
                P.dma("sp", x_t[:, 0:NH], xh[c * 128:(c + 1) * 128, :], writes=[x_b])
            rmsnorm([(xs[c][0][:, 0:NH], xs[c][1]) for c in range(DC)], 0,
                    [(hs[c][0][:, 0:NH], hs[c][1]) for c in range(DC)], NH)
            (mixer_cc if mixer == "cc" else mixer_sc)(NH, True)

        for ti in range(ntok // NT):
            t0 = ti * NT
            N = NT
            for c in range(DC):
                x_t, x_b = xs[c]
                P.dma("sp", x_t[:, 0:N], xT[c * 128:(c + 1) * 128, t0:t0 + N], writes=[x_b])
            xsrc = [(xs[c][0][:, 0:N], xs[c][1]) for c in range(DC)]
            hdst = [(hs[c][0][:, 0:N], hs[c][1]) for c in range(DC)]
            if mixer == "post":
                mixer_post(N, t0)
            else:
                rmsnorm(xsrc, 0, hdst, N)
                (mixer_cc if mixer == "cc" else mixer_sc)(N, False)
            rmsnorm(xsrc, 16, hdst, N)
            mlp(N)
            if final:
                rmsnorm(xsrc, 32, xsrc, N)
            for c in range(DC):
                x_t, x_b = xs[c]
                P.dma("sp", out[c * 128:(c + 1) * 128, t0:t0 + N], x_t[:, 0:N], reads=[x_b])
        if env is None:
            P.finish([b for _, b in xs])
            P.emit()
    return nc


_NC_CACHE = {}


def _get_nc(key, builder):
    if key not in _NC_CACHE:
        _NC_CACHE[key] = builder()
    return _NC_CACHE[key]


def _f32(a):
    return np.ascontiguousarray(np.asarray(a, dtype=np.float32))


def run_tok(mixer, final, x, g1, g2, gf, w1, w2, mp, m_in=None):
    nc = _get_nc(("tok", mixer, final), lambda: build_tok(mixer, final))
    base = [pack_vec(g1), pack_vec(g2), pack_vec(gf), np.full((128, 1), RMS_EPS, np.float32),
            np.full((128, 1), LN_EPS, np.float32)]
    in_maps = []
    for c in range(NCORES):
        b, half = c // 2, c % 2
        t0 = half * NTOK
        m = {"xT": _f32(x[b, t0:t0 + NTOK, :].T), "w1": w1, "w2": w2}
        mask = np.full((128, 1), 1.0 if half == 1 else 0.0, np.float32)
        if mixer in ("cc", "sc"):
            if half == 1:
                m["xh"] = _f32(x[b, t0 - NH:t0, :].T)
            else:
                m["xh"] = np.zeros((D, NH), np.float32)
        if mixer == "cc":
            vecs = base + [pack_vec(mp["b_in"]), ] + [pack_vec(mp["dw"][k]) for k in range(31)] + \
                [pack_vec(mp["dw_b"]), pack_vec(mp["ln_g"]), pack_vec(mp["ln_b"]), pack_vec(mp["b_out"]), mask]
            m["w_in"] = mp["w_in"]
            m["w_out"] = mp["w_out"]
        elif mixer == "sc":
            vecs = base + [pack_vec(mp["conv_w"][k]) for k in range(3)] + [mask]
            m["w_in"] = mp["w_in"]
            m["w_out"] = mp["w_out"]
        else:
            vecs = base
            m["mT"] = _f32(m_in[b, t0:t0 + NTOK, :].T)
            m["w_out"] = mp["w_out"]
        m["vec"] = _f32(np.concatenate(vecs, axis=1))
        in_maps.append(m)
    res = run_bass_kernel_spmd(nc, in_maps, core_ids=list(range(NCORES)))
    y = np.empty((B, T, D), np.float32)
    for c in range(NCORES):
        b, half = c // 2, c % 2
        t0 = half * NTOK
        y[b, t0:t0 + NTOK, :] = res.results[c]["out"].T
    return y


FH = 8
FDH = 128
FOX_NEG = -30000.0


def build_fox(T=T, env=None):
    nc = env.nc if env else bass.Bass("TRN2", target_bir_lowering=False)
    dram = env.dram if env else (lambda n, s, k="ExternalInput": nc.dram_tensor(n, list(s), F32, kind=k).ap())
    xT = dram("xT", [D, T])
    wall = dram("wall", [D, 3 * FH * FDH])
    wf_d = dram("wf", [D, FH])
    vec_d = dram("vec", [128, 20])
    ident_d = dram("ident", [128, 128])
    mask_d = dram("mask", [128, 896])
    sel_d = dram("sel", [FH, FH * 128])
    out = dram("out", [FH * FDH, T], "ExternalOutput")
    NTI = T // NT
    scale = 1.0 / float(np.sqrt(FDH))

    with (env.scope() if env else contextlib.ExitStack()) as st:
        cx = env.cx if env else Ctx(nc, st)
        P = cx.P
        if env is None:
            cx.init_psum(8)
        cx.rot_n = 4
        ws = WStream(cx, nslots=2, cols=256)
        vec = cx.sb([128, 20], F32, "vec"); vec_b = Buf("vec")
        ident = cx.sb([128, 128], F32, "ident"); ident_b = Buf("ident")
        mask = cx.sb([128, 896], F32, "mask"); mask_b = Buf("mask")
        sel = cx.sb([FH, FH * 128], F32, "sel"); sel_b = Buf("sel")
        ones = cx.sb([128, 128], BF16, "ones"); ones_b = Buf("ones")
        onesf = cx.sb([FH, NT], F32, "onesf"); onesf_b = Buf("onesf")
        wf = cx.sb([128, DC, FH], BF16, "wfs"); wf_b = Buf("wfs")
        negbf = cx.sb([FH, 1], F32, "negbf"); negbf_b = Buf("negbf")
        kT = cx.sb([128, FH, T], BF16, "kT"); kT_b = [Buf("kT%d" % j) for j in range(NTI)]
        vtm = cx.sb([128, T // 128, FH * FDH], BF16, "vtm"); vtm_b = [Buf("vtm%d" % j) for j in range(NTI)]
        qT = cx.sb([128, FH, NT], BF16, "qT"); qT_b = [Buf("qT%d" % h) for h in range(FH)]
        hs = [(cx.sb([128, NT], BF16, "h%d" % c), Buf("h%d" % c)) for c in range(DC)]
        xst = [(cx.sb([128, NT], F32, "xs%d" % i), Buf("xs%d" % i)) for i in range(2)]
        sq = [(cx.sb([128, NT], BF16, "sq%d" % i), Buf("sq%d" % i)) for i in range(2)]
        tmpf = [(cx.sb([128, NT], F32, "tf%d" % i), Buf("tf%d" % i)) for i in range(3)]
        ptb = [(cx.sb([128, NT], BF16, "pt%d" % i), Buf("pt%d" % i)) for i in range(2)]
        cqb = [(cx.sb([128, NT], F32, "cqb%d" % i), Buf("cqb%d" % i)) for i in range(1)]
        rstd = (cx.sb([128, NT], F32, "rstd"), Buf("rstd"))
        cfm = [(cx.sb([FH, NT], F32, "cfm%d" % i), Buf("cfm%d" % i)) for i in range(2)]
        lfm = (cx.sb([FH, NT], F32, "lfm"), Buf("lfm"))
        negc = cx.sb([128, T // 128, FH], F32, "negc"); negc_b = [Buf("negc%d" % j) for j in range(NTI)]
        KK = cx.sb([128, FH], F32, "KK"); KK_b = Buf("KK")
        kmax = (cx.sb([128, 1], F32, "kmax"), Buf("kmax"))
        ost = [(cx.sb([128, NT], F32, "ost%d" % i), Buf("ost%d" % i)) for i in range(1)]
        cnt = {}

        def rot(lst, key):
            cnt[key] = cnt.get(key, 0) + 1
            return lst[(cnt[key] - 1) % len(lst)]

        def vcol(o, n=1):
            return vec[:, o:o + n]

        P.dma("sp", vec[:, :], vec_d[:, :], writes=[vec_b])
        P.dma("sp", ident[:, :], ident_d[:, :], writes=[ident_b])
        P.dma("sp", mask[:, :], mask_d[:, :], writes=[mask_b])
        P.dma("sp", sel[:, :], sel_d[:, :], writes=[sel_b])
        P.dma("pool", wf[:, :, :], wf_d.rearrange("(kc p) m -> p kc m", p=128), writes=[wf_b])
        P.op("dve", lambda e: e.memset(ones[:, :], 1.0), writes=[ones_b])
        P.op("dve", lambda e: e.memset(onesf[:, :], 1.0), writes=[onesf_b])
        P.op("dve", lambda e: e.memset(KK[:, :], 0.0), writes=[KK_b])
        P.op("dve", lambda e: e.tensor_scalar(out=negbf[:, :], in0=vec[0:FH, 18:19], scalar1=-1.0, scalar2=None,
                                              op0=ALU.mult), reads=[vec_b], writes=[negbf_b])

        @_each(range(NTI))
        def _body_j(j):
            t0 = j * NT
            N = NT
            pt, pb = cx.ps()
            for c in range(DC):
                x_t, x_b = rot(xst, "xs")
                s_t, s_b = rot(sq, "sq")
                P.dma("sp", x_t[:, :], xT[c * 128:(c + 1) * 128, t0:t0 + N], writes=[x_b])
                P.op("act", lambda e, s_t=s_t, x_t=x_t: e.activation(out=s_t[:, :], in_=x_t[:, :], func=AF.Square),
                     reads=[x_b], writes=[s_b])
                P.op("pe", lambda e, s_t=s_t, c=c, pt=pt: e.matmul(pt[:, 0:N], ones[:, :], s_t[:, :], start=(c == 0),
                                                                 stop=(c == DC - 1)), reads=[s_b, ones_b], writes=[pb])
            r_t, r_b = rstd
            P.op("act", lambda e, pt=pt: e.activation(out=r_t[:, :], in_=pt[:, 0:N], func=AF.Sqrt, scale=1.0 / D,
                                                      bias=vcol(16)), reads=[pb, vec_b], writes=[r_b])
            P.op("dve", lambda e: e.reciprocal(out=r_t[:, :], in_=r_t[:, :]), reads=[r_b], writes=[r_b])
            for c in range(DC):
                x_t, x_b = rot(xst, "xs")
                P.dma("sp", x_t[:, :], xT[c * 128:(c + 1) * 128, t0:t0 + N], writes=[x_b])
                h_t, h_b = hs[c]
                P.op("dve", lambda e, x_t=x_t, h_t=h_t, c=c: e.scalar_tensor_tensor(
                    out=h_t[:, :], in0=x_t[:, :], scalar=vcol(c), in1=r_t[:, :], op0=ALU.mult, op1=ALU.mult),
                    reads=[x_b, r_b, vec_b], writes=[h_b])

            def epi_qk(mc, ps, pb, j=j, t0=t0):
                if mc < FH:
                    P.op("act", lambda e: e.mul(out=qT[:, mc, :], in_=ps, mul=scale), reads=[pb], writes=[qT_b[mc]])
                else:
                    hh = mc - FH
                    P.op("dve", lambda e: e.tensor_copy(out=kT[:, hh, t0:t0 + N], in_=ps), reads=[pb], writes=[kT_b[j]])

            gemm(cx, ws, wall[:, 0:2 * FH * FDH], D, 2 * FH * FDH, lambda kc: (hs[kc][0][:, :], hs[kc][1]), N, epi_qk, mgrp=2)

            for hh in range(FH):
                s_t, s_b = rot(sq, "sq")
                P.op("act", lambda e, s_t=s_t, hh=hh: e.activation(out=s_t[:, :], in_=kT[:, hh, t0:t0 + N], func=AF.Square),
                     reads=[kT_b[j]], writes=[s_b])
                p2, p2b = cx.ps()
                P.op("pe", lambda e, s_t=s_t, p2=p2: e.matmul(p2[:, 0:N], ones[:, :], s_t[:, :], start=True, stop=True),
                     reads=[s_b, ones_b], writes=[p2b])
                km_t, km_b = kmax
                P.op("dve", lambda e, p2=p2: e.tensor_reduce(out=km_t[:, :], in_=p2[:, 0:N], axis=AX.X, op=ALU.max),
                     reads=[p2b], writes=[km_b])
                P.op("dve", lambda e, hh=hh: e.tensor_tensor(out=KK[:, hh:hh + 1], in0=KK[:, hh:hh + 1], in1=km_t[:, :],
                                                            op=ALU.max), reads=[km_b, KK_b], writes=[KK_b])

            vc0 = 2 * FH * FDH
            for c0 in range(0, FH * FDH, 256):
                ncols = min(256, FH * FDH - c0)
                wt, wb = ws.load(wall, 0, 128, DC, vc0 + c0, ncols)
                for tb in range(4):
                    pv, pvb = cx.ps()
                    for kc in range(DC):
                        P.op("pe", lambda e, pv=pv, wt=wt, kc=kc, tb=tb, ncols=ncols: e.matmul(
                            pv[:, 0:ncols], hs[kc][0][:, tb * 128:(tb + 1) * 128], wt[:, kc, 0:ncols],
                            start=(kc == 0), stop=(kc == DC - 1)), reads=[wb, hs[kc][1]], writes=[pvb])
                    eng = "act" if tb % 2 == 0 else "dve"
                    if eng == "act":
                        P.op("act", lambda e, pv=pv, tb=tb, c0=c0, ncols=ncols: e.copy(
                            out=vtm[:, j * 4 + tb, c0:c0 + ncols], in_=pv[:, 0:ncols]), reads=[pvb], writes=[vtm_b[j]])
                    else:
                        P.op("dve", lambda e, pv=pv, tb=tb, c0=c0, ncols=ncols: e.tensor_copy(
                            out=vtm[:, j * 4 + tb, c0:c0 + ncols], in_=pv[:, 0:ncols]), reads=[pvb], writes=[vtm_b[j]])

            pf, pfb = cx.ps()
            for kc in range(DC):
                P.op("pe", lambda e, pf=pf, kc=kc: e.matmul(pf[0:FH, 0:N], wf[:, kc, :], hs[kc][0][:, :], start=(kc == 0),
                                                          stop=(kc == DC - 1)), reads=[wf_b, hs[kc][1]], writes=[pfb])
            l_t, l_b = lfm
            P.op("act", lambda e, pf=pf: e.activation(out=l_t[:, :], in_=pf[0:FH, 0:N], func=AF.Exp, scale=-1.0,
                                                      bias=negbf[:, :]), reads=[pfb, negbf_b], writes=[l_b])
            P.op("act", lambda e: e.activation(out=l_t[:, :], in_=l_t[:, :], func=AF.Ln, bias=vec[0:FH, 17:18]),
                 reads=[l_b, vec_b], writes=[l_b])
            c_t, c_b = cfm[j % 2]
            cp_t, cp_b = cfm[(j + 1) % 2]
            if j == 0:
                P.op("dve", lambda e, c_t=c_t: e.tensor_tensor_scan(out=c_t[:, :], data0=onesf[:, :], data1=l_t[:, :],
                                                                    initial=0.0, op0=ALU.mult, op1=ALU.subtract),
                     reads=[onesf_b, l_b], writes=[c_b])
            else:
                P.op("dve", lambda e, c_t=c_t, cp_t=cp_t: e.tensor_tensor_scan(
                    out=c_t[:, :], data0=onesf[:, :], data1=l_t[:, :], initial=cp_t[:, N - 1:N], op0=ALU.mult,
                    op1=ALU.subtract), reads=[onesf_b, l_b, cp_b], writes=[c_b])
            for tb in range(4):
                ptr, ptrb = cx.ps()
                P.op("pe", lambda e, ptr=ptr, tb=tb, c_t=c_t: e.transpose(out=ptr[:, 0:FH], in_=c_t[:, tb * 128:(tb + 1) * 128],
                                                                        identity=ident[0:FH, 0:FH]),
                     reads=[c_b, ident_b], writes=[ptrb])
                P.op("act", lambda e, ptr=ptr, tb=tb: e.mul(out=negc[:, j * 4 + tb, :], in_=ptr[:, 0:FH], mul=-1.0),
                     reads=[ptrb], writes=[negc_b[j]])

            for hh in range(FH):
                s_t, s_b = rot(sq, "sq")
                P.op("act", lambda e, s_t=s_t, hh=hh: e.activation(out=s_t[:, :], in_=qT[:, hh, :], func=AF.Square),
                     reads=[qT_b[hh]], writes=[s_b])
                pq, pqb = cx.ps()
                P.op("pe", lambda e, s_t=s_t, pq=pq: e.matmul(pq[:, 0:N], ones[:, :], s_t[:, :], start=True, stop=True),
                     reads=[s_b, ones_b], writes=[pqb])
                m_t, m_b = rot(tmpf, "tf")
                P.op("act", lambda e, pq=pq, m_t=m_t, hh=hh: e.activation(out=m_t[:, :], in_=pq[:, 0:N], func=AF.Sqrt,
                                                                         scale=KK[:, hh:hh + 1]),
                     reads=[pqb, KK_b], writes=[m_b])
                pc, pcb = cx.ps()
                P.op("pe", lambda e, pc=pc, hh=hh, c_t=c_t: e.matmul(pc[:, 0:N], sel[:, hh * 128:(hh + 1) * 128], c_t[:, :],
                                                                   start=True, stop=True), reads=[sel_b, c_b], writes=[pcb])
                q_t, q_b = rot(cqb, "cqb")
                P.op("dve", lambda e, q_t=q_t, m_t=m_t, pc=pc: e.scalar_tensor_tensor(
                    out=q_t[:, :], in0=m_t[:, :], scalar=-1.02, in1=pc[:, 0:N], op0=ALU.mult, op1=ALU.add),
                    reads=[m_b, pcb], writes=[q_b])
                accO, accOb = cx.psum[4 + 2 * (hh % 2)]
                accD, accDb = cx.psum[5 + 2 * (hh % 2)]
                nblk = 4 * j + 4
                for i in range(nblk):
                    pS, pSb = cx.ps()
                    P.op("pe", lambda e, pS=pS, i=i, hh=hh: e.matmul(pS[:, 0:N], kT[:, hh, i * 128:(i + 1) * 128], qT[:, hh, :],
                                                                   start=True, stop=True),
                         reads=[kT_b[i // 4], qT_b[hh]], writes=[pSb])
                    f_t, f_b = rot(tmpf, "tf")
                    P.op("dve", lambda e, f_t=f_t, pS=pS, q_t=q_t: e.tensor_tensor(out=f_t[:, :], in0=pS[:, 0:N], in1=q_t[:, :],
                                                                                 op=ALU.add), reads=[pSb, q_b], writes=[f_b])
                    if i >= 4 * j:
                        r = i - 4 * j
                        P.op("dve", lambda e, f_t=f_t, r=r: e.tensor_tensor(
                            out=f_t[:, :], in0=f_t[:, :], in1=mask[:, 384 - 128 * r:384 - 128 * r + NT], op=ALU.add),
                            reads=[f_b, mask_b], writes=[f_b])
                    p_t, p_b = rot(ptb, "pt")
                    P.op("act", lambda e, p_t=p_t, f_t=f_t, i=i, hh=hh: e.activation(
                        out=p_t[:, :], in_=f_t[:, :], func=AF.Exp, bias=negc[:, i, hh:hh + 1]),
                        reads=[f_b, negc_b[i // 4]], writes=[p_b])
                    P.op("pe", lambda e, accO=accO, p_t=p_t, i=i, hh=hh, nblk=nblk: e.matmul(
                        accO[:, 0:N], vtm[:, i, hh * FDH:(hh + 1) * FDH], p_t[:, :], start=(i == 0), stop=(i == nblk - 1)),
                        reads=[vtm_b[i // 4], p_b], writes=[accOb])
                    P.op("pe", lambda e, accD=accD, p_t=p_t, i=i, nblk=nblk: e.matmul(
                        accD[:, 0:N], ones[:, :], p_t[:, :], start=(i == 0), stop=(i == nblk - 1)),
                        reads=[ones_b, p_b], writes=[accDb])
                rc_t, rc_b = rot(tmpf, "tf")
                P.op("dve", lambda e, rc_t=rc_t, accD=accD: e.reciprocal(out=rc_t[:, :], in_=accD[:, 0:N]),
                     reads=[accDb], writes=[rc_b])
                o_t, o_b = rot(ost, "ost")
                P.op("dve", lambda e, o_t=o_t, rc_t=rc_t, accO=accO: e.tensor_tensor(out=o_t[:, :], in0=accO[:, 0:N],
                                                                                     in1=rc_t[:, :], op=ALU.mult),
                     reads=[accOb, rc_b], writes=[o_b])
                P.dma("sp", out[hh * FDH:(hh + 1) * FDH, t0:t0 + N], o_t[:, :], reads=[o_b])
        if env is None:
            P.finish([b for _, b in ost])
            P.emit()
    return nc


def fox_consts():
    ident = np.eye(128, dtype=np.float32)
    p = np.arange(128)[:, None]
    xx = np.arange(896)[None, :]
    mask = np.where(xx - 384 >= p, 0.0, FOX_NEG).astype(np.float32)
    sel = np.zeros((FH, FH * 128), np.float32)
    for h in range(FH):
        sel[h, h * 128:(h + 1) * 128] = 1.0
    return ident, mask, sel


def run_fox(x, g1, w_qkvf, b_f):
    nc = _get_nc(("fox",), build_fox)
    ident, mask, sel = fox_consts()
    in_maps = []
    for c in range(NCORES):
        b, hg = c // 2, c % 2
        h0 = hg * FH
        cols = slice(h0 * FDH, (h0 + FH) * FDH)
        wall = np.concatenate([w_qkvf[:, 0:D][:, cols], w_qkvf[:, D:2 * D][:, cols], w_qkvf[:, 2 * D:3 * D][:, cols]], axis=1)
        vec = np.zeros((128, 20), np.float32)
        vec[:, 0:16] = pack_vec(g1)
        vec[:, 16] = RMS_EPS
        vec[:, 17] = 1.0
        vec[0:FH, 18] = np.asarray(b_f)[h0:h0 + FH]
        in_maps.append({"xT": _f32(x[b].T), "wall": _f32(wall), "wf": _f32(w_qkvf[:, 3 * D + h0:3 * D + h0 + FH]),
                        "vec": vec, "ident": ident, "mask": mask, "sel": sel})
    res = run_bass_kernel_spmd(nc, in_maps, core_ids=list(range(NCORES)))
    o = np.empty((B, T, D), np.float32)
    for c in range(NCORES):
        b, hg = c // 2, c % 2
        o[b, :, hg * FH * FDH:(hg + 1) * FH * FDH] = res.results[c]["out"].T
    return o


RH = 16
RN = 64
RF = RH * RN
RFC = RF // 128
CH = 64
C0 = float(np.exp(-0.5))
GN_EPS = 64e-5
RA_OUTS = ("At", "Kt", "Bt", "Rt", "Kh", "Bh", "Vb", "T1", "T2")


def build_rwkv_a(T=T, env=None):
    nc = env.nc if env else bass.Bass("TRN2", target_bir_lowering=False)
    dram = env.dram if env else (lambda n, s, k="ExternalInput": nc.dram_tensor(n, list(s), F32, kind=k).ap())
    xT = dram("xT", [D, T])
    wrkv = dram("wrkv", [D, 3 * RF])
    w1_d = dram("w1", [D, 96]); w2_d = dram("w2", [96, RF])
    a1_d = dram("a1", [D, 96]); a2_d = dram("a2", [96, RF])
    g1_d = dram("g1", [D, 256]); g2_d = dram("g2", [256, RF])
    o_mu = 18
    o_own = 114
    o_w0, o_a0, o_kk, o_ka, o_gng, o_gnb, o_rk = [o_own + 8 * i for i in range(7)]
    NV = o_own + 56
    vec_d = dram("vec", [128, NV])
    bones_d = dram("bones", [128, 128])
    rmask_d = dram("rmask", [128, NT])
    outs = {n: dram(n, [RF, T], "ExternalOutput") for n in RA_OUTS}
    pc_d = dram("PC", [RF, T // CH], "ExternalOutput")
    NTI = T // NT
    NCH = NT // CH

    with (env.scope() if env else contextlib.ExitStack()) as st:
        cx = env.cx if env else Ctx(nc, st)
        P = cx.P
        if env is None:
            cx.init_psum(8)
        ws = WStream(cx, nslots=2, cols=256)
        vec = cx.sb([128, NV], F32, "vec"); vec_b = Buf("vec")
        omm = cx.sb([128, 96], F32, "omm"); omm_b = Buf("omm")
        bonesf = cx.sb([128, 128], F32, "bonesf"); bonesf_b = Buf("bonesf")
        bones = cx.sb([128, 128], BF16, "bones"); bones_b = Buf("bones")
        ones = cx.sb([128, 128], BF16, "ones"); ones_b = Buf("ones")
        rmask = cx.sb([128, NT], F32, "rmask"); rmask_b = Buf("rmask")
        hf = [(cx.sb([128, NT + 1], F32, "hf%d" % c), Buf("hf%d" % c)) for c in range(DC)]
        mixb = [(cx.sb([128, NT], BF16, "mx%d" % c), Buf("mx%d" % c)) for c in range(DC)]
        G = {}
        for nm in ("r", "k", "v", "s", "a", "g"):
            G[nm] = [(cx.sb([128, NT], F32, "G%s%d" % (nm, f)), Buf("G%s%d" % (nm, f))) for f in range(RFC)]
        th = (cx.sb([96, NT], BF16, "th"), Buf("th"))
        ah = (cx.sb([96, NT], BF16, "ah"), Buf("ah"))
        gh = [(cx.sb([128, NT], BF16, "gh%d" % i), Buf("gh%d" % i)) for i in range(2)]
        tf = [(cx.sb([128, NT], F32, "tf%d" % i), Buf("tf%d" % i)) for i in range(6)]
        tb = [(cx.sb([128, NT], BF16, "tb%d" % i), Buf("tb%d" % i)) for i in range(2)]
        og = [(cx.sb([128, NT], F32, "og%d" % i), Buf("og%d" % i)) for i in range(4)]
        rstd = (cx.sb([128, NT], F32, "rstd"), Buf("rstd"))
        pcs = cx.sb([128, RFC, T // CH], F32, "pcs"); pcs_b = Buf("pcs")
        cnt = {}

        def rot(lst, key):
            cnt[key] = cnt.get(key, 0) + 1
            return lst[(cnt[key] - 1) % len(lst)]

        def vcol(o, n=1):
            return vec[:, o:o + n]

        P.dma("sp", vec[:, :], vec_d[:, :], writes=[vec_b])
        P.dma("sp", bonesf[:, :], bones_d[:, :], writes=[bonesf_b])
        P.dma("sp", rmask[:, :], rmask_d[:, :], writes=[rmask_b])
        P.op("dve", lambda e: e.memset(ones[:, :], 1.0), writes=[ones_b])
        P.op("dve", lambda e: e.tensor_copy(out=bones[:, :], in_=bonesf[:, :]), reads=[bonesf_b], writes=[bones_b])
        P.op("dve", lambda e: e.tensor_scalar(out=omm[:, :], in0=vec[:, o_mu:o_mu + 96], scalar1=-1.0, scalar2=1.0,
                                              op0=ALU.mult, op1=ALU.add), reads=[vec_b], writes=[omm_b])
        for c in range(DC):
            P.op("dve", lambda e, c=c: e.memset(hf[c][0][:, 0:1], 0.0), writes=[hf[c][1]])

        @_each(range(NTI))
        def _body_j(j):
            t0 = j * NT
            N = NT
            if j > 0:
                for c in range(DC):
                    P.op("act", lambda e, c=c: e.copy(out=hf[c][0][:, 0:1], in_=hf[c][0][:, N:N + 1]),
                         reads=[hf[c][1]], writes=[hf[c][1]])
            pt, pb = cx.ps()
            for c in range(DC):
                h_t, h_b = hf[c]
                s_t, s_b = rot(tb, "tb")
                P.dma("sp", h_t[:, 1:N + 1], xT[c * 128:(c + 1) * 128, t0:t0 + N], writes=[h_b])
                P.op("act", lambda e, s_t=s_t, h_t=h_t: e.activation(out=s_t[:, :], in_=h_t[:, 1:N + 1], func=AF.Square),
                     reads=[h_b], writes=[s_b])
                P.op("pe", lambda e, s_t=s_t, c=c, pt=pt: e.matmul(pt[:, 0:N], ones[:, :], s_t[:, :], start=(c == 0),
                                                                 stop=(c == DC - 1)), reads=[s_b, ones_b], writes=[pb])
            r_t, r_b = rstd
            P.op("act", lambda e, pt=pt: e.activation(out=r_t[:, :], in_=pt[:, 0:N], func=AF.Sqrt, scale=1.0 / D,
                                                      bias=vcol(16)), reads=[pb, vec_b], writes=[r_b])
            P.op("dve", lambda e: e.reciprocal(out=r_t[:, :], in_=r_t[:, :]), reads=[r_b], writes=[r_b])
            for c in range(DC):
                h_t, h_b = hf[c]
                P.op("dve", lambda e, h_t=h_t, c=c: e.scalar_tensor_tensor(
                    out=h_t[:, 1:N + 1], in0=h_t[:, 1:N + 1], scalar=vcol(c), in1=r_t[:, :], op0=ALU.mult, op1=ALU.mult),
                    reads=[h_b, r_b, vec_b], writes=[h_b])

            def mix(i):
                for c in range(DC):
                    h_t, h_b = hf[c]
                    m_t, m_b = mixb[c]
                    f_t, f_b = rot(tf, "tf")
                    P.op("dve", lambda e, f_t=f_t, h_t=h_t, c=c: e.tensor_scalar(
                        out=f_t[:, :], in0=h_t[:, 1:N + 1], scalar1=omm[:, i * 16 + c:i * 16 + c + 1], scalar2=None,
                        op0=ALU.mult), reads=[h_b, omm_b], writes=[f_b])
                    P.op("dve", lambda e, f_t=f_t, h_t=h_t, m_t=m_t, c=c: e.scalar_tensor_tensor(
                        out=m_t[:, :], in0=h_t[:, 0:N], scalar=vcol(o_mu + i * 16 + c), in1=f_t[:, :], op0=ALU.mult,
                        op1=ALU.add), reads=[h_b, f_b, vec_b], writes=[m_b])

            actf = lambda kc: (mixb[kc][0][:, :], mixb[kc][1])

            def epi_store(nm, func=None, bias_o=None):
                def epi(mc, ps, pb):
                    g_t, g_b = G[nm][mc]
                    if func is None:
                        eng = "act" if mc % 2 == 0 else "dve"
                        if eng == "act":
                            P.op("act", lambda e: e.copy(out=g_t[:, :], in_=ps), reads=[pb], writes=[g_b])
                        else:
                            P.op("dve", lambda e: e.tensor_copy(out=g_t[:, :], in_=ps), reads=[pb], writes=[g_b])
                    else:
                        P.op("act", lambda e: e.activation(out=g_t[:, :], in_=ps, func=func, bias=vcol(bias_o + mc)),
                             reads=[pb, vec_b], writes=[g_b])
                return epi

            mix(0)
            gemm(cx, ws, wrkv[:, 0:RF], D, RF, actf, N, epi_store("r"), mgrp=2)
            mix(2)
            gemm(cx, ws, wrkv[:, RF:2 * RF], D, RF, actf, N, epi_store("k"), mgrp=2)
            mix(3)
            gemm(cx, ws, wrkv[:, 2 * RF:3 * RF], D, RF, actf, N, epi_store("v"), mgrp=2)
            mix(1)

            def epi_th(mc, ps, pb):
                P.op("act", lambda e: e.activation(out=th[0][:, :], in_=ps, func=AF.Tanh), reads=[pb], writes=[th[1]])
            gemm(cx, ws, w1_d, D, 96, actf, N, epi_th, mgrp=1, mp=96)
            gemm(cx, ws, w2_d, 96, RF, lambda kc: (th[0][:, :], th[1]), N, epi_store("s", AF.Sigmoid, o_w0), kp=96, mgrp=2)
            mix(4)

            def epi_ah(mc, ps, pb):
                P.op("act", lambda e: e.copy(out=ah[0][:, :], in_=ps), reads=[pb], writes=[ah[1]])
            gemm(cx, ws, a1_d, D, 96, actf, N, epi_ah, mgrp=1, mp=96)
            gemm(cx, ws, a2_d, 96, RF, lambda kc: (ah[0][:, :], ah[1]), N, epi_store("a", AF.Sigmoid, o_a0), kp=96, mgrp=2)
            mix(5)

            def epi_gh(mc, ps, pb):
                P.op("act", lambda e: e.activation(out=gh[mc][0][:, :], in_=ps, func=AF.Sigmoid), reads=[pb],
                     writes=[gh[mc][1]])
            gemm(cx, ws, g1_d, D, 256, actf, N, epi_gh, mgrp=2)
            gemm(cx, ws, g2_d, 256, RF, lambda kc: (gh[kc][0][:, :], gh[kc][1]), N, epi_store("g"), mgrp=2)

            @_each(range(RFC))
            def _body_f(f):
                r_t_, r_b_ = G["r"][f]; k_t, k_b = G["k"][f]; v_t, v_b = G["v"][f]
                s_t, s_b = G["s"][f]; a_t, a_b = G["a"][f]; g_t, g_b = G["g"][f]
                rows = slice(f * 128, (f + 1) * 128)

                def store(nm, o_t, o_b):
                    P.dma("sp", outs[nm][rows, t0:t0 + N], o_t[:, :], reads=[o_b])

                kk_t, kk_b = rot(tf, "tf")
                P.op("dve", lambda e: e.tensor_scalar(out=kk_t[:, :], in0=k_t[:, :], scalar1=vcol(o_kk + f), scalar2=None,
                                                      op0=ALU.mult), reads=[k_b, vec_b], writes=[kk_b])
                q_t, q_b = rot(tb, "tb")
                P.op("act", lambda e: e.activation(out=q_t[:, :], in_=kk_t[:, :], func=AF.Square), reads=[kk_b], writes=[q_b])
                p1, p1b = cx.ps()
                P.op("pe", lambda e: e.matmul(p1[:, 0:N], bones[:, :], q_t[:, :], start=True, stop=True),
                     reads=[bones_b, q_b], writes=[p1b])
                n_t, n_b = rot(tf, "tf")
                P.op("act", lambda e: e.activation(out=n_t[:, :], in_=p1[:, 0:N], func=AF.Sqrt), reads=[p1b], writes=[n_b])
                P.op("dve", lambda e: e.tensor_scalar(out=n_t[:, :], in0=n_t[:, :], scalar1=1e-12, scalar2=None, op0=ALU.max),
                     reads=[n_b], writes=[n_b])
                P.op("dve", lambda e: e.reciprocal(out=n_t[:, :], in_=n_t[:, :]), reads=[n_b], writes=[n_b])
                P.op("dve", lambda e: e.tensor_tensor(out=kk_t[:, :], in0=kk_t[:, :], in1=n_t[:, :], op=ALU.mult),
                     reads=[kk_b, n_b], writes=[kk_b])
                u_t, u_b = rot(tf, "tf")
                P.op("dve", lambda e: e.tensor_scalar(out=u_t[:, :], in0=a_t[:, :], scalar1=-1.0, scalar2=vcol(o_ka + f),
                                                      op0=ALU.add, op1=ALU.mult), reads=[a_b, vec_b], writes=[u_b])
                P.op("dve", lambda e: e.scalar_tensor_tensor(out=k_t[:, :], in0=u_t[:, :], scalar=1.0, in1=k_t[:, :],
                                                             op0=ALU.add, op1=ALU.mult), reads=[u_b, k_b], writes=[k_b])
                P.op("dve", lambda e: e.tensor_tensor(out=a_t[:, :], in0=kk_t[:, :], in1=a_t[:, :], op=ALU.mult),
                     reads=[kk_b, a_b], writes=[a_b])
                P.op("dve", lambda e: e.tensor_tensor(out=u_t[:, :], in0=r_t_[:, :], in1=k_t[:, :], op=ALU.mult),
                     reads=[r_b_, k_b], writes=[u_b])
                q2_t, q2_b = rot(tb, "tb")
                P.op("dve", lambda e: e.tensor_scalar(out=q2_t[:, :], in0=u_t[:, :], scalar1=vcol(o_rk + f), scalar2=None,
                                                      op0=ALU.mult), reads=[u_b, vec_b], writes=[q2_b])
                p2, p2b = cx.ps()
                P.op("pe", lambda e: e.matmul(p2[:, 0:N], bones[:, :], q2_t[:, :], start=True, stop=True),
                     reads=[bones_b, q2_b], writes=[p2b])
                o1_t, o1_b = rot(og, "og")
                P.op("dve", lambda e: e.tensor_tensor(out=o1_t[:, :], in0=p2[:, 0:N], in1=v_t[:, :], op=ALU.mult),
                     reads=[p2b, v_b], writes=[o1_b])
                P.op("dve", lambda e: e.scalar_tensor_tensor(out=o1_t[:, :], in0=o1_t[:, :], scalar=vcol(o_gnb + f),
                                                             in1=g_t[:, :], op0=ALU.add, op1=ALU.mult),
                     reads=[o1_b, g_b, vec_b], writes=[o1_b])
                store("T1", o1_t, o1_b)
                o2_t, o2_b = rot(og, "og")
                P.op("dve", lambda e: e.tensor_scalar(out=o2_t[:, :], in0=g_t[:, :], scalar1=vcol(o_gng + f), scalar2=None,
                                                      op0=ALU.mult), reads=[g_b, vec_b], writes=[o2_b])
                store("T2", o2_t, o2_b)
                store("Vb", v_t, v_b)
                cs_t, cs_b = rot(tf, "tf")
                P.op("dve", lambda e: e.tensor_tensor_scan(out=cs_t[:, :], data0=rmask[:, :], data1=s_t[:, :], initial=0.0,
                                                           op0=ALU.mult, op1=ALU.add), reads=[rmask_b, s_b], writes=[cs_b])
                e_t, e_b = rot(tf, "tf")
                P.op("act", lambda e: e.activation(out=e_t[:, :], in_=cs_t[:, :], func=AF.Exp, scale=-C0), reads=[cs_b], writes=[e_b])
                o_t, o_b = rot(og, "og")
                P.op("dve", lambda e, o_t=o_t: e.tensor_tensor(out=o_t[:, :], in0=r_t_[:, :], in1=e_t[:, :], op=ALU.mult),
                     reads=[r_b_, e_b], writes=[o_b])
                store("Rt", o_t, o_b)
                e2_t, e2_b = rot(tf, "tf")
                P.op("act", lambda e: e.activation(out=e2_t[:, :], in_=cs_t[:, :], func=AF.Exp, scale=C0), reads=[cs_b], writes=[e2_b])
                o_t, o_b = rot(og, "og")
                P.op("dve", lambda e, o_t=o_t: e.tensor_tensor(out=o_t[:, :], in0=k_t[:, :], in1=e2_t[:, :], op=ALU.mult),
                     reads=[k_b, e2_b], writes=[o_b])
                store("Kt", o_t, o_b)
                o_t, o_b = rot(og, "og")
                P.op("dve", lambda e, o_t=o_t: e.tensor_tensor(out=o_t[:, :], in0=a_t[:, :], in1=e2_t[:, :], op=ALU.mult),
                     reads=[a_b, e2_b], writes=[o_b])
                store("Bt", o_t, o_b)
                P.op("dve", lambda e: e.tensor_tensor(out=e_t[:, :], in0=cs_t[:, :], in1=s_t[:, :], op=ALU.subtract),
                     reads=[cs_b, s_b], writes=[e_b])
                P.op("act", lambda e: e.activation(out=e_t[:, :], in_=e_t[:, :], func=AF.Exp, scale=-C0), reads=[e_b], writes=[e_b])
                o_t, o_b = rot(og, "og")
                P.op("dve", lambda e, o_t=o_t: e.scalar_tensor_tensor(out=o_t[:, :], in0=kk_t[:, :], scalar=-1.0, in1=e_t[:, :],
                                                                      op0=ALU.mult, op1=ALU.mult),
                     reads=[kk_b, e_b], writes=[o_b])
                store("At", o_t, o_b)
                cs3 = cs_t[:, :].rearrange("p (c t) -> p c t", t=CH)
                P.op("dve", lambda e: e.tensor_tensor(out=e2_t[:, :].rearrange("p (c t) -> p c t", t=CH),
                                                      in0=cs3[:, :, CH - 1:CH].to_broadcast([128, NCH, CH]), in1=cs3,
                                                      op=ALU.subtract), reads=[cs_b], writes=[e2_b])
                P.op("act", lambda e: e.activation(out=e2_t[:, :], in_=e2_t[:, :], func=AF.Exp, scale=-C0), reads=[e2_b], writes=[e2_b])
                o_t, o_b = rot(og, "og")
                P.op("dve", lambda e, o_t=o_t: e.tensor_tensor(out=o_t[:, :], in0=k_t[:, :], in1=e2_t[:, :], op=ALU.mult),
                     reads=[k_b, e2_b], writes=[o_b])
                store("Kh", o_t, o_b)
                o_t, o_b = rot(og, "og")
                P.op("dve", lambda e, o_t=o_t: e.tensor_tensor(out=o_t[:, :], in0=a_t[:, :], in1=e2_t[:, :], op=ALU.mult),
                     reads=[a_b, e2_b], writes=[o_b])
                store("Bh", o_t, o_b)
                P.op("act", lambda e: e.activation(out=pcs[:, f, j * NCH:(j + 1) * NCH], in_=cs3[:, :, CH - 1], func=AF.Exp,
                                                   scale=-C0), reads=[cs_b], writes=[pcs_b])
        for f in range(RFC):
            P.dma("sp", pc_d[f * 128:(f + 1) * 128, :], pcs[:, f, :], reads=[pcs_b])
        if env is None:
            P.finish([b for _, b in og] + [pcs_b] + [b for _, b in G["v"]])
            P.emit()
    return nc


RB_TN = 128
_RWB_STOP = None


def build_rwkv_b(T=T, env=None):
    nc = env.nc if env else bass.Bass("TRN2", target_bir_lowering=False)
    dram = env.dram if env else (lambda n, s, k="ExternalInput": nc.dram_tensor(n, list(s), F32, kind=k).ap())
    FMN = ("At", "Kt", "Bt", "Rt", "T1", "T2", "Vb", "Kh", "Bh")
    if env:
        fm3 = {n: env.io[n] for n in FMN}
        pc3 = env.io["PC"]
        out3 = env.io["out"]
    else:
        fm3 = {n: dram(n, [RN, RH * T]).rearrange("j (h t) -> j h t", h=RH) for n in FMN}
        pc3 = dram("PC", [RN, RH * (T // CH)]).rearrange("j (h c) -> j h c", h=RH)
    mask_d = dram("mask5", [RN, 320])
    ident_d = dram("ident", [RN, RN])
    eps_d = dram("eps", [RN, 1])
    if not env:
        out3 = dram("out", [RN, RH * T], "ExternalOutput").rearrange("j (h t) -> j h t", h=RH)
    NTI = T // RB_TN
    NCC = RB_TN // CH
    HW = RH * RN

    def fmv(ap):
        return ap.rearrange("j (h t) -> j h t", h=RH)

    with (env.scope() if env else contextlib.ExitStack()) as st:
        cx = env.cx if env else Ctx(nc, st)
        P = cx.P
        pst = []
        for i in range(4):
            t = env.psd[i] if env else st.enter_context(nc.psum_tensor("psd%d" % i, [128, 1024], F32))
            pst.append((t, Buf("psd%d" % i)))
        pcount = [0]

        def pstile():
            r = pst[pcount[0] % 4]
            pcount[0] += 1
            return r

        mask5 = cx.sb([RN, 320], F32, "mask5"); mask_b = Buf("mask5")
        ident = cx.sb([RN, RN], F32, "ident"); ident_b = Buf("ident")
        epsc = cx.sb([RN, 1], F32, "eps"); eps_b = Buf("eps")
        PC = cx.sb([RN, RH, T // CH], F32, "PC"); PC_b = Buf("PC")
        fm = {}
        for n in ("At", "Kt", "Bt", "Rt"):
            fm[n] = [(cx.sb([RN, RH, RB_TN], BF16, "%s%d" % (n, i)), Buf("%s%d" % (n, i))) for i in range(2)]
        for n in ("T1", "T2"):
            fm[n] = [(cx.sb([RN, RH, RB_TN], F32, "%s%d" % (n, i)), Buf("%s%d" % (n, i))) for i in range(1)]
        for n in ("Vb", "Kh", "Bh"):
            fm[n] = [(cx.sb([RN, RH, RB_TN], BF16, "%s%d" % (n, i)), Buf("%s%d" % (n, i))) for i in range(2)]
        tmt = {n: (cx.sb([CH, HW], BF16, "tm%s" % n), Buf("tm%s" % n)) for n in ("Vb", "Kh", "Bh")}
        identb = cx.sb([RN, RN], BF16, "identb"); identb_b = Buf("identb")
        Mm = cx.sb([RN, RH, 320], BF16, "Mm"); Mm_b = Buf("Mm")
        XX = [(cx.sb([RN, RH, 2, RN], BF16, "XX%d" % i), Buf("XX%d" % i)) for i in range(2)]
        Rf = cx.sb([RN, RH, RN], F32, "Rf"); Rf_b = Buf("Rf")
        Rb = cx.sb([RN, RH, RN], BF16, "Rb"); Rb_b = Buf("Rb")
        Wt = cx.sb([RN, HW], BF16, "Wt"); Wt_b = Buf("Wt")
        Ut = cx.sb([RN, HW], BF16, "Ut"); Ut_b = Buf("Ut")
        Sf = cx.sb([RN, RH, RN], F32, "Sf"); Sf_b = Buf("Sf")
        Sb = cx.sb([RN, RH, RN], BF16, "Sb"); Sb_b = Buf("Sb")
        ysq = cx.sb([RN, HW], F32, "ysq"); ysq_b = Buf("ysq")
        yn = cx.sb([RN, HW], F32, "yn"); yn_b = Buf("yn")
        st1 = cx.sb([RN, RH], F32, "st1"); st1_b = Buf("st1")
        st2 = cx.sb([RN, RH], F32, "st2"); st2_b = Buf("st2")
        st3 = cx.sb([RN, RH], F32, "st3"); st3_b = Buf("st3")
        ostg = [(cx.sb([RN, RH, RB_TN], F32, "ostg%d" % i), Buf("ostg%d" % i)) for i in range(2)]

        P.dma("sp", mask5[:, :], mask_d[:, :], writes=[mask_b])
        P.dma("sp", ident[:, :], ident_d[:, :], writes=[ident_b])
        P.dma("sp", epsc[:, :], eps_d[:, :], writes=[eps_b])
        P.dma("sp", PC[:, :, :], pc3, writes=[PC_b])
        P.op("dve", lambda e: e.tensor_copy(out=identb[:, :], in_=ident[:, :]), reads=[ident_b], writes=[identb_b])
        P.op("dve", lambda e: e.memset(Sf[:, :, :], 0.0), writes=[Sf_b])
        P.op("dve", lambda e: e.memset(Sb[:, :, :], 0.0), writes=[Sb_b])
        ev = [0]

        def evac_copy(out_ap, in_ap, reads, writes):
            ev[0] += 1
            if ev[0] % 2:
                P.op("act", lambda e: e.copy(out=out_ap, in_=in_ap), reads=reads, writes=writes)
            else:
                P.op("dve", lambda e: e.tensor_copy(out=out_ap, in_=in_ap), reads=reads, writes=writes)

        @_each(range(NTI))
        def _body_ti(ti):
            t0 = ti * RB_TN
            cur = {}
            for n in ("At", "Kt", "Bt", "Rt", "Vb", "Kh", "Bh"):
                t_, b_ = fm[n][ti % 2]
                P.dma("pool", t_[:, :, :], fm3[n][:, :, t0:t0 + RB_TN], writes=[b_])
                cur[n] = (t_, b_)
            for n in ("T1", "T2"):
                t_, b_ = fm[n][0]
                P.dma("sp", t_[:, :, :], fm3[n][:, :, t0:t0 + RB_TN], writes=[b_])
                cur[n] = (t_, b_)
            og_t, og_b = ostg[ti % 2]
            At, At_b = cur["At"]; Kt, Kt_b = cur["Kt"]; Bt, Bt_b = cur["Bt"]; Rt, Rt_b = cur["Rt"]
            Vf, Vf_b = cur["Vb"]; Khf, Khf_b = cur["Kh"]; Bhf, Bhf_b = cur["Bh"]
            Vt, Vt_b = tmt["Vb"]; Kh, Kh_b = tmt["Kh"]; Bh, Bh_b = tmt["Bh"]
            T1, T1_b = cur["T1"]; T2, T2_b = cur["T2"]
            @_each(range(NCC))
            def _body_cc(cc):
                gc = ti * NCC + cc
                tc = slice(cc * CH, (cc + 1) * CH)
                hs_ = lambda h: slice(h * RN, (h + 1) * RN)
                for (src, src_b, dst, dst_b) in ((Vf, Vf_b, Vt, Vt_b), (Khf, Khf_b, Kh, Kh_b), (Bhf, Bhf_b, Bh, Bh_b)):
                    ptt, ptb_ = pstile()
                    pv16 = ptt[0:RN, 0:512].bitcast(BF16)
                    for h in range(RH):
                        P.op("pe", lambda e, pv16=pv16, h=h, src=src: e.transpose(out=pv16[:, hs_(h)], in_=src[:, h, tc],
                                                                                 identity=identb[:, :]),
                             reads=[src_b, identb_b], writes=[ptb_])
                    evac_copy(dst[:, :], pv16[:, 0:HW], [ptb_], [dst_b])
                for h0 in range(0, RH, 3):
                    nh = min(3, RH - h0)
                    pt, pb = pstile()
                    for hh in range(nh):
                        h = h0 + hh
                        o = hh * 320
                        for bi, (L, Lb, R_, R_b) in enumerate(((Kt, Kt_b, At, At_b), (Kt, Kt_b, Rt, Rt_b),
                                                              (Bt, Bt_b, At, At_b), (Bt, Bt_b, Rt, Rt_b),
                                                              (At, At_b, Bt, Bt_b))):
                            P.op("pe", lambda e, pt=pt, o=o, bi=bi, L=L, R_=R_, h=h: e.matmul(
                                pt[0:RN, o + bi * 64:o + (bi + 1) * 64], L[:, h, tc], R_[:, h, tc], start=True, stop=True),
                                reads=[Lb, R_b], writes=[pb])
                    P.op("dve", lambda e, pt=pt, h0=h0, nh=nh: e.tensor_tensor(
                        out=Mm[:, h0:h0 + nh, :], in0=pt[0:RN, 0:nh * 320].rearrange("p (h x) -> p h x", h=nh),
                        in1=mask5[:, :].unsqueeze(1).to_broadcast([RN, nh, 320]), op=ALU.mult),
                        reads=[pb, mask_b], writes=[Mm_b])
                if _RWB_STOP == "M":
                    return
                P.op("dve", lambda e: e.tensor_tensor(out=Rf[:, :, :], in0=Mm[:, :, 128:192],
                                                      in1=ident[:, :].unsqueeze(1).to_broadcast([RN, RH, RN]), op=ALU.add),
                     reads=[Mm_b, ident_b], writes=[Rf_b])
                P.op("act", lambda e: e.copy(out=Rb[:, :, :], in_=Rf[:, :, :]), reads=[Rf_b], writes=[Rb_b])
                for it in range(5):
                    if it == 0:
                        Xf = lambda h: Mm[:, h, 128:192]
                        XTf = lambda h: Mm[:, h, 256:320]
                        src_b = Mm_b
                    else:
                        xs_t, xs_b = XX[(it - 1) % 2]
                        Xf = lambda h, xs_t=xs_t: xs_t[:, h, 0, :]
                        XTf = lambda h, xs_t=xs_t: xs_t[:, h, 1, :]
                        src_b = xs_b
                    xd_t, xd_b = XX[it % 2]
                    for h0 in range(0, RH, 8):
                        pt, pb = pstile()
                        for hh in range(8):
                            h = h0 + hh
                            if it < 4:
                                P.op("pe", lambda e, pt=pt, hh=hh, h=h, Xf=Xf, XTf=XTf: e.matmul(
                                    pt[0:RN, hh * 128:hh * 128 + 64], XTf(h), Xf(h), start=True, stop=True),
                                    reads=[src_b], writes=[pb])
                            P.op("pe", lambda e, pt=pt, hh=hh, h=h, Xf=Xf, XTf=XTf: e.matmul(
                                pt[0:RN, hh * 128 + 64:hh * 128 + 128], Xf(h), XTf(h), start=True, stop=True),
                                reads=[src_b], writes=[pb])
                        if it < 4:
                            evac_copy(xd_t[:, h0:h0 + 8, :, :], pt[0:RN, 0:1024].rearrange("p (h a x) -> p h a x", h=8, a=2),
                                      [pb], [xd_b])
                        else:
                            evac_copy(xd_t[:, h0:h0 + 8, 1, :],
                                      pt[0:RN, 0:1024].rearrange("p (h a x) -> p h a x", h=8, a=2)[:, :, 1, :], [pb], [xd_b])
                    pt, pb = pstile()
                    for h in range(RH):
                        P.op("pe", lambda e, pt=pt, h=h, xd_t=xd_t: e.matmul(pt[0:RN, hs_(h)], xd_t[:, h, 1, :], Rb[:, h, :],
                                                                           start=True, stop=True),
                             reads=[xd_b, Rb_b], writes=[pb])
                    P.op("dve", lambda e, pt=pt: e.tensor_tensor(out=Rf[:, :, :].rearrange("p h x -> p (h x)"),
                                                                 in0=Rf[:, :, :].rearrange("p h x -> p (h x)"),
                                                                 in1=pt[0:RN, 0:HW], op=ALU.add),
                         reads=[pb, Rf_b], writes=[Rf_b])
                    P.op("act", lambda e: e.copy(out=Rb[:, :, :], in_=Rf[:, :, :]), reads=[Rf_b], writes=[Rb_b])
                if _RWB_STOP == "inv":
                    return
                pt, pb = pstile()
                for h in range(RH):
                    P.op("pe", lambda e, pt=pt, h=h: e.matmul(pt[0:RN, hs_(h)], At[:, h, tc], Sb[:, h, :], start=True, stop=False),
                         reads=[At_b, Sb_b], writes=[pb])
                    P.op("pe", lambda e, pt=pt, h=h: e.matmul(pt[0:RN, hs_(h)], Mm[:, h, 0:64], Vt[:, hs_(h)], start=False,
                                                            stop=True), reads=[Mm_b, Vt_b], writes=[pb])
                evac_copy(Wt[:, :], pt[0:RN, 0:HW], [pb], [Wt_b])
                if _RWB_STOP == "W":
                    return
                pt, pb = pstile()
                for h in range(RH):
                    P.op("pe", lambda e, pt=pt, h=h: e.matmul(pt[0:RN, hs_(h)], Rb[:, h, :], Wt[:, hs_(h)], start=True, stop=True),
                         reads=[Rb_b, Wt_b], writes=[pb])
                evac_copy(Ut[:, :], pt[0:RN, 0:HW], [pb], [Ut_b])
                if _RWB_STOP == "U":
                    return
                py, pyb = pstile()
                for h in range(RH):
                    P.op("pe", lambda e, py=py, h=h: e.matmul(py[0:RN, hs_(h)], Rt[:, h, tc], Sb[:, h, :], start=True, stop=False),
                         reads=[Rt_b, Sb_b], writes=[pyb])
                    P.op("pe", lambda e, py=py, h=h: e.matmul(py[0:RN, hs_(h)], Mm[:, h, 192:256], Ut[:, hs_(h)], start=False,
                                                            stop=False), reads=[Mm_b, Ut_b], writes=[pyb])
                    P.op("pe", lambda e, py=py, h=h: e.matmul(py[0:RN, hs_(h)], Mm[:, h, 64:128], Vt[:, hs_(h)], start=False,
                                                            stop=True), reads=[Mm_b, Vt_b], writes=[pyb])
                if _RWB_STOP == "Y":
                    return
                pS, pSb = pstile()
                for h in range(RH):
                    P.op("pe", lambda e, pS=pS, h=h: e.matmul(pS[0:RN, hs_(h)], Bh[:, hs_(h)], Ut[:, hs_(h)], start=True,
                                                            stop=False), reads=[Bh_b, Ut_b], writes=[pSb])
                    P.op("pe", lambda e, pS=pS, h=h: e.matmul(pS[0:RN, hs_(h)], Kh[:, hs_(h)], Vt[:, hs_(h)], start=False,
                                                            stop=True), reads=[Kh_b, Vt_b], writes=[pSb])
                if _RWB_STOP == "S":
                    return
                y3 = py[0:RN, 0:HW].rearrange("p (h x) -> p h x", h=RH)
                P.op("act", lambda e, py=py: e.activation(out=ysq[:, :], in_=py[0:RN, 0:HW], func=AF.Square), reads=[pyb],
                     writes=[ysq_b])
                if _RWB_STOP == "g1":
                    return
                P.op("act", lambda e, py=py: e.copy(out=yn[:, :], in_=py[0:RN, 0:HW]), reads=[pyb], writes=[yn_b])
                P.op("dve", lambda e: e.tensor_reduce(out=st1[:, :], in_=yn[:, :].rearrange("p (h x) -> p h x", h=RH), axis=AX.X,
                                                      op=ALU.add), reads=[yn_b], writes=[st1_b])
                if _RWB_STOP == "g2":
                    return
                P.op("dve", lambda e: e.tensor_reduce(out=st2[:, :], in_=ysq[:, :].rearrange("p (h x) -> p h x", h=RH), axis=AX.X,
                                                      op=ALU.add), reads=[ysq_b], writes=[st2_b])
                if _RWB_STOP == "g3":
                    return
                P.op("dve", lambda e: e.tensor_scalar(out=st1[:, :], in0=st1[:, :], scalar1=1.0 / RN, scalar2=None, op0=ALU.mult),
                     reads=[st1_b], writes=[st1_b])
                P.op("dve", lambda e: e.tensor_tensor(out=st3[:, :], in0=st1[:, :], in1=st1[:, :], op=ALU.mult),
                     reads=[st1_b], writes=[st3_b])
                P.op("dve", lambda e: e.scalar_tensor_tensor(out=st2[:, :], in0=st2[:, :], scalar=1.0 / RN, in1=st3[:, :],
                                                             op0=ALU.mult, op1=ALU.subtract), reads=[st2_b, st3_b], writes=[st2_b])
                P.op("act", lambda e: e.activation(out=st2[:, :], in_=st2[:, :], func=AF.Sqrt, bias=epsc[:, :]),
                     reads=[st2_b, eps_b], writes=[st2_b])
                P.op("dve", lambda e: e.reciprocal(out=st2[:, :], in_=st2[:, :]), reads=[st2_b], writes=[st2_b])
                if _RWB_STOP == "g4":
                    return
                yn3 = yn[:, :].rearrange("p (h x) -> p h x", h=RH)
                P.op("dve", lambda e: e.tensor_tensor(out=yn3, in0=yn3, in1=st1[:, :].unsqueeze(2).to_broadcast([RN, RH, RN]),
                                                      op=ALU.subtract), reads=[yn_b, st1_b], writes=[yn_b])
                P.op("dve", lambda e: e.tensor_tensor(out=yn3, in0=yn3, in1=st2[:, :].unsqueeze(2).to_broadcast([RN, RH, RN]),
                                                      op=ALU.mult), reads=[yn_b, st2_b], writes=[yn_b])
                if _RWB_STOP == "gn":
                    return
                P.op("dve", lambda e, gc=gc: e.tensor_tensor(out=Sf[:, :, :], in0=Sf[:, :, :],
                                                             in1=PC[:, :, gc:gc + 1].to_broadcast([RN, RH, RN]), op=ALU.mult),
                     reads=[Sf_b, PC_b], writes=[Sf_b])
                P.op("dve", lambda e, pS=pS: e.tensor_tensor(out=Sf[:, :, :].rearrange("p h x -> p (h x)"),
                                                             in0=Sf[:, :, :].rearrange("p h x -> p (h x)"), in1=pS[0:RN, 0:HW],
                                                             op=ALU.add), reads=[pSb, Sf_b], writes=[Sf_b])
                P.op("act", lambda e: e.copy(out=Sb[:, :, :], in_=Sf[:, :, :]), reads=[Sf_b], writes=[Sb_b])
                if _RWB_STOP == "st":
                    return
                po, pob = pstile()
                for h in range(RH):
                    P.op("pe", lambda e, po=po, h=h: e.transpose(out=po[0:RN, hs_(h)], in_=yn[:, hs_(h)], identity=ident[:, :]),
                         reads=[yn_b, ident_b], writes=[pob])
                po3 = po[0:RN, 0:HW].rearrange("p (h x) -> p h x", h=RH)
                P.op("dve", lambda e, po3=po3, og_t=og_t: e.tensor_tensor(out=og_t[:, :, tc], in0=po3, in1=T2[:, :, tc], op=ALU.mult),
                     reads=[pob, T2_b], writes=[og_b])
                P.op("dve", lambda e, og_t=og_t: e.tensor_tensor(out=og_t[:, :, tc], in0=og_t[:, :, tc], in1=T1[:, :, tc], op=ALU.add),
                     reads=[og_b, T1_b], writes=[og_b])
            P.dma("sp", out3[:, :, t0:t0 + RB_TN], og_t[:, :, :], reads=[og_b])
        if env is None:
            P.finish([b for _, b in ostg])
            P.emit()
    return nc


def rwkv_consts():
    bones = np.zeros((128, 128), np.float32)
    bones[0:64, 0:64] = 1.0
    bones[64:128, 64:128] = 1.0
    rmask = np.ones((128, NT), np.float32)
    rmask[:, ::CH] = 0.0
    s = np.arange(64)[:, None]
    t = np.arange(64)[None, :]
    su = (s < t).astype(np.float32)
    iu = (s <= t).astype(np.float32)
    sl = (t < s).astype(np.float32)
    mask5 = np.concatenate([su, iu, su, iu, sl], axis=1).astype(np.float32)
    ident = np.eye(64, dtype=np.float32)
    return bones, rmask, mask5, ident


def run_rwkv(x, g1, p):
    nca = _get_nc(("rwa",), build_rwkv_a)
    bones, rmask, mask5, ident = rwkv_consts()
    in_maps = []
    for c in range(NCORES):
        b, hg = c // 2, c % 2
        own = slice(hg * RF, (hg + 1) * RF)
        vec = np.zeros((128, 114 + 56), np.float32)
        vec[:, 0:16] = pack_vec(g1)
        vec[:, 16] = RMS_EPS
        for i in range(6):
            vec[:, 18 + 16 * i:18 + 16 * (i + 1)] = pack_vec(p["mu"][i])
        ownv = [p["w0"], p["a0"], p["k_k"], p["k_a"], p["gn_g"], p["gn_b"], np.asarray(p["r_k"]).reshape(-1)]
        for i, v in enumerate(ownv):
            vec[:, 114 + 8 * i:114 + 8 * (i + 1)] = pack_vec(np.asarray(v)[own])
        in_maps.append({
            "xT": _f32(x[b].T),
            "wrkv": _f32(np.concatenate([p["w_rkv"][0][:, own], p["w_rkv"][1][:, own], p["w_rkv"][2][:, own]], axis=1)),
            "w1": _f32(p["w1"]), "w2": _f32(p["w2"][:, own]), "a1": _f32(p["a1"]), "a2": _f32(p["a2"][:, own]),
            "g1": _f32(p["g1"]), "g2": _f32(p["g2"][:, own]), "vec": vec, "bones": bones, "rmask": rmask})
    resa = run_bass_kernel_spmd(nca, in_maps, core_ids=list(range(NCORES))).results
    ncb = _get_nc(("rwb",), build_rwkv_b)
    in_maps = []
    for c in range(NCORES):
        r = resa[c]

        def fm(a):
            return _f32(a.reshape(RH, RN, -1).transpose(1, 0, 2).reshape(RN, -1))
        m = {n: fm(r[n]) for n in ("At", "Kt", "Bt", "Rt", "T1", "T2", "Vb", "Kh", "Bh")}
        m["PC"] = fm(r["PC"])
        m["mask5"] = mask5
        m["ident"] = ident
        m["eps"] = np.full((RN, 1), GN_EPS, np.float32)
        in_maps.append(m)
    resb = run_bass_kernel_spmd(ncb, in_maps, core_ids=list(range(NCORES))).results
    y = np.empty((B, T, D), np.float32)
    for c in range(NCORES):
        b, hg = c // 2, c % 2
        o = resb[c]["out"].reshape(RN, RH, T)
        y[b, :, hg * RF:(hg + 1) * RF] = o.transpose(2, 1, 0).reshape(T, RF)
    return y


def kernel_unfused(x, norm1_g, norm2_g, mlp_w1, mlp_w2,
           cc_w_in, cc_b_in, cc_dw, cc_dw_b, cc_ln_g, cc_ln_b, cc_w_out, cc_b_out,
           rw_mu, rw_w_rkv, rw_w0, rw_w1, rw_w2, rw_a0, rw_a1, rw_a2, rw_g1, rw_g2,
           rw_k_k, rw_k_a, rw_r_k, rw_gn_g, rw_gn_b, rw_w_o,
           sc_w_in, sc_conv_w, sc_w_out,
           fx_w_qkvf, fx_b_f, fx_w_o, final_g):
    A = lambda a: np.asarray(a, dtype=np.float32)
    x = A(x)
    n1, n2, gf = A(norm1_g), A(norm2_g), A(final_g)
    w1, w2 = A(mlp_w1), A(mlp_w2)
    mp = dict(w_in=_f32(cc_w_in[0]), b_in=A(cc_b_in[0]), dw=A(cc_dw[0]), dw_b=A(cc_dw_b[0]), ln_g=A(cc_ln_g[0]),
              ln_b=A(cc_ln_b[0]), w_out=_f32(cc_w_out[0]), b_out=A(cc_b_out[0]))
    x = run_tok("cc", False, x, n1[0], n2[0], gf, _f32(w1[0]), _f32(w2[0]), mp)
    p = dict(mu=A(rw_mu[0]), w_rkv=A(rw_w_rkv[0]), w0=A(rw_w0[0]), w1=A(rw_w1[0]), w2=A(rw_w2[0]), a0=A(rw_a0[0]),
             a1=A(rw_a1[0]), a2=A(rw_a2[0]), g1=A(rw_g1[0]), g2=A(rw_g2[0]), k_k=A(rw_k_k[0]), k_a=A(rw_k_a[0]),
             r_k=A(rw_r_k[0]), gn_g=A(rw_gn_g[0]), gn_b=A(rw_gn_b[0]))
    m = run_rwkv(x, n1[1], p)
    x = run_tok("post", False, x, n1[1], n2[1], gf, _f32(w1[1]), _f32(w2[1]), dict(w_out=_f32(rw_w_o[0])), m_in=m)
    mp = dict(w_in=_f32(sc_w_in[0]), conv_w=A(sc_conv_w[0]), w_out=_f32(sc_w_out[0]))
    x = run_tok("sc", False, x, n1[2], n2[2], gf, _f32(w1[2]), _f32(w2[2]), mp)
    m = run_fox(x, n1[3], A(fx_w_qkvf[0]), A(fx_b_f[0]))
    x = run_tok("post", True, x, n1[3], n2[3], gf, _f32(w1[3]), _f32(w2[3]), dict(w_out=_f32(fx_w_o[0])), m_in=m)
    return x


def build_fused(T=T):
    nc = bass.Bass("TRN2", target_bir_lowering=False)
    ext = lambda n, s: nc.dram_tensor(n, list(s), F32, kind="ExternalInput").ap()
    internal = lambda n, s: nc.dram_tensor(n, list(s), F32).ap()
    xT = ext("xT", [D, T])
    zh = ext("zh", [D, NH])
    out = nc.dram_tensor("out", [D, T], F32, kind="ExternalOutput").ap()
    w1 = [ext("w1_%d" % l, [D, DFF]) for l in range(4)]
    w2 = [ext("w2_%d" % l, [DFF, D]) for l in range(4)]
    vec = [ext("vec_%d" % l, [128, nv]) for l, nv in enumerate((50 + 593, 50, 50 + 49, 50))]
    cc_w_in = ext("cc_w_in", [D, 2 * D]); cc_w_out = ext("cc_w_out", [D, D])
    sc_w_in = ext("sc_w_in", [D, 3 * D]); sc_w_out = ext("sc_w_out", [D, D])
    rw_w_o = ext("rw_w_o", [D, D]); fx_w_o = ext("fx_w_o", [D, D])
    rw = []
    for h in range(2):
        rw.append(dict(wrkv=ext("rw_wrkv_%d" % h, [D, 3 * RF]), w2=ext("rw_w2_%d" % h, [96, RF]), a2=ext("rw_a2_%d" % h, [96, RF]),
                       g2=ext("rw_g2_%d" % h, [256, RF]), vec=ext("rw_vec_%d" % h, [128, 170])))
    rw_w1 = ext("rw_w1", [D, 96]); rw_a1 = ext("rw_a1", [D, 96]); rw_g1 = ext("rw_g1", [D, 256])
    bones = ext("bones", [128, 128]); rmask = ext("rmask", [128, NT]); mask5 = ext("mask5", [RN, 320])
    ident64 = ext("ident64", [RN, RN]); gneps = ext("gneps", [RN, 1])
    fx = []
    for h in range(2):
        fx.append(dict(wall=ext("fx_wall_%d" % h, [D, 3 * FH * FDH]), wf=ext("fx_wf_%d" % h, [D, FH]), vec=ext("fx_vec_%d" % h, [128, 20])))
    ident128 = ext("ident128", [128, 128]); fmask = ext("fmask", [128, 896]); sel = ext("sel", [FH, FH * 128])
    X1 = internal("X1", [D, T]); X2 = internal("X2", [D, T]); X3 = internal("X3", [D, T])
    YG = internal("YG", [D, T]); OO = internal("OO", [D, T])
    RA = {n: internal("RA_" + n, [RF, T]) for n in RA_OUTS}
    RA_PC = internal("RA_PC", [RF, T // CH])

    with contextlib.ExitStack() as st:
        env = Env(nc, st)
        env.io = dict(xT=xT, out=X1, w1=w1[0], w2=w2[0], xh=zh, w_in=cc_w_in, w_out=cc_w_out, vec=vec[0])
        build_tok("cc", False, ntok=T, env=env)
        for h in range(2):
            env.io = dict(xT=X1, wrkv=rw[h]["wrkv"], w1=rw_w1, w2=rw[h]["w2"], a1=rw_a1, a2=rw[h]["a2"], g1=rw_g1, g2=rw[h]["g2"],
                          vec=rw[h]["vec"], bones=bones, rmask=rmask, PC=RA_PC, **{n: RA[n] for n in RA_OUTS})
            build_rwkv_a(T, env=env)
            io = {n: RA[n].rearrange("(h j) t -> j h t", j=RN) for n in RA_OUTS}
            io.update(PC=RA_PC.rearrange("(h j) c -> j h c", j=RN), mask5=mask5, ident=ident64, eps=gneps,
                      out=YG[h * RF:(h + 1) * RF, :].rearrange("(h i) t -> i h t", i=RN))
            env.io = io
            build_rwkv_b(T, env=env)
        env.io = dict(xT=X1, out=X2, w1=w1[1], w2=w2[1], mT=YG, w_out=rw_w_o, vec=vec[1])
        build_tok("post", False, ntok=T, env=env)
        env.io = dict(xT=X2, out=X3, w1=w1[2], w2=w2[2], xh=zh, w_in=sc_w_in, w_out=sc_w_out, vec=vec[2])
        build_tok("sc", False, ntok=T, env=env)
        for h in range(2):
            env.io = dict(xT=X3, wall=fx[h]["wall"], wf=fx[h]["wf"], vec=fx[h]["vec"], ident=ident128, mask=fmask, sel=sel,
                          out=OO[h * FH * FDH:(h + 1) * FH * FDH, :])
            build_fox(T, env=env)
        env.io = dict(xT=X3, out=out, w1=w1[3], w2=w2[3], mT=OO, w_out=fx_w_o, vec=vec[3])
        build_tok("post", True, ntok=T, env=env)
        env.cx.P.barrier()
        env.cx.P.emit()
    return nc


def fused_inputs(x_bT, p):
    A = lambda a: np.asarray(a, dtype=np.float32)
    Tn = x_bT.shape[0]
    m = {"xT": _f32(x_bT.T), "zh": np.zeros((D, NH), np.float32)}
    eps2 = [np.full((128, 1), RMS_EPS, np.float32), np.full((128, 1), LN_EPS, np.float32)]
    zero1 = np.zeros((128, 1), np.float32)
    gf = pack_vec(p["final_g"])
    for l in range(4):
        m["w1_%d" % l] = _f32(p["mlp_w1"][l])
        m["w2_%d" % l] = _f32(p["mlp_w2"][l])
    base = lambda l: [pack_vec(p["norm1_g"][l]), pack_vec(p["norm2_g"][l]), gf] + eps2
    m["vec_0"] = _f32(np.concatenate(base(0) + [pack_vec(p["cc_b_in"][0])] + [pack_vec(p["cc_dw"][0][k]) for k in range(31)] +
                                     [pack_vec(p["cc_dw_b"][0]), pack_vec(p["cc_ln_g"][0]), pack_vec(p["cc_ln_b"][0]),
                                      pack_vec(p["cc_b_out"][0]), zero1], axis=1))
    m["vec_1"] = _f32(np.concatenate(base(1), axis=1))
    m["vec_2"] = _f32(np.concatenate(base(2) + [pack_vec(p["sc_conv_w"][0][k]) for k in range(3)] + [zero1], axis=1))
    m["vec_3"] = _f32(np.concatenate(base(3), axis=1))
    m["cc_w_in"] = _f32(p["cc_w_in"][0]); m["cc_w_out"] = _f32(p["cc_w_out"][0])
    m["sc_w_in"] = _f32(p["sc_w_in"][0]); m["sc_w_out"] = _f32(p["sc_w_out"][0])
    m["rw_w_o"] = _f32(p["rw_w_o"][0]); m["fx_w_o"] = _f32(p["fx_w_o"][0])
    bones, rmask, mask5, ident64 = rwkv_consts()
    m.update(bones=bones, rmask=rmask, mask5=mask5, ident64=ident64, gneps=np.full((RN, 1), GN_EPS, np.float32))
    m["rw_w1"] = _f32(p["rw_w1"][0]); m["rw_a1"] = _f32(p["rw_a1"][0]); m["rw_g1"] = _f32(p["rw_g1"][0])
    wr = A(p["rw_w_rkv"][0])
    for h in range(2):
        own = slice(h * RF, (h + 1) * RF)
        m["rw_wrkv_%d" % h] = _f32(np.concatenate([wr[0][:, own], wr[1][:, own], wr[2][:, own]], axis=1))
        m["rw_w2_%d" % h] = _f32(p["rw_w2"][0][:, own]); m["rw_a2_%d" % h] = _f32(p["rw_a2"][0][:, own])
        m["rw_g2_%d" % h] = _f32(p["rw_g2"][0][:, own])
        vec = np.zeros((128, 170), np.float32)
        vec[:, 0:16] = pack_vec(p["norm1_g"][1]); vec[:, 16] = RMS_EPS
        for i in range(6):
            vec[:, 18 + 16 * i:18 + 16 * (i + 1)] = pack_vec(p["rw_mu"][0][i])
        ownv = [p["rw_w0"][0], p["rw_a0"][0], p["rw_k_k"][0], p["rw_k_a"][0], p["rw_gn_g"][0], p["rw_gn_b"][0],
                A(p["rw_r_k"][0]).reshape(-1)]
        for i, v in enumerate(ownv):
            vec[:, 114 + 8 * i:114 + 8 * (i + 1)] = pack_vec(A(v)[own])
        m["rw_vec_%d" % h] = vec
    ident128, fmask, sel = fox_consts()
    m.update(ident128=ident128, fmask=fmask, sel=sel)
    wq = A(p["fx_w_qkvf"][0])
    for h in range(2):
        h0 = h * FH
        cols = slice(h0 * FDH, (h0 + FH) * FDH)
        m["fx_wall_%d" % h] = _f32(np.concatenate([wq[:, 0:D][:, cols], wq[:, D:2 * D][:, cols], wq[:, 2 * D:3 * D][:, cols]], axis=1))
        m["fx_wf_%d" % h] = _f32(wq[:, 3 * D + h0:3 * D + h0 + FH])
        vec = np.zeros((128, 20), np.float32)
        vec[:, 0:16] = pack_vec(p["norm1_g"][3]); vec[:, 16] = RMS_EPS; vec[:, 17] = 1.0
        vec[0:FH, 18] = A(p["fx_b_f"][0])[h0:h0 + FH]
        m["fx_vec_%d" % h] = vec
    return m


def kernel_fused(**p):
    x = np.asarray(p["x"], dtype=np.float32)
    nc = _get_nc(("fused",), build_fused)
    shared = None
    in_maps = []
    for c in range(NCORES):
        b = c // 2
        if shared is None:
            shared = fused_inputs(x[b], p)
            m = shared
        else:
            m = dict(shared)
            m["xT"] = _f32(x[b].T)
        in_maps.append(m)
    res = run_bass_kernel_spmd(nc, in_maps, core_ids=list(range(NCORES))).results
    y = np.empty((B, T, D), np.float32)
    for b in range(B):
        y[b] = res[2 * b]["out"].T
    return y


def kernel(**inputs):
    return kernel_fused(**inputs)
```

```python
import contextlib
import numpy as np
import concourse.bass as bass
import concourse.mybir as mybir
from concourse.bass_utils import run_bass_kernel_spmd

F32 = mybir.dt.float32
BF16 = mybir.dt.bfloat16
AF = mybir.ActivationFunctionType
ALU = mybir.AluOpType
AX = mybir.AxisListType

D = 2048
DC = 16
DFF = 8192
B = 4
T = 4096
NCORES = 8


class Buf:
    __slots__ = ("name", "w", "r", "dsem")

    def __init__(self, name):
        self.name = name
        self.w = None
        self.r = {}
        self.dsem = None


class Prog:
    ENGS = ("pe", "act", "dve", "pool", "sp")

    def __init__(self, nc):
        self.nc = nc
        self.ops = {e: [] for e in self.ENGS}
        self.seen = {e: {} for e in self.ENGS}
        self.ndsem = 0
        self.dcount = []
        self.free_dsems = []
        self.dkind = []

    def _dsem(self, buf, eng="sp"):
        if buf.dsem is None:
            kind = "sw" if eng == "pool" else "hw"
            fl = [i for i in self.free_dsems if self.dkind[i] == kind]
            if fl:
                buf.dsem = fl[-1]
                self.free_dsems.remove(fl[-1])
            else:
                buf.dsem = self.ndsem
                self.ndsem += 1
                self.dcount.append(0)
                self.dkind.append(kind)
        return buf.dsem

    def barrier(self):
        last = {}
        for e in self.ENGS:
            for i in range(len(self.ops[e]) - 1, -1, -1):
                o = self.ops[e][i]
                if o["fn"] is not None and o["dma"] is None:
                    last[e] = ("eng", e, i)
                    break
        for e in self.ENGS:
            deps = [tok for e2, tok in last.items() if e2 != e]
            deps += [("dma", s_, c) for s_, c in enumerate(self.dcount) if c > 0]
            idx = len(self.ops[e])
            waits = self._waits(e, idx, deps)
            self.ops[e].append({"waits": waits, "fn": None, "sig": False, "dma": None})
        self.free_dsems = list(range(self.ndsem))

    def _waits(self, eng, idx, deps):
        waits = []
        seen = self.seen[eng]
        for d in deps:
            if d[0] == "eng":
                _, e2, i2 = d
                if e2 == eng:
                    if eng in ("pe", "sp"):
                        continue
                key = ("eng", e2)
                if seen.get(key, -1) >= i2:
                    continue
                seen[key] = i2
                self.ops[e2][i2]["sig"] = True
                waits.append(d)
            else:
                _, s, c = d
                key = ("dma", s)
                if seen.get(key, -1) >= c:
                    continue
                seen[key] = c
                waits.append(d)
        return waits

    def _deps(self, reads, writes):
        deps = []
        for b in reads:
            if b.w is not None:
                deps.append(b.w)
        for b in writes:
            if b.w is not None:
                deps.append(b.w)
            deps.extend(b.r.values())
        return deps

    def op(self, eng, fn, reads=(), writes=()):
        idx = len(self.ops[eng])
        waits = self._waits(eng, idx, self._deps(reads, writes))
        self.ops[eng].append({"waits": waits, "fn": fn, "sig": False, "dma": None})
        tok = ("eng", eng, idx)
        for b in reads:
            b.r[("eng", eng)] = tok
        for b in writes:
            b.w = tok
            b.r = {}
        return tok

    def dma(self, eng, out, in_, reads=(), writes=(), sembuf=None, **kw):
        idx = len(self.ops[eng])
        waits = self._waits(eng, idx, self._deps(reads, writes))
        sb = sembuf or (writes[0] if writes else reads[0])
        s = self._dsem(sb, eng)
        self.dcount[s] += 16
        tok = ("dma", s, self.dcount[s])
        self.ops[eng].append({"waits": waits, "fn": lambda e: e.dma_start(out=out, in_=in_, **kw),
                              "sig": False, "dma": s})
        for b in reads:
            b.r[("dma", s)] = tok
        for b in writes:
            b.w = tok
            b.r = {}
        return tok

    def finish(self, bufs):
        deps = []
        for b in bufs:
            if b.w is not None:
                deps.append(b.w)
            deps.extend(b.r.values())
        idx = len(self.ops["sp"])
        waits = self._waits("sp", idx, deps)
        self.ops["sp"].append({"waits": waits, "fn": None, "sig": False, "dma": None})

    def emit(self):
        nc = self.nc
        with contextlib.ExitStack() as st:
            esem = {e: st.enter_context(nc.semaphore("s_" + e)) for e in self.ENGS}
            dsem = [st.enter_context(nc.semaphore("d%d" % i)) for i in range(self.ndsem)]
            cnt = {}
            for e in self.ENGS:
                c = 0
                arr = []
                for o in self.ops[e]:
                    if o["sig"]:
                        c += 1
                    arr.append(c)
                cnt[e] = arr
            block = st.enter_context(nc.Block())

            def run(e, eng):
                for o in self.ops[e]:
                    for w in o["waits"]:
                        if w[0] == "eng":
                            eng.wait_ge(esem[w[1]], cnt[w[1]][w[2]])
                        else:
                            eng.wait_ge(dsem[w[1]], w[2])
                    if o["fn"] is None:
                        continue
                    ins = o["fn"](eng)
                    if o["dma"] is not None:
                        ins.then_inc(dsem[o["dma"]], 16)
                    elif o["sig"]:
                        ins.then_inc(esem[e], 1)

            @block.tensor
            def _(eng):
                run("pe", eng)

            @block.scalar
            def _(eng):
                run("act", eng)

            @block.vector
            def _(eng):
                run("dve", eng)

            @block.gpsimd
            def _(eng):
                run("pool", eng)

            @block.sync
            def _(eng):
                run("sp", eng)


def _each(rng):
    def deco(fn):
        for x in rng:
            fn(x)
        return fn
    return deco


class Ctx:
    def __init__(self, nc, st):
        self.nc = nc
        self.st = st
        self.P = Prog(nc)
        self.n = 0
        self.psum = []
        self.pi = 0

    def sb(self, shape, dt, name=None):
        self.n += 1
        arena = getattr(self, "arena", None)
        if arena is None:
            return self.st.enter_context(self.nc.sbuf_tensor("sb_" + (name or ("t%d" % self.n)), list(shape), dt))
        shape = list(shape)
        elems = 1
        for d_ in shape[1:]:
            elems *= d_
        esz = 2 if dt == BF16 else 4
        words = (elems * esz + 3) // 4
        words = (words + 7) // 8 * 8
        assert self.off + words <= self.arena_words, ("arena overflow", name, self.off, words)
        v = arena[0:shape[0], self.off:self.off + words]
        self.off += words
        if dt == BF16:
            v = v.bitcast(BF16)
        v = v[:, 0:elems]
        if len(shape) == 3:
            v = v.rearrange("p (a b) -> p a b", a=shape[1])
        elif len(shape) == 4:
            v = v.rearrange("p (a b c) -> p a b c", a=shape[1], b=shape[2])
        return v

    def init_psum(self, nbanks=8):
        for i in range(nbanks):
            t = self.st.enter_context(self.nc.psum_tensor("ps%d" % i, [128, 512], F32))
            self.psum.append((t, Buf("ps%d" % i)))

    def ps(self):
        n = getattr(self, "rot_n", len(self.psum))
        r = self.psum[self.pi % n]
        self.pi += 1
        return r


class Env:
    def __init__(self, nc, st):
        self.nc = nc
        self.st = st
        self.cx = Ctx(nc, st)
        cx = self.cx
        cx.arena_words = 53000
        cx.arena = st.enter_context(nc.sbuf_tensor("arena", [128, cx.arena_words], F32))
        cx.off = 0
        self.psd = [st.enter_context(nc.psum_tensor("psd%d" % i, [128, 1024], F32)) for i in range(4)]
        self.io = {}
        self.first = True

    def dram(self, n, s, k="ExternalInput"):
        ap = self.io[n]
        assert list(ap.shape) == list(s), (n, ap.shape, s)
        return ap

    @contextlib.contextmanager
    def scope(self):
        cx = self.cx
        if not self.first:
            cx.P.barrier()
        self.first = False
        cx.off = 0
        cx.rot_n = 8
        cx.pi = 0
        cx.psum = [(self.psd[i // 2][:, (i % 2) * 512:(i % 2 + 1) * 512], Buf("ps%d" % i)) for i in range(8)]
        yield self.st


class WStream:
    def __init__(self, cx, nslots=3, kc=16, cols=512):
        self.cx = cx
        self.slots = [(cx.sb([128, kc, cols], BF16, "wsl%d" % i), Buf("wsl%d" % i)) for i in range(nslots)]
        self.i = 0
        self.kc = kc
        self.cols = cols

    def load(self, W, k0, kp, nk, c0, ncols):
        t, b = self.slots[self.i % len(self.slots)]
        self.i += 1
        src = W[k0:k0 + nk * kp, c0:c0 + ncols].rearrange("(kc p) m -> p kc m", p=kp)
        self.cx.P.dma("pool", t[0:kp, 0:nk, 0:ncols], src, writes=[b])
        return t, b


def gemm(cx, ws, W, K, M, act, N, epi, kp=128, mgrp=3, mp=128):
    P = cx.P
    nk = K // kp
    nm = M // mp
    kblk = ws.kc
    nkb = (nk + kblk - 1) // kblk
    for m0 in range(0, nm, mgrp):
        mg = min(mgrp, nm - m0)
        pss = [cx.ps() for _ in range(mg)]
        for kb in range(nkb):
            nkc = min(kblk, nk - kb * kblk)
            wt, wb = ws.load(W, kb * kblk * kp, kp, nkc, m0 * mp, mg * mp)
            for mi in range(mg):
                pt, pb = pss[mi]
                for kc in range(nkc):
                    a_ap, a_buf = act(kb * kblk + kc)
                    first = (kb == 0 and kc == 0)
                    last = (kb == nkb - 1 and kc == nkc - 1)
                    P.op("pe", (lambda e, pt=pt, wt=wt, kc=kc, mi=mi, a_ap=a_ap, first=first, last=last:
                                e.matmul(pt[0:mp, 0:N], wt[0:kp, kc, mi * mp:(mi + 1) * mp], a_ap,
                                         start=first, stop=last)),
                         reads=[wb, a_buf], writes=[pb])
        for mi in range(mg):
            pt, pb = pss[mi]
            epi(m0 + mi, pt[0:mp, 0:N], pb)


NTOK = 2048
NT = 512
NH = 32
RMS_EPS = 1e-6
LN_EPS = 1e-5


def pack_vec(v):
    v = np.ascontiguousarray(np.asarray(v, dtype=np.float32).reshape(-1))
    return np.ascontiguousarray(v.reshape(-1, 128).T)


def build_tok(mixer, final, ntok=NTOK, env=None):
    nc = env.nc if env else bass.Bass("TRN2", target_bir_lowering=False)
    dram = env.dram if env else (lambda n, s, k="ExternalInput": nc.dram_tensor(n, list(s), F32, kind=k).ap())
    xT = dram("xT", [D, ntok])
    out = dram("out", [D, ntok], "ExternalOutput")
    w1 = dram("w1", [D, DFF])
    w2 = dram("w2", [DFF, D])
    o_eps = 48
    VB = 50
    if mixer == "cc":
        xh = dram("xh", [D, NH])
        w_in = dram("w_in", [D, 2 * D])
        w_out = dram("w_out", [D, D])
        o_bin, o_dw, o_dwb, o_lng, o_lnb, o_bout = VB, VB + 32, VB + 32 + 496, VB + 544, VB + 560, VB + 576
        o_mask = VB + 592
        NV = VB + 593
        CARRY = 30
    elif mixer == "sc":
        xh = dram("xh", [D, NH])
        w_in = dram("w_in", [D, 3 * D])
        w_out = dram("w_out", [D, D])
        o_cw = VB
        o_mask = VB + 48
        NV = VB + 49
        CARRY = 2
    else:
        mT = dram("mT", [D, ntok])
        w_out = dram("w_out", [D, D])
        NV = VB
    vec_d = dram("vec", [128, NV])

    with (env.scope() if env else contextlib.ExitStack()) as st:
        cx = env.cx if env else Ctx(nc, st)
        P = cx.P
        if env is None:
            cx.init_psum(8)
        ws = WStream(cx, nslots=2, cols=384)
        vec = cx.sb([128, NV], F32, "vec")
        vec_b = Buf("vec")
        ones = cx.sb([128, 128], BF16, "ones")
        ones_b = Buf("ones")
        xs = [(cx.sb([128, NT], F32, "x%d" % c), Buf("x%d" % c)) for c in range(DC)]
        hs = [(cx.sb([128, NT], BF16, "h%d" % c), Buf("h%d" % c)) for c in range(DC)]
        qs = [(cx.sb([128, NT], BF16, "q%d" % c), Buf("q%d" % c)) for c in range(DC)]
        BIGW = 1568
        bigs = [(cx.sb([128, BIGW], F32, "big%d" % c), Buf("big%d" % c)) for c in range(DC)]
        sq = [(cx.sb([128, NT], BF16, "sq%d" % i), Buf("sq%d" % i)) for i in range(2)]
        tmpb = [(cx.sb([128, NT], BF16, "tb%d" % i), Buf("tb%d" % i)) for i in range(2)]
        tmpf = [(cx.sb([128, NT], F32, "tf%d" % i), Buf("tf%d" % i)) for i in range(3)]
        rstd = (cx.sb([128, NT], F32, "rstd"), Buf("rstd"))
        mean = (cx.sb([128, NT], F32, "mean"), Buf("mean"))
        cnt = {"sq": 0, "tb": 0, "tf": 0, "ev": 0}

        def rot(lst, key):
            r = lst[cnt[key] % len(lst)]
            cnt[key] += 1
            return r

        def a_view(k):
            t, b = bigs[k // 4]
            return t[:, 0:1024].bitcast(BF16)[:, (k % 4) * NT:(k % 4 + 1) * NT], b

        P.dma("sp", vec[:, :], vec_d[:, :], writes=[vec_b])
        P.op("dve", lambda e: e.memset(ones[:, :], 1.0), writes=[ones_b])

        def vcol(o, n=1):
            return vec[:, o:o + n]

        def rmsnorm(src, gcol, dst, N, dst_is_x=False):
            pt, pb = cx.ps()
            for c in range(DC):
                s_t, s_b = rot(sq, "sq")
                x_ap, x_b = src[c]
                P.op("act", lambda e, s_t=s_t, x_ap=x_ap: e.activation(out=s_t[:, 0:N], in_=x_ap, func=AF.Square),
                     reads=[x_b], writes=[s_b])
                P.op("pe", lambda e, s_t=s_t, c=c: e.matmul(pt[:, 0:N], ones[:, :], s_t[:, 0:N],
                                                          start=(c == 0), stop=(c == DC - 1)),
                     reads=[s_b, ones_b], writes=[pb])
            r_t, r_b = rstd
            P.op("act", lambda e: e.activation(out=r_t[:, 0:N], in_=pt[:, 0:N], func=AF.Sqrt, scale=1.0 / D,
                                               bias=vcol(o_eps)), reads=[pb, vec_b], writes=[r_b])
            P.op("dve", lambda e: e.reciprocal(out=r_t[:, 0:N], in_=r_t[:, 0:N]), reads=[r_b], writes=[r_b])
            for c in range(DC):
                x_ap, x_b = src[c]
                d_ap, d_b = dst[c]
                eng = "dve"
                P.op(eng, lambda e, x_ap=x_ap, d_ap=d_ap, c=c: e.scalar_tensor_tensor(
                    out=d_ap, in0=x_ap, scalar=vcol(gcol + c), in1=r_t[:, 0:N], op0=ALU.mult, op1=ALU.mult),
                    reads=[x_b, r_b, vec_b], writes=[d_b])

        def evac_engine():
            cnt["ev"] += 1
            return "act" if cnt["ev"] % 2 else "dve"

        def mixer_sc(N, halo):
            def gb(c): return bigs[c][0][:, 0:NT]
            def gc(c): return bigs[c][0][:, 512:512 + NT]
            def pp(c): return bigs[c][0][:, 1024:1024 + CARRY + NT]

            def epi(mc, ps, pb):
                if mc < 16:
                    if halo:
                        return
                    t, b = bigs[mc]
                    P.op("act", lambda e: e.copy(out=gb(mc)[:, 0:N], in_=ps), reads=[pb], writes=[b])
                elif mc < 32:
                    c = mc - 16
                    t, b = bigs[c]
                    P.op("act", lambda e: e.copy(out=gc(c)[:, 0:N], in_=ps), reads=[pb], writes=[b])
                else:
                    c = mc - 32
                    t, b = bigs[c]
                    if halo:
                        f_t, f_b = rot(tmpf, "tf")
                        P.op("dve", lambda e: e.tensor_tensor(out=f_t[:, 0:N], in0=gc(c)[:, 0:N], in1=ps, op=ALU.mult),
                             reads=[pb, b], writes=[f_b])
                        P.op("dve", lambda e: e.tensor_scalar(out=pp(c)[:, 0:CARRY], in0=f_t[:, N - CARRY:N],
                                                              scalar1=vcol(o_mask), scalar2=None, op0=ALU.mult),
                             reads=[f_b, vec_b], writes=[b])
                        return
                    P.op("dve", lambda e: e.tensor_tensor(out=pp(c)[:, CARRY:CARRY + N], in0=gc(c)[:, 0:N], in1=ps,
                                                          op=ALU.mult), reads=[pb, b], writes=[b])
                    z_t, z_b = rot(tmpf, "tf")
                    P.op("dve", lambda e: e.tensor_scalar(out=z_t[:, 0:N], in0=pp(c)[:, 0:N], scalar1=vcol(o_cw + c),
                                                          scalar2=None, op0=ALU.mult), reads=[b, vec_b], writes=[z_b])
                    for k in (1, 2):
                        P.op("dve", lambda e, k=k: e.scalar_tensor_tensor(
                            out=z_t[:, 0:N], in0=pp(c)[:, k:k + N], scalar=vcol(o_cw + 16 * k + c), in1=z_t[:, 0:N],
                            op0=ALU.mult, op1=ALU.add), reads=[b, vec_b, z_b], writes=[z_b])
                    q_t, q_b = qs[c]
                    P.op("dve", lambda e: e.tensor_tensor(out=q_t[:, 0:N], in0=gb(c)[:, 0:N], in1=z_t[:, 0:N],
                                                          op=ALU.mult), reads=[b, z_b], writes=[q_b])
                    P.op("act", lambda e: e.copy(out=pp(c)[:, 0:CARRY], in_=pp(c)[:, N:N + CARRY]),
                         reads=[b], writes=[b])

            gemm(cx, ws, w_in, D, 3 * D, lambda kc: (hs[kc][0][:, 0:N], hs[kc][1]), N, epi)
            if halo:
                return

            def epi_o(mc, ps, pb):
                x_t, x_b = xs[mc]
                P.op("dve", lambda e: e.tensor_tensor(out=x_t[:, 0:N], in0=x_t[:, 0:N], in1=ps, op=ALU.add),
                     reads=[pb, x_b], writes=[x_b])

            gemm(cx, ws, w_out, D, D, lambda kc: (qs[kc][0][:, 0:N], qs[kc][1]), N, epi_o)

        def mixer_cc(N, halo):
            def cv(c): return bigs[c][0][:, 0:NT]
            def uu(c): return bigs[c][0][:, 1024:1024 + CARRY + NT]
            KW = 31

            def epi(mc, ps, pb):
                if mc < 16:
                    t, b = bigs[mc]
                    P.op("act", lambda e: e.activation(out=cv(mc)[:, 0:N], in_=ps, func=AF.Identity,
                                                       bias=vcol(o_bin + mc)),
                         reads=[pb, vec_b], writes=[b])
                else:
                    c = mc - 16
                    t, b = bigs[c]
                    f_t, f_b = rot(tmpf, "tf")
                    P.op("act", lambda e: e.activation(out=f_t[:, 0:N], in_=ps, func=AF.Sigmoid,
                                                       bias=vcol(o_bin + 16 + c)),
                         reads=[pb, vec_b], writes=[f_b])
                    if halo:
                        P.op("dve", lambda e: e.tensor_tensor(out=f_t[:, 0:N], in0=cv(c)[:, 0:N], in1=f_t[:, 0:N],
                                                              op=ALU.mult), reads=[b, f_b], writes=[f_b])
                        P.op("dve", lambda e: e.tensor_scalar(out=uu(c)[:, 0:CARRY], in0=f_t[:, N - CARRY:N],
                                                              scalar1=vcol(o_mask), scalar2=None, op0=ALU.mult),
                             reads=[f_b, vec_b], writes=[b])
                        return
                    P.op("dve", lambda e: e.tensor_tensor(out=uu(c)[:, CARRY:CARRY + N], in0=cv(c)[:, 0:N],
                                                          in1=f_t[:, 0:N], op=ALU.mult), reads=[b, f_b], writes=[b])
                    eng = "dve"
                    P.op(eng, lambda e: e.tensor_scalar(out=cv(c)[:, 0:N], in0=uu(c)[:, 0:N], scalar1=vcol(o_dw + c),
                                                        scalar2=vcol(o_dwb + c), op0=ALU.mult, op1=ALU.add),
                         reads=[b, vec_b], writes=[b])
                    for k in range(1, KW):
                        P.op(eng, lambda e, k=k: e.scalar_tensor_tensor(
                            out=cv(c)[:, 0:N], in0=uu(c)[:, k:k + N], scalar=vcol(o_dw + 16 * k + c), in1=cv(c)[:, 0:N],
                            op0=ALU.mult, op1=ALU.add), reads=[b, vec_b], writes=[b])
                    P.op("act", lambda e: e.copy(out=uu(c)[:, 0:CARRY], in_=uu(c)[:, N:N + CARRY]),
                         reads=[b], writes=[b])

            gemm(cx, ws, w_in, D, 2 * D, lambda kc: (hs[kc][0][:, 0:N], hs[kc][1]), N, epi)
            if halo:
                return
            p1, p1b = cx.ps()
            p2, p2b = cx.ps()
            for c in range(DC):
                t, b = bigs[c]
                a_t, a_b = rot(tmpb, "tb")
                s_t, s_b = rot(sq, "sq")
                P.op("act", lambda e, a_t=a_t, c=c: e.copy(out=a_t[:, 0:N], in_=cv(c)[:, 0:N]), reads=[b], writes=[a_b])
                P.op("act", lambda e, s_t=s_t, c=c: e.activation(out=s_t[:, 0:N], in_=cv(c)[:, 0:N], func=AF.Square),
                     reads=[b], writes=[s_b])
                P.op("pe", lambda e, a_t=a_t, c=c: e.matmul(p1[:, 0:N], ones[:, :], a_t[:, 0:N], start=(c == 0),
                                                          stop=(c == DC - 1)), reads=[a_b, ones_b], writes=[p1b])
                P.op("pe", lambda e, s_t=s_t, c=c: e.matmul(p2[:, 0:N], ones[:, :], s_t[:, 0:N], start=(c == 0),
                                                          stop=(c == DC - 1)), reads=[s_b, ones_b], writes=[p2b])
            m_t, m_b = mean
            r_t, r_b = rstd
            P.op("dve", lambda e: e.tensor_scalar(out=m_t[:, 0:N], in0=p1[:, 0:N], scalar1=1.0 / D, scalar2=None,
                                                  op0=ALU.mult), reads=[p1b], writes=[m_b])
            f_t, f_b = rot(tmpf, "tf")
            P.op("dve", lambda e: e.tensor_tensor(out=f_t[:, 0:N], in0=m_t[:, 0:N], in1=m_t[:, 0:N], op=ALU.mult),
                 reads=[m_b], writes=[f_b])
            P.op("dve", lambda e: e.scalar_tensor_tensor(out=r_t[:, 0:N], in0=p2[:, 0:N], scalar=1.0 / D,
                                                         in1=f_t[:, 0:N], op0=ALU.mult, op1=ALU.subtract),
                 reads=[p2b, f_b], writes=[r_b])
            P.op("act", lambda e: e.activation(out=r_t[:, 0:N], in_=r_t[:, 0:N], func=AF.Sqrt,
                                               bias=vcol(o_eps + 1)), reads=[r_b, vec_b], writes=[r_b])
            P.op("dve", lambda e: e.reciprocal(out=r_t[:, 0:N], in_=r_t[:, 0:N]), reads=[r_b], writes=[r_b])
            for c in range(DC):
                t, b = bigs[c]
                g_t, g_b = rot(tmpf, "tf")
                P.op("dve", lambda e, c=c, g_t=g_t: e.tensor_tensor(out=g_t[:, 0:N], in0=cv(c)[:, 0:N], in1=m_t[:, 0:N],
                                                                  op=ALU.subtract), reads=[b, m_b], writes=[g_b])
                P.op("dve", lambda e, c=c, g_t=g_t: e.scalar_tensor_tensor(
                    out=g_t[:, 0:N], in0=g_t[:, 0:N], scalar=vcol(o_lng + c), in1=r_t[:, 0:N], op0=ALU.mult,
                    op1=ALU.mult), reads=[g_b, r_b, vec_b], writes=[g_b])
                q_t, q_b = qs[c]
                P.op("act", lambda e, c=c, g_t=g_t, q_t=q_t: e.activation(out=q_t[:, 0:N], in_=g_t[:, 0:N], func=AF.Silu,
                                                                        bias=vcol(o_lnb + c)),
                     reads=[g_b, vec_b], writes=[q_b])

            def epi_o(mc, ps, pb):
                x_t, x_b = xs[mc]
                P.op("dve", lambda e: e.scalar_tensor_tensor(out=x_t[:, 0:N], in0=ps, scalar=vcol(o_bout + mc),
                                                             in1=x_t[:, 0:N], op0=ALU.add, op1=ALU.add),
                     reads=[pb, x_b, vec_b], writes=[x_b])

            gemm(cx, ws, w_out, D, D, lambda kc: (qs[kc][0][:, 0:N], qs[kc][1]), N, epi_o)

        def mixer_post(N, t0):
            for c in range(DC):
                f_t, f_b = rot(tmpf, "tf")
                P.dma("sp", f_t[:, 0:N], mT[c * 128:(c + 1) * 128, t0:t0 + N], writes=[f_b])
                q_t, q_b = qs[c]
                P.op("act", lambda e, f_t=f_t, q_t=q_t: e.copy(out=q_t[:, 0:N], in_=f_t[:, 0:N]), reads=[f_b], writes=[q_b])

            def epi_o(mc, ps, pb):
                x_t, x_b = xs[mc]
                P.op("dve", lambda e: e.tensor_tensor(out=x_t[:, 0:N], in0=x_t[:, 0:N], in1=ps, op=ALU.add),
                     reads=[pb, x_b], writes=[x_b])

            gemm(cx, ws, w_out, D, D, lambda kc: (qs[kc][0][:, 0:N], qs[kc][1]), N, epi_o)

        def mlp(N):
            def epi1(mc, ps, pb):
                r_t, r_b = rot(tmpb, "tb")
                P.op("act", lambda e: e.activation(out=r_t[:, 0:N], in_=ps, func=AF.Relu), reads=[pb], writes=[r_b])
                a_ap, a_b = a_view(mc)
                P.op("dve", lambda e: e.tensor_tensor(out=a_ap[:, 0:N], in0=r_t[:, 0:N], in1=r_t[:, 0:N], op=ALU.mult),
                     reads=[r_b], writes=[a_b])

            gemm(cx, ws, w1, D, DFF, lambda kc: (hs[kc][0][:, 0:N], hs[kc][1]), N, epi1)

            def epi2(mc, ps, pb):
                x_t, x_b = xs[mc]
                P.op("dve", lambda e: e.tensor_tensor(out=x_t[:, 0:N], in0=x_t[:, 0:N], in1=ps, op=ALU.add),
                     reads=[pb, x_b], writes=[x_b])

            def actf(kc):
                ap, b = a_view(kc)
                return ap[:, 0:N], b

            gemm(cx, ws, w2, DFF, D, actf, N, epi2)

        if mixer in ("cc", "sc"):
            for c in range(DC):
                x_t, x_b = xs[c]
                P.dma("sp", x_t[:, 0:NH], xh[c * 128:(c + 1) * 128, :], writes=[x_b])
            rmsnorm([(xs[c][0][:, 0:NH], xs[c][1]) for c in range(DC)], 0,
                    [(hs[c][0][:, 0:NH], hs[c][1]) for c in range(DC)], NH)
            (mixer_cc if mixer == "cc" else mixer_sc)(NH, True)

        for ti in range(ntok // NT):
            t0 = ti * NT
            N = NT
            for c in range(DC):
                x_t, x_b = xs[c]
                P.dma("sp", x_t[:, 0:N], xT[c * 128:(c + 1) * 128, t0:t0 + N], writes=[x_b])
            xsrc = [(xs[c][0][:, 0:N], xs[c][1]) for c in range(DC)]
            hdst = [(hs[c][0][:, 0:N], hs[c][1]) for c in range(DC)]
            if mixer == "post":
                mixer_post(N, t0)
            else:
                rmsnorm(xsrc, 0, hdst, N)
                (mixer_cc if mixer == "cc" else mixer_sc)(N, False)
            rmsnorm(xsrc, 16, hdst, N)
            mlp(N)
            if final:
                rmsnorm(xsrc, 32, xsrc, N)
            for c in range(DC):
                x_t, x_b = xs[c]
                P.dma("sp", out[c * 128:(c + 1) * 128, t0:t0 + N], x_t[:, 0:N], reads=[x_b])
        if env is None:
            P.finish([b for _, b in xs])
            P.emit()
    return nc


_NC_CACHE = {}


def _get_nc(key, builder):
    if key not in _NC_CACHE:
        _NC_CACHE[key] = builder()
    return _NC_CACHE[key]


def _f32(a):
    return np.ascontiguousarray(np.asarray(a, dtype=np.float32))


def run_tok(mixer, final, x, g1, g2, gf, w1, w2, mp, m_in=None):
    nc = _get_nc(("tok", mixer, final), lambda: build_tok(mixer, final))
    base = [pack_vec(g1), pack_vec(g2), pack_vec(gf), np.full((128, 1), RMS_EPS, np.float32),
            np.full((128, 1), LN_EPS, np.float32)]
    in_maps = []
    for c in range(NCORES):
        b, half = c // 2, c % 2
        t0 = half * NTOK
        m = {"xT": _f32(x[b, t0:t0 + NTOK, :].T), "w1": w1, "w2": w2}
        mask = np.full((128, 1), 1.0 if half == 1 else 0.0, np.float32)
        if mixer in ("cc", "sc"):
            if half == 1:
                m["xh"] = _f32(x[b, t0 - NH:t0, :].T)
            else:
                m["xh"] = np.zeros((D, NH), np.float32)
        if mixer == "cc":
            vecs = base + [pack_vec(mp["b_in"]), ] + [pack_vec(mp["dw"][k]) for k in range(31)] + \
                [pack_vec(mp["dw_b"]), pack_vec(mp["ln_g"]), pack_vec(mp["ln_b"]), pack_vec(mp["b_out"]), mask]
            m["w_in"] = mp["w_in"]
            m["w_out"] = mp["w_out"]
        elif mixer == "sc":
            vecs = base + [pack_vec(mp["conv_w"][k]) for k in range(3)] + [mask]
            m["w_in"] = mp["w_in"]
            m["w_out"] = mp["w_out"]
        else:
            vecs = base
            m["mT"] = _f32(m_in[b, t0:t0 + NTOK, :].T)
            m["w_out"] = mp["w_out"]
        m["vec"] = _f32(np.concatenate(vecs, axis=1))
        in_maps.append(m)
    res = run_bass_kernel_spmd(nc, in_maps, core_ids=list(range(NCORES)))
    y = np.empty((B, T, D), np.float32)
    for c in range(NCORES):
        b, half = c // 2, c % 2
        t0 = half * NTOK
        y[b, t0:t0 + NTOK, :] = res.results[c]["out"].T
    return y


FH = 8
FDH = 128
FOX_NEG = -30000.0


def build_fox(T=T, env=None):
    nc = env.nc if env else bass.Bass("TRN2", target_bir_lowering=False)
    dram = env.dram if env else (lambda n, s, k="ExternalInput": nc.dram_tensor(n, list(s), F32, kind=k).ap())
    xT = dram("xT", [D, T])
    wall = dram("wall", [D, 3 * FH * FDH])
    wf_d = dram("wf", [D, FH])
    vec_d = dram("vec", [128, 20])
    ident_d = dram("ident", [128, 128])
    mask_d = dram("mask", [128, 896])
    sel_d = dram("sel", [FH, FH * 128])
    out = dram("out", [FH * FDH, T], "ExternalOutput")
    NTI = T // NT
    scale = 1.0 / float(np.sqrt(FDH))

    with (env.scope() if env else contextlib.ExitStack()) as st:
        cx = env.cx if env else Ctx(nc, st)
        P = cx.P
        if env is None:
            cx.init_psum(8)
        cx.rot_n = 4
        ws = WStream(cx, nslots=2, cols=256)
        vec = cx.sb([128, 20], F32, "vec"); vec_b = Buf("vec")
        ident = cx.sb([128, 128], F32, "ident"); ident_b = Buf("ident")
        mask = cx.sb([128, 896], F32, "mask"); mask_b = Buf("mask")
        sel = cx.sb([FH, FH * 128], F32, "sel"); sel_b = Buf("sel")
        ones = cx.sb([128, 128], BF16, "ones"); ones_b = Buf("ones")
        onesf = cx.sb([FH, NT], F32, "onesf"); onesf_b = Buf("onesf")
        wf = cx.sb([128, DC, FH], BF16, "wfs"); wf_b = Buf("wfs")
        negbf = cx.sb([FH, 1], F32, "negbf"); negbf_b = Buf("negbf")
        kT = cx.sb([128, FH, T], BF16, "kT"); kT_b = [Buf("kT%d" % j) for j in range(NTI)]
        vtm = cx.sb([128, T // 128, FH * FDH], BF16, "vtm"); vtm_b = [Buf("vtm%d" % j) for j in range(NTI)]
        qT = cx.sb([128, FH, NT], BF16, "qT"); qT_b = [Buf("qT%d" % h) for h in range(FH)]
        hs = [(cx.sb([128, NT], BF16, "h%d" % c), Buf("h%d" % c)) for c in range(DC)]
        xst = [(cx.sb([128, NT], F32, "xs%d" % i), Buf("xs%d" % i)) for i in range(2)]
        sq = [(cx.sb([128, NT], BF16, "sq%d" % i), Buf("sq%d" % i)) for i in range(2)]
        tmpf = [(cx.sb([128, NT], F32, "tf%d" % i), Buf("tf%d" % i)) for i in range(3)]
        ptb = [(cx.sb([128, NT], BF16, "pt%d" % i), Buf("pt%d" % i)) for i in range(3)]
        cqb = [(cx.sb([128, NT], F32, "cqb%d" % i), Buf("cqb%d" % i)) for i in range(1)]
        rstd = (cx.sb([128, NT], F32, "rstd"), Buf("rstd"))
        cfm = [(cx.sb([FH, NT], F32, "cfm%d" % i), Buf("cfm%d" % i)) for i in range(2)]
        lfm = (cx.sb([FH, NT], F32, "lfm"), Buf("lfm"))
        negc = cx.sb([128, T // 128, FH], F32, "negc"); negc_b = [Buf("negc%d" % j) for j in range(NTI)]
        KK = cx.sb([128, FH], F32, "KK"); KK_b = Buf("KK")
        kmax = (cx.sb([128, 1], F32, "kmax"), Buf("kmax"))
        ost = [(cx.sb([128, NT], F32, "ost%d" % i), Buf("ost%d" % i)) for i in range(1)]
        cnt = {}

        def rot(lst, key):
            cnt[key] = cnt.get(key, 0) + 1
            return lst[(cnt[key] - 1) % len(lst)]

        def vcol(o, n=1):
            return vec[:, o:o + n]

        P.dma("sp", vec[:, :], vec_d[:, :], writes=[vec_b])
        P.dma("sp", ident[:, :], ident_d[:, :], writes=[ident_b])
        P.dma("sp", mask[:, :], mask_d[:, :], writes=[mask_b])
        P.dma("sp", sel[:, :], sel_d[:, :], writes=[sel_b])
        P.dma("pool", wf[:, :, :], wf_d.rearrange("(kc p) m -> p kc m", p=128), writes=[wf_b])
        P.op("dve", lambda e: e.memset(ones[:, :], 1.0), writes=[ones_b])
        P.op("dve", lambda e: e.memset(onesf[:, :], 1.0), writes=[onesf_b])
        P.op("dve", lambda e: e.memset(KK[:, :], 0.0), writes=[KK_b])
        P.op("dve", lambda e: e.tensor_scalar(out=negbf[:, :], in0=vec[0:FH, 18:19], scalar1=-1.0, scalar2=None,
                                              op0=ALU.mult), reads=[vec_b], writes=[negbf_b])

        @_each(range(NTI))
        def _body_j(j):
            t0 = j * NT
            N = NT
            pt, pb = cx.ps()
            for c in range(DC):
                x_t, x_b = rot(xst, "xs")
                s_t, s_b = rot(sq, "sq")
                P.dma("sp", x_t[:, :], xT[c * 128:(c + 1) * 128, t0:t0 + N], writes=[x_b])
                P.op("act", lambda e, s_t=s_t, x_t=x_t: e.activation(out=s_t[:, :], in_=x_t[:, :], func=AF.Square),
                     reads=[x_b], writes=[s_b])
                P.op("pe", lambda e, s_t=s_t, c=c, pt=pt: e.matmul(pt[:, 0:N], ones[:, :], s_t[:, :], start=(c == 0),
                                                                 stop=(c == DC - 1)), reads=[s_b, ones_b], writes=[pb])
            r_t, r_b = rstd
            P.op("act", lambda e, pt=pt: e.activation(out=r_t[:, :], in_=pt[:, 0:N], func=AF.Sqrt, scale=1.0 / D,
                                                      bias=vcol(16)), reads=[pb, vec_b], writes=[r_b])
            P.op("dve", lambda e: e.reciprocal(out=r_t[:, :], in_=r_t[:, :]), reads=[r_b], writes=[r_b])
            for c in range(DC):
                x_t, x_b = rot(xst, "xs")
                P.dma("sp", x_t[:, :], xT[c * 128:(c + 1) * 128, t0:t0 + N], writes=[x_b])
                h_t, h_b = hs[c]
                P.op("dve", lambda e, x_t=x_t, h_t=h_t, c=c: e.scalar_tensor_tensor(
                    out=h_t[:, :], in0=x_t[:, :], scalar=vcol(c), in1=r_t[:, :], op0=ALU.mult, op1=ALU.mult),
                    reads=[x_b, r_b, vec_b], writes=[h_b])

            def epi_qk(mc, ps, pb, j=j, t0=t0):
                if mc < FH:
                    P.op("act", lambda e: e.mul(out=qT[:, mc, :], in_=ps, mul=scale), reads=[pb], writes=[qT_b[mc]])
                else:
                    hh = mc - FH
                    P.op("dve", lambda e: e.tensor_copy(out=kT[:, hh, t0:t0 + N], in_=ps), reads=[pb], writes=[kT_b[j]])

            gemm(cx, ws, wall[:, 0:2 * FH * FDH], D, 2 * FH * FDH, lambda kc: (hs[kc][0][:, :], hs[kc][1]), N, epi_qk, mgrp=2)

            for hh in range(FH):
                s_t, s_b = rot(sq, "sq")
                P.op("act", lambda e, s_t=s_t, hh=hh: e.activation(out=s_t[:, :], in_=kT[:, hh, t0:t0 + N], func=AF.Square),
                     reads=[kT_b[j]], writes=[s_b])
                p2, p2b = cx.ps()
                P.op("pe", lambda e, s_t=s_t, p2=p2: e.matmul(p2[:, 0:N], ones[:, :], s_t[:, :], start=True, stop=True),
                     reads=[s_b, ones_b], writes=[p2b])
                km_t, km_b = kmax
                P.op("dve", lambda e, p2=p2: e.tensor_reduce(out=km_t[:, :], in_=p2[:, 0:N], axis=AX.X, op=ALU.max),
                     reads=[p2b], writes=[km_b])
                P.op("dve", lambda e, hh=hh: e.tensor_tensor(out=KK[:, hh:hh + 1], in0=KK[:, hh:hh + 1], in1=km_t[:, :],
                                                            op=ALU.max), reads=[km_b, KK_b], writes=[KK_b])

            vc0 = 2 * FH * FDH
            for c0 in range(0, FH * FDH, 256):
                ncols = min(256, FH * FDH - c0)
                wt, wb = ws.load(wall, 0, 128, DC, vc0 + c0, ncols)
                for tb in range(4):
                    pv, pvb = cx.ps()
                    for kc in range(DC):
                        P.op("pe", lambda e, pv=pv, wt=wt, kc=kc, tb=tb, ncols=ncols: e.matmul(
                            pv[:, 0:ncols], hs[kc][0][:, tb * 128:(tb + 1) * 128], wt[:, kc, 0:ncols],
                            start=(kc == 0), stop=(kc == DC - 1)), reads=[wb, hs[kc][1]], writes=[pvb])
                    eng = "act" if tb % 2 == 0 else "dve"
                    if eng == "act":
                        P.op("act", lambda e, pv=pv, tb=tb, c0=c0, ncols=ncols: e.copy(
                            out=vtm[:, j * 4 + tb, c0:c0 + ncols], in_=pv[:, 0:ncols]), reads=[pvb], writes=[vtm_b[j]])
                    else:
                        P.op("dve", lambda e, pv=pv, tb=tb, c0=c0, ncols=ncols: e.tensor_copy(
                            out=vtm[:, j * 4 + tb, c0:c0 + ncols], in_=pv[:, 0:ncols]), reads=[pvb], writes=[vtm_b[j]])

            pf, pfb = cx.ps()
            for kc in range(DC):
                P.op("pe", lambda e, pf=pf, kc=kc: e.matmul(pf[0:FH, 0:N], wf[:, kc, :], hs[kc][0][:, :], start=(kc == 0),
                                                          stop=(kc == DC - 1)), reads=[wf_b, hs[kc][1]], writes=[pfb])
            l_t, l_b = lfm
            P.op("act", lambda e, pf=pf: e.activation(out=l_t[:, :], in_=pf[0:FH, 0:N], func=AF.Exp, scale=-1.0,
                                                      bias=negbf[:, :]), reads=[pfb, negbf_b], writes=[l_b])
            P.op("act", lambda e: e.activation(out=l_t[:, :], in_=l_t[:, :], func=AF.Ln, bias=vec[0:FH, 17:18]),
                 reads=[l_b, vec_b], writes=[l_b])
            c_t, c_b = cfm[j % 2]
            cp_t, cp_b = cfm[(j + 1) % 2]
            if j == 0:
                P.op("dve", lambda e, c_t=c_t: e.tensor_tensor_scan(out=c_t[:, :], data0=onesf[:, :], data1=l_t[:, :],
                                                                    initial=0.0, op0=ALU.mult, op1=ALU.subtract),
                     reads=[onesf_b, l_b], writes=[c_b])
            else:
                P.op("dve", lambda e, c_t=c_t, cp_t=cp_t: e.tensor_tensor_scan(
                    out=c_t[:, :], data0=onesf[:, :], data1=l_t[:, :], initial=cp_t[:, N - 1:N], op0=ALU.mult,
                    op1=ALU.subtract), reads=[onesf_b, l_b, cp_b], writes=[c_b])
            for tb in range(4):
                ptr, ptrb = cx.ps()
                P.op("pe", lambda e, ptr=ptr, tb=tb, c_t=c_t: e.transpose(out=ptr[:, 0:FH], in_=c_t[:, tb * 128:(tb + 1) * 128],
                                                                        identity=ident[0:FH, 0:FH]),
                     reads=[c_b, ident_b], writes=[ptrb])
                P.op("act", lambda e, ptr=ptr, tb=tb: e.mul(out=negc[:, j * 4 + tb, :], in_=ptr[:, 0:FH], mul=-1.0),
                     reads=[ptrb], writes=[negc_b[j]])

            for hh in range(FH):
                s_t, s_b = rot(sq, "sq")
                P.op("act", lambda e, s_t=s_t, hh=hh: e.activation(out=s_t[:, :], in_=qT[:, hh, :], func=AF.Square),
                     reads=[qT_b[hh]], writes=[s_b])
                pq, pqb = cx.ps()
                P.op("pe", lambda e, s_t=s_t, pq=pq: e.matmul(pq[:, 0:N], ones[:, :], s_t[:, :], start=True, stop=True),
                     reads=[s_b, ones_b], writes=[pqb])
                m_t, m_b = rot(tmpf, "tf")
                P.op("act", lambda e, pq=pq, m_t=m_t, hh=hh: e.activation(out=m_t[:, :], in_=pq[:, 0:N], func=AF.Sqrt,
                                                                         scale=KK[:, hh:hh + 1]),
                     reads=[pqb, KK_b], writes=[m_b])
                pc, pcb = cx.ps()
                P.op("pe", lambda e, pc=pc, hh=hh, c_t=c_t: e.matmul(pc[:, 0:N], sel[:, hh * 128:(hh + 1) * 128], c_t[:, :],
                                                                   start=True, stop=True), reads=[sel_b, c_b], writes=[pcb])
                q_t, q_b = rot(cqb, "cqb")
                P.op("dve", lambda e, q_t=q_t, m_t=m_t, pc=pc: e.scalar_tensor_tensor(
                    out=q_t[:, :], in0=m_t[:, :], scalar=-1.02, in1=pc[:, 0:N], op0=ALU.mult, op1=ALU.add),
                    reads=[m_b, pcb], writes=[q_b])
                accO, accOb = cx.psum[4 + 2 * (hh % 2)]
                accD, accDb = cx.psum[5 + 2 * (hh % 2)]
                nblk = 4 * j + 4
                LA = 2
                pend = {}

                def stage1(i):
                    pS, pSb = cx.ps()
                    P.op("pe", lambda e, pS=pS, i=i, hh=hh: e.matmul(pS[:, 0:N], kT[:, hh, i * 128:(i + 1) * 128], qT[:, hh, :],
                                                                   start=True, stop=True),
                         reads=[kT_b[i // 4], qT_b[hh]], writes=[pSb])
                    f_t, f_b = rot(tmpf, "tf")
                    P.op("dve", lambda e, f_t=f_t, pS=pS, q_t=q_t: e.tensor_tensor(out=f_t[:, :], in0=pS[:, 0:N], in1=q_t[:, :],
                                                                                 op=ALU.add), reads=[pSb, q_b], writes=[f_b])
                    if i >= 4 * j:
                        r = i - 4 * j
                        P.op("dve", lambda e, f_t=f_t, r=r: e.tensor_tensor(
                            out=f_t[:, :], in0=f_t[:, :], in1=mask[:, 384 - 128 * r:384 - 128 * r + NT], op=ALU.add),
                            reads=[f_b, mask_b], writes=[f_b])
                    p_t, p_b = rot(ptb, "pt")
                    P.op("act", lambda e, p_t=p_t, f_t=f_t, i=i, hh=hh: e.activation(
                        out=p_t[:, :], in_=f_t[:, :], func=AF.Exp, bias=negc[:, i, hh:hh + 1]),
                        reads=[f_b, negc_b[i // 4]], writes=[p_b])
                    pend[i] = (p_t, p_b)

                def stage2(i):
                    p_t, p_b = pend.pop(i)
                    P.op("pe", lambda e, accO=accO, p_t=p_t, i=i, hh=hh, nblk=nblk: e.matmul(
                        accO[:, 0:N], vtm[:, i, hh * FDH:(hh + 1) * FDH], p_t[:, :], start=(i == 0), stop=(i == nblk - 1)),
                        reads=[vtm_b[i // 4], p_b], writes=[accOb])
                    P.op("pe", lambda e, accD=accD, p_t=p_t, i=i, nblk=nblk: e.matmul(
                        accD[:, 0:N], ones[:, :], p_t[:, :], start=(i == 0), stop=(i == nblk - 1)),
                        reads=[ones_b, p_b], writes=[accDb])

                for i in range(nblk + LA):
                    if i < nblk:
                        stage1(i)
                    if i >= LA:
                        stage2(i - LA)
                rc_t, rc_b = rot(tmpf, "tf")
                P.op("dve", lambda e, rc_t=rc_t, accD=accD: e.reciprocal(out=rc_t[:, :], in_=accD[:, 0:N]),
                     reads=[accDb], writes=[rc_b])
                o_t, o_b = rot(ost, "ost")
                P.op("dve", lambda e, o_t=o_t, rc_t=rc_t, accO=accO: e.tensor_tensor(out=o_t[:, :], in0=accO[:, 0:N],
                                                                                     in1=rc_t[:, :], op=ALU.mult),
                     reads=[accOb, rc_b], writes=[o_b])
                P.dma("sp", out[hh * FDH:(hh + 1) * FDH, t0:t0 + N], o_t[:, :], reads=[o_b])
        if env is None:
            P.finish([b for _, b in ost])
            P.emit()
    return nc


def fox_consts():
    ident = np.eye(128, dtype=np.float32)
    p = np.arange(128)[:, None]
    xx = np.arange(896)[None, :]
    mask = np.where(xx - 384 >= p, 0.0, FOX_NEG).astype(np.float32)
    sel = np.zeros((FH, FH * 128), np.float32)
    for h in range(FH):
        sel[h, h * 128:(h + 1) * 128] = 1.0
    return ident, mask, sel


def run_fox(x, g1, w_qkvf, b_f):
    nc = _get_nc(("fox",), build_fox)
    ident, mask, sel = fox_consts()
    in_maps = []
    for c in range(NCORES):
        b, hg = c // 2, c % 2
        h0 = hg * FH
        cols = slice(h0 * FDH, (h0 + FH) * FDH)
        wall = np.concatenate([w_qkvf[:, 0:D][:, cols], w_qkvf[:, D:2 * D][:, cols], w_qkvf[:, 2 * D:3 * D][:, cols]], axis=1)
        vec = np.zeros((128, 20), np.float32)
        vec[:, 0:16] = pack_vec(g1)
        vec[:, 16] = RMS_EPS
        vec[:, 17] = 1.0
        vec[0:FH, 18] = np.asarray(b_f)[h0:h0 + FH]
        in_maps.append({"xT": _f32(x[b].T), "wall": _f32(wall), "wf": _f32(w_qkvf[:, 3 * D + h0:3 * D + h0 + FH]),
                        "vec": vec, "ident": ident, "mask": mask, "sel": sel})
    res = run_bass_kernel_spmd(nc, in_maps, core_ids=list(range(NCORES)))
    o = np.empty((B, T, D), np.float32)
    for c in range(NCORES):
        b, hg = c // 2, c % 2
        o[b, :, hg * FH * FDH:(hg + 1) * FH * FDH] = res.results[c]["out"].T
    return o


RH = 16
RN = 64
RF = RH * RN
RFC = RF // 128
CH = 64
C0 = float(np.exp(-0.5))
GN_EPS = 64e-5
RA_OUTS = ("At", "Kt", "Bt", "Rt", "Kh", "Bh", "Vb", "T1", "T2")


def build_rwkv_a(T=T, env=None):
    nc = env.nc if env else bass.Bass("TRN2", target_bir_lowering=False)
    dram = env.dram if env else (lambda n, s, k="ExternalInput": nc.dram_tensor(n, list(s), F32, kind=k).ap())
    xT = dram("xT", [D, T])
    wrkv = dram("wrkv", [D, 3 * RF])
    w1_d = dram("w1", [D, 96]); w2_d = dram("w2", [96, RF])
    a1_d = dram("a1", [D, 96]); a2_d = dram("a2", [96, RF])
    g1_d = dram("g1", [D, 256]); g2_d = dram("g2", [256, RF])
    o_mu = 18
    o_own = 114
    o_w0, o_a0, o_kk, o_ka, o_gng, o_gnb, o_rk = [o_own + 8 * i for i in range(7)]
    NV = o_own + 56
    vec_d = dram("vec", [128, NV])
    bones_d = dram("bones", [128, 128])
    rmask_d = dram("rmask", [128, NT])
    outs = {n: dram(n, [RF, T], "ExternalOutput") for n in RA_OUTS}
    pc_d = dram("PC", [RF, T // CH], "ExternalOutput")
    NTI = T // NT
    NCH = NT // CH

    with (env.scope() if env else contextlib.ExitStack()) as st:
        cx = env.cx if env else Ctx(nc, st)
        P = cx.P
        if env is None:
            cx.init_psum(8)
        ws = WStream(cx, nslots=2, cols=256)
        vec = cx.sb([128, NV], F32, "vec"); vec_b = Buf("vec")
        omm = cx.sb([128, 96], F32, "omm"); omm_b = Buf("omm")
        bonesf = cx.sb([128, 128], F32, "bonesf"); bonesf_b = Buf("bonesf")
        bones = cx.sb([128, 128], BF16, "bones"); bones_b = Buf("bones")
        ones = cx.sb([128, 128], BF16, "ones"); ones_b = Buf("ones")
        rmask = cx.sb([128, NT], F32, "rmask"); rmask_b = Buf("rmask")
        hf = [(cx.sb([128, NT + 1], F32, "hf%d" % c), Buf("hf%d" % c)) for c in range(DC)]
        mixb = [(cx.sb([128, NT], BF16, "mx%d" % c), Buf("mx%d" % c)) for c in range(DC)]
        G = {}
        for nm in ("r", "k", "v", "s", "a", "g"):
            G[nm] = [(cx.sb([128, NT], F32, "G%s%d" % (nm, f)), Buf("G%s%d" % (nm, f))) for f in range(RFC)]
        th = (cx.sb([96, NT], BF16, "th"), Buf("th"))
        ah = (cx.sb([96, NT], BF16, "ah"), Buf("ah"))
        gh = [(cx.sb([128, NT], BF16, "gh%d" % i), Buf("gh%d" % i)) for i in range(2)]
        tf = [(cx.sb([128, NT], F32, "tf%d" % i), Buf("tf%d" % i)) for i in range(6)]
        tb = [(cx.sb([128, NT], BF16, "tb%d" % i), Buf("tb%d" % i)) for i in range(2)]
        og = [(cx.sb([128, NT], F32, "og%d" % i), Buf("og%d" % i)) for i in range(4)]
        rstd = (cx.sb([128, NT], F32, "rstd"), Buf("rstd"))
        pcs = cx.sb([128, RFC, T // CH], F32, "pcs"); pcs_b = Buf("pcs")
        cnt = {}

        def rot(lst, key):
            cnt[key] = cnt.get(key, 0) + 1
            return lst[(cnt[key] - 1) % len(lst)]

        def vcol(o, n=1):
            return vec[:, o:o + n]

        P.dma("sp", vec[:, :], vec_d[:, :], writes=[vec_b])
        P.dma("sp", bonesf[:, :], bones_d[:, :], writes=[bonesf_b])
        P.dma("sp", rmask[:, :], rmask_d[:, :], writes=[rmask_b])
        P.op("dve", lambda e: e.memset(ones[:, :], 1.0), writes=[ones_b])
        P.op("dve", lambda e: e.tensor_copy(out=bones[:, :], in_=bonesf[:, :]), reads=[bonesf_b], writes=[bones_b])
        P.op("dve", lambda e: e.tensor_scalar(out=omm[:, :], in0=vec[:, o_mu:o_mu + 96], scalar1=-1.0, scalar2=1.0,
                                              op0=ALU.mult, op1=ALU.add), reads=[vec_b], writes=[omm_b])
        for c in range(DC):
            P.op("dve", lambda e, c=c: e.memset(hf[c][0][:, 0:1], 0.0), writes=[hf[c][1]])

        @_each(range(NTI))
        def _body_j(j):
            t0 = j * NT
            N = NT
            if j > 0:
                for c in range(DC):
                    P.op("act", lambda e, c=c: e.copy(out=hf[c][0][:, 0:1], in_=hf[c][0][:, N:N + 1]),
                         reads=[hf[c][1]], writes=[hf[c][1]])
            pt, pb = cx.ps()
            for c in range(DC):
                h_t, h_b = hf[c]
                s_t, s_b = rot(tb, "tb")
                P.dma("sp", h_t[:, 1:N + 1], xT[c * 128:(c + 1) * 128, t0:t0 + N], writes=[h_b])
                P.op("act", lambda e, s_t=s_t, h_t=h_t: e.activation(out=s_t[:, :], in_=h_t[:, 1:N + 1], func=AF.Square),
                     reads=[h_b], writes=[s_b])
                P.op("pe", lambda e, s_t=s_t, c=c, pt=pt: e.matmul(pt[:, 0:N], ones[:, :], s_t[:, :], start=(c == 0),
                                                                 stop=(c == DC - 1)), reads=[s_b, ones_b], writes=[pb])
            r_t, r_b = rstd
            P.op("act", lambda e, pt=pt: e.activation(out=r_t[:, :], in_=pt[:, 0:N], func=AF.Sqrt, scale=1.0 / D,
                                                      bias=vcol(16)), reads=[pb, vec_b], writes=[r_b])
            P.op("dve", lambda e: e.reciprocal(out=r_t[:, :], in_=r_t[:, :]), reads=[r_b], writes=[r_b])
            for c in range(DC):
                h_t, h_b = hf[c]
                P.op("dve", lambda e, h_t=h_t, c=c: e.scalar_tensor_tensor(
                    out=h_t[:, 1:N + 1], in0=h_t[:, 1:N + 1], scalar=vcol(c), in1=r_t[:, :], op0=ALU.mult, op1=ALU.mult),
                    reads=[h_b, r_b, vec_b], writes=[h_b])

            def mix(i):
                for c in range(DC):
                    h_t, h_b = hf[c]
                    m_t, m_b = mixb[c]
                    f_t, f_b = rot(tf, "tf")
                    P.op("dve", lambda e, f_t=f_t, h_t=h_t, c=c: e.tensor_scalar(
                        out=f_t[:, :], in0=h_t[:, 1:N + 1], scalar1=omm[:, i * 16 + c:i * 16 + c + 1], scalar2=None,
                        op0=ALU.mult), reads=[h_b, omm_b], writes=[f_b])
                    P.op("dve", lambda e, f_t=f_t, h_t=h_t, m_t=m_t, c=c: e.scalar_tensor_tensor(
                        out=m_t[:, :], in0=h_t[:, 0:N], scalar=vcol(o_mu + i * 16 + c), in1=f_t[:, :], op0=ALU.mult,
                        op1=ALU.add), reads=[h_b, f_b, vec_b], writes=[m_b])

            actf = lambda kc: (mixb[kc][0][:, :], mixb[kc][1])

            def epi_store(nm, func=None, bias_o=None):
                def epi(mc, ps, pb):
                    g_t, g_b = G[nm][mc]
                    if func is None:
                        eng = "act" if mc % 2 == 0 else "dve"
                        if eng == "act":
                            P.op("act", lambda e: e.copy(out=g_t[:, :], in_=ps), reads=[pb], writes=[g_b])
                        else:
                            P.op("dve", lambda e: e.tensor_copy(out=g_t[:, :], in_=ps), reads=[pb], writes=[g_b])
                    else:
                        P.op("act", lambda e: e.activation(out=g_t[:, :], in_=ps, func=func, bias=vcol(bias_o + mc)),
                             reads=[pb, vec_b], writes=[g_b])
                return epi

            mix(0)
            gemm(cx, ws, wrkv[:, 0:RF], D, RF, actf, N, epi_store("r"), mgrp=2)
            mix(2)
            gemm(cx, ws, wrkv[:, RF:2 * RF], D, RF, actf, N, epi_store("k"), mgrp=2)
            mix(3)
            gemm(cx, ws, wrkv[:, 2 * RF:3 * RF], D, RF, actf, N, epi_store("v"), mgrp=2)
            mix(1)

            def epi_th(mc, ps, pb):
                P.op("act", lambda e: e.activation(out=th[0][:, :], in_=ps, func=AF.Tanh), reads=[pb], writes=[th[1]])
            gemm(cx, ws, w1_d, D, 96, actf, N, epi_th, mgrp=1, mp=96)
            gemm(cx, ws, w2_d, 96, RF, lambda kc: (th[0][:, :], th[1]), N, epi_store("s", AF.Sigmoid, o_w0), kp=96, mgrp=2)
            mix(4)

            def epi_ah(mc, ps, pb):
                P.op("act", lambda e: e.copy(out=ah[0][:, :], in_=ps), reads=[pb], writes=[ah[1]])
            gemm(cx, ws, a1_d, D, 96, actf, N, epi_ah, mgrp=1, mp=96)
            gemm(cx, ws, a2_d, 96, RF, lambda kc: (ah[0][:, :], ah[1]), N, epi_store("a", AF.Sigmoid, o_a0), kp=96, mgrp=2)
            mix(5)

            def epi_gh(mc, ps, pb):
                P.op("act", lambda e: e.activation(out=gh[mc][0][:, :], in_=ps, func=AF.Sigmoid), reads=[pb],
                     writes=[gh[mc][1]])
            gemm(cx, ws, g1_d, D, 256, actf, N, epi_gh, mgrp=2)
            gemm(cx, ws, g2_d, 256, RF, lambda kc: (gh[kc][0][:, :], gh[kc][1]), N, epi_store("g"), mgrp=2)

            @_each(range(RFC))
            def _body_f(f):
                r_t_, r_b_ = G["r"][f]; k_t, k_b = G["k"][f]; v_t, v_b = G["v"][f]
                s_t, s_b = G["s"][f]; a_t, a_b = G["a"][f]; g_t, g_b = G["g"][f]
                rows = slice(f * 128, (f + 1) * 128)

                def store(nm, o_t, o_b):
                    P.dma("sp", outs[nm][rows, t0:t0 + N], o_t[:, :], reads=[o_b])

                kk_t, kk_b = rot(tf, "tf")
                P.op("dve", lambda e: e.tensor_scalar(out=kk_t[:, :], in0=k_t[:, :], scalar1=vcol(o_kk + f), scalar2=None,
                                                      op0=ALU.mult), reads=[k_b, vec_b], writes=[kk_b])
                q_t, q_b = rot(tb, "tb")
                P.op("act", lambda e: e.activation(out=q_t[:, :], in_=kk_t[:, :], func=AF.Square), reads=[kk_b], writes=[q_b])
                p1, p1b = cx.ps()
                P.op("pe", lambda e: e.matmul(p1[:, 0:N], bones[:, :], q_t[:, :], start=True, stop=True),
                     reads=[bones_b, q_b], writes=[p1b])
                n_t, n_b = rot(tf, "tf")
                P.op("act", lambda e: e.activation(out=n_t[:, :], in_=p1[:, 0:N], func=AF.Sqrt), reads=[p1b], writes=[n_b])
                P.op("dve", lambda e: e.tensor_scalar(out=n_t[:, :], in0=n_t[:, :], scalar1=1e-12, scalar2=None, op0=ALU.max),
                     reads=[n_b], writes=[n_b])
                P.op("dve", lambda e: e.reciprocal(out=n_t[:, :], in_=n_t[:, :]), reads=[n_b], writes=[n_b])
                P.op("dve", lambda e: e.tensor_tensor(out=kk_t[:, :], in0=kk_t[:, :], in1=n_t[:, :], op=ALU.mult),
                     reads=[kk_b, n_b], writes=[kk_b])
                u_t, u_b = rot(tf, "tf")
                P.op("dve", lambda e: e.tensor_scalar(out=u_t[:, :], in0=a_t[:, :], scalar1=-1.0, scalar2=vcol(o_ka + f),
                                                      op0=ALU.add, op1=ALU.mult), reads=[a_b, vec_b], writes=[u_b])
                P.op("dve", lambda e: e.scalar_tensor_tensor(out=k_t[:, :], in0=u_t[:, :], scalar=1.0, in1=k_t[:, :],
                                                             op0=ALU.add, op1=ALU.mult), reads=[u_b, k_b], writes=[k_b])
                P.op("dve", lambda e: e.tensor_tensor(out=a_t[:, :], in0=kk_t[:, :], in1=a_t[:, :], op=ALU.mult),
                     reads=[kk_b, a_b], writes=[a_b])
                P.op("dve", lambda e: e.tensor_tensor(out=u_t[:, :], in0=r_t_[:, :], in1=k_t[:, :], op=ALU.mult),
                     reads=[r_b_, k_b], writes=[u_b])
                q2_t, q2_b = rot(tb, "tb")
                P.op("dve", lambda e: e.tensor_scalar(out=q2_t[:, :], in0=u_t[:, :], scalar1=vcol(o_rk + f), scalar2=None,
                                                      op0=ALU.mult), reads=[u_b, vec_b], writes=[q2_b])
                p2, p2b = cx.ps()
                P.op("pe", lambda e: e.matmul(p2[:, 0:N], bones[:, :], q2_t[:, :], start=True, stop=True),
                     reads=[bones_b, q2_b], writes=[p2b])
                o1_t, o1_b = rot(og, "og")
                P.op("dve", lambda e: e.tensor_tensor(out=o1_t[:, :], in0=p2[:, 0:N], in1=v_t[:, :], op=ALU.mult),
                     reads=[p2b, v_b], writes=[o1_b])
                P.op("dve", lambda e: e.scalar_tensor_tensor(out=o1_t[:, :], in0=o1_t[:, :], scalar=vcol(o_gnb + f),
                                                             in1=g_t[:, :], op0=ALU.add, op1=ALU.mult),
                     reads=[o1_b, g_b, vec_b], writes=[o1_b])
                store("T1", o1_t, o1_b)
                o2_t, o2_b = rot(og, "og")
                P.op("dve", lambda e: e.tensor_scalar(out=o2_t[:, :], in0=g_t[:, :], scalar1=vcol(o_gng + f), scalar2=None,
                                                      op0=ALU.mult), reads=[g_b, vec_b], writes=[o2_b])
                store("T2", o2_t, o2_b)
                store("Vb", v_t, v_b)
                cs_t, cs_b = rot(tf, "tf")
                P.op("dve", lambda e: e.tensor_tensor_scan(out=cs_t[:, :], data0=rmask[:, :], data1=s_t[:, :], initial=0.0,
                                                           op0=ALU.mult, op1=ALU.add), reads=[rmask_b, s_b], writes=[cs_b])
                e_t, e_b = rot(tf, "tf")
                P.op("act", lambda e: e.activation(out=e_t[:, :], in_=cs_t[:, :], func=AF.Exp, scale=-C0), reads=[cs_b], writes=[e_b])
                o_t, o_b = rot(og, "og")
                P.op("dve", lambda e, o_t=o_t: e.tensor_tensor(out=o_t[:, :], in0=r_t_[:, :], in1=e_t[:, :], op=ALU.mult),
                     reads=[r_b_, e_b], writes=[o_b])
                store("Rt", o_t, o_b)
                e2_t, e2_b = rot(tf, "tf")
                P.op("act", lambda e: e.activation(out=e2_t[:, :], in_=cs_t[:, :], func=AF.Exp, scale=C0), reads=[cs_b], writes=[e2_b])
                o_t, o_b = rot(og, "og")
                P.op("dve", lambda e, o_t=o_t: e.tensor_tensor(out=o_t[:, :], in0=k_t[:, :], in1=e2_t[:, :], op=ALU.mult),
                     reads=[k_b, e2_b], writes=[o_b])
                store("Kt", o_t, o_b)
                o_t, o_b = rot(og, "og")
                P.op("dve", lambda e, o_t=o_t: e.tensor_tensor(out=o_t[:, :], in0=a_t[:, :], in1=e2_t[:, :], op=ALU.mult),
                     reads=[a_b, e2_b], writes=[o_b])
                store("Bt", o_t, o_b)
                P.op("dve", lambda e: e.tensor_tensor(out=e_t[:, :], in0=cs_t[:, :], in1=s_t[:, :], op=ALU.subtract),
                     reads=[cs_b, s_b], writes=[e_b])
                P.op("act", lambda e: e.activation(out=e_t[:, :], in_=e_t[:, :], func=AF.Exp, scale=-C0), reads=[e_b], writes=[e_b])
                o_t, o_b = rot(og, "og")
                P.op("dve", lambda e, o_t=o_t: e.scalar_tensor_tensor(out=o_t[:, :], in0=kk_t[:, :], scalar=-1.0, in1=e_t[:, :],
                                                                      op0=ALU.mult, op1=ALU.mult),
                     reads=[kk_b, e_b], writes=[o_b])
                store("At", o_t, o_b)
                cs3 = cs_t[:, :].rearrange("p (c t) -> p c t", t=CH)
                P.op("dve", lambda e: e.tensor_tensor(out=e2_t[:, :].rearrange("p (c t) -> p c t", t=CH),
                                                      in0=cs3[:, :, CH - 1:CH].to_broadcast([128, NCH, CH]), in1=cs3,
                                                      op=ALU.subtract), reads=[cs_b], writes=[e2_b])
                P.op("act", lambda e: e.activation(out=e2_t[:, :], in_=e2_t[:, :], func=AF.Exp, scale=-C0), reads=[e2_b], writes=[e2_b])
                o_t, o_b = rot(og, "og")
                P.op("dve", lambda e, o_t=o_t: e.tensor_tensor(out=o_t[:, :], in0=k_t[:, :], in1=e2_t[:, :], op=ALU.mult),
                     reads=[k_b, e2_b], writes=[o_b])
                store("Kh", o_t, o_b)
                o_t, o_b = rot(og, "og")
                P.op("dve", lambda e, o_t=o_t: e.tensor_tensor(out=o_t[:, :], in0=a_t[:, :], in1=e2_t[:, :], op=ALU.mult),
                     reads=[a_b, e2_b], writes=[o_b])
                store("Bh", o_t, o_b)
                P.op("act", lambda e: e.activation(out=pcs[:, f, j * NCH:(j + 1) * NCH], in_=cs3[:, :, CH - 1], func=AF.Exp,
                                                   scale=-C0), reads=[cs_b], writes=[pcs_b])
        for f in range(RFC):
            P.dma("sp", pc_d[f * 128:(f + 1) * 128, :], pcs[:, f, :], reads=[pcs_b])
        if env is None:
            P.finish([b for _, b in og] + [pcs_b] + [b for _, b in G["v"]])
            P.emit()
    return nc


RB_TN = 128
_RWB_STOP = None


def build_rwkv_b(T=T, env=None):
    nc = env.nc if env else bass.Bass("TRN2", target_bir_lowering=False)
    dram = env.dram if env else (lambda n, s, k="ExternalInput": nc.dram_tensor(n, list(s), F32, kind=k).ap())
    FMN = ("At", "Kt", "Bt", "Rt", "T1", "T2", "Vb", "Kh", "Bh")
    if env:
        fm3 = {n: env.io[n] for n in FMN}
        pc3 = env.io["PC"]
        out3 = env.io["out"]
    else:
        fm3 = {n: dram(n, [RN, RH * T]).rearrange("j (h t) -> j h t", h=RH) for n in FMN}
        pc3 = dram("PC", [RN, RH * (T // CH)]).rearrange("j (h c) -> j h c", h=RH)
    mask_d = dram("mask5", [RN, 320])
    ident_d = dram("ident", [RN, RN])
    eps_d = dram("eps", [RN, 1])
    if not env:
        out3 = dram("out", [RN, RH * T], "ExternalOutput").rearrange("j (h t) -> j h t", h=RH)
    NTI = T // RB_TN
    NCC = RB_TN // CH
    HW = RH * RN

    def fmv(ap):
        return ap.rearrange("j (h t) -> j h t", h=RH)

    with (env.scope() if env else contextlib.ExitStack()) as st:
        cx = env.cx if env else Ctx(nc, st)
        P = cx.P
        pst = []
        for i in range(4):
            t = env.psd[i] if env else st.enter_context(nc.psum_tensor("psd%d" % i, [128, 1024], F32))
            pst.append((t, Buf("psd%d" % i)))
        pcount = [0]

        def pstile():
            r = pst[pcount[0] % 4]
            pcount[0] += 1
            return r

        mask5 = cx.sb([RN, 320], F32, "mask5"); mask_b = Buf("mask5")
        ident = cx.sb([RN, RN], F32, "ident"); ident_b = Buf("ident")
        epsc = cx.sb([RN, 1], F32, "eps"); eps_b = Buf("eps")
        PC = cx.sb([RN, RH, T // CH], F32, "PC"); PC_b = Buf("PC")
        fm = {}
        for n in ("At", "Kt", "Bt", "Rt"):
            fm[n] = [(cx.sb([RN, RH, RB_TN], BF16, "%s%d" % (n, i)), Buf("%s%d" % (n, i))) for i in range(2)]
        for n in ("T1", "T2"):
            fm[n] = [(cx.sb([RN, RH, RB_TN], F32, "%s%d" % (n, i)), Buf("%s%d" % (n, i))) for i in range(1)]
        for n in ("Vb", "Kh", "Bh"):
            fm[n] = [(cx.sb([RN, RH, RB_TN], BF16, "%s%d" % (n, i)), Buf("%s%d" % (n, i))) for i in range(2)]
        tmt = {n: (cx.sb([CH, HW], BF16, "tm%s" % n), Buf("tm%s" % n)) for n in ("Vb", "Kh", "Bh")}
        identb = cx.sb([RN, RN], BF16, "identb"); identb_b = Buf("identb")
        Mm = cx.sb([RN, RH, 320], BF16, "Mm"); Mm_b = Buf("Mm")
        XX = [(cx.sb([RN, RH, 2, RN], BF16, "XX%d" % i), Buf("XX%d" % i)) for i in range(2)]
        Rf = cx.sb([RN, RH, RN], F32, "Rf"); Rf_b = Buf("Rf")
        Rb = cx.sb([RN, RH, RN], BF16, "Rb"); Rb_b = Buf("Rb")
        Wt = cx.sb([RN, HW], BF16, "Wt"); Wt_b = Buf("Wt")
        Ut = cx.sb([RN, HW], BF16, "Ut"); Ut_b = Buf("Ut")
        Sf = cx.sb([RN, RH, RN], F32, "Sf"); Sf_b = Buf("Sf")
        Sb = cx.sb([RN, RH, RN], BF16, "Sb"); Sb_b = Buf("Sb")
        ysq = cx.sb([RN, HW], F32, "ysq"); ysq_b = Buf("ysq")
        yn = cx.sb([RN, HW], F32, "yn"); yn_b = Buf("yn")
        st1 = cx.sb([RN, RH], F32, "st1"); st1_b = Buf("st1")
        st2 = cx.sb([RN, RH], F32, "st2"); st2_b = Buf("st2")
        st3 = cx.sb([RN, RH], F32, "st3"); st3_b = Buf("st3")
        ostg = [(cx.sb([RN, RH, RB_TN], F32, "ostg%d" % i), Buf("ostg%d" % i)) for i in range(2)]

        P.dma("sp", mask5[:, :], mask_d[:, :], writes=[mask_b])
        P.dma("sp", ident[:, :], ident_d[:, :], writes=[ident_b])
        P.dma("sp", epsc[:, :], eps_d[:, :], writes=[eps_b])
        P.dma("sp", PC[:, :, :], pc3, writes=[PC_b])
        P.op("dve", lambda e: e.tensor_copy(out=identb[:, :], in_=ident[:, :]), reads=[ident_b], writes=[identb_b])
        P.op("dve", lambda e: e.memset(Sf[:, :, :], 0.0), writes=[Sf_b])
        P.op("dve", lambda e: e.memset(Sb[:, :, :], 0.0), writes=[Sb_b])
        ev = [0]

        def evac_copy(out_ap, in_ap, reads, writes):
            ev[0] += 1
            if ev[0] % 2:
                P.op("act", lambda e: e.copy(out=out_ap, in_=in_ap), reads=reads, writes=writes)
            else:
                P.op("dve", lambda e: e.tensor_copy(out=out_ap, in_=in_ap), reads=reads, writes=writes)

        @_each(range(NTI))
        def _body_ti(ti):
            t0 = ti * RB_TN
            cur = {}
            for n in ("At", "Kt", "Bt", "Rt", "Vb", "Kh", "Bh"):
                t_, b_ = fm[n][ti % 2]
                P.dma("pool", t_[:, :, :], fm3[n][:, :, t0:t0 + RB_TN], writes=[b_])
                cur[n] = (t_, b_)
            for n in ("T1", "T2"):
                t_, b_ = fm[n][0]
                P.dma("sp", t_[:, :, :], fm3[n][:, :, t0:t0 + RB_TN], writes=[b_])
                cur[n] = (t_, b_)
            og_t, og_b = ostg[ti % 2]
            At, At_b = cur["At"]; Kt, Kt_b = cur["Kt"]; Bt, Bt_b = cur["Bt"]; Rt, Rt_b = cur["Rt"]
            Vf, Vf_b = cur["Vb"]; Khf, Khf_b = cur["Kh"]; Bhf, Bhf_b = cur["Bh"]
            Vt, Vt_b = tmt["Vb"]; Kh, Kh_b = tmt["Kh"]; Bh, Bh_b = tmt["Bh"]
            T1, T1_b = cur["T1"]; T2, T2_b = cur["T2"]
            @_each(range(NCC))
            def _body_cc(cc):
                gc = ti * NCC + cc
                tc = slice(cc * CH, (cc + 1) * CH)
                hs_ = lambda h: slice(h * RN, (h + 1) * RN)
                for (src, src_b, dst, dst_b) in ((Vf, Vf_b, Vt, Vt_b), (Khf, Khf_b, Kh, Kh_b), (Bhf, Bhf_b, Bh, Bh_b)):
                    ptt, ptb_ = pstile()
                    pv16 = ptt[0:RN, 0:512].bitcast(BF16)
                    for h in range(RH):
                        P.op("pe", lambda e, pv16=pv16, h=h, src=src: e.transpose(out=pv16[:, hs_(h)], in_=src[:, h, tc],
                                                                                 identity=identb[:, :]),
                             reads=[src_b, identb_b], writes=[ptb_])
                    evac_copy(dst[:, :], pv16[:, 0:HW], [ptb_], [dst_b])
                for h0 in range(0, RH, 3):
                    nh = min(3, RH - h0)
                    pt, pb = pstile()
                    for hh in range(nh):
                        h = h0 + hh
                        o = hh * 320
                        for bi, (L, Lb, R_, R_b) in enumerate(((Kt, Kt_b, At, At_b), (Kt, Kt_b, Rt, Rt_b),
                                                              (Bt, Bt_b, At, At_b), (Bt, Bt_b, Rt, Rt_b),
                                                              (At, At_b, Bt, Bt_b))):
                            P.op("pe", lambda e, pt=pt, o=o, bi=bi, L=L, R_=R_, h=h: e.matmul(
                                pt[0:RN, o + bi * 64:o + (bi + 1) * 64], L[:, h, tc], R_[:, h, tc], start=True, stop=True),
                                reads=[Lb, R_b], writes=[pb])
                    P.op("dve", lambda e, pt=pt, h0=h0, nh=nh: e.tensor_tensor(
                        out=Mm[:, h0:h0 + nh, :], in0=pt[0:RN, 0:nh * 320].rearrange("p (h x) -> p h x", h=nh),
                        in1=mask5[:, :].unsqueeze(1).to_broadcast([RN, nh, 320]), op=ALU.mult),
                        reads=[pb, mask_b], writes=[Mm_b])
                if _RWB_STOP == "M":
                    return
                P.op("dve", lambda e: e.tensor_tensor(out=Rf[:, :, :], in0=Mm[:, :, 128:192],
                                                      in1=ident[:, :].unsqueeze(1).to_broadcast([RN, RH, RN]), op=ALU.add),
                     reads=[Mm_b, ident_b], writes=[Rf_b])
                P.op("act", lambda e: e.copy(out=Rb[:, :, :], in_=Rf[:, :, :]), reads=[Rf_b], writes=[Rb_b])
                for it in range(5):
                    if it == 0:
                        Xf = lambda h: Mm[:, h, 128:192]
                        XTf = lambda h: Mm[:, h, 256:320]
                        src_b = Mm_b
                    else:
                        xs_t, xs_b = XX[(it - 1) % 2]
                        Xf = lambda h, xs_t=xs_t: xs_t[:, h, 0, :]
                        XTf = lambda h, xs_t=xs_t: xs_t[:, h, 1, :]
                        src_b = xs_b
                    xd_t, xd_b = XX[it % 2]
                    for h0 in range(0, RH, 8):
                        pt, pb = pstile()
                        for hh in range(8):
                            h = h0 + hh
                            if it < 4:
                                P.op("pe", lambda e, pt=pt, hh=hh, h=h, Xf=Xf, XTf=XTf: e.matmul(
                                    pt[0:RN, hh * 128:hh * 128 + 64], XTf(h), Xf(h), start=True, stop=True),
                                    reads=[src_b], writes=[pb])
                            P.op("pe", lambda e, pt=pt, hh=hh, h=h, Xf=Xf, XTf=XTf: e.matmul(
                                pt[0:RN, hh * 128 + 64:hh * 128 + 128], Xf(h), XTf(h), start=True, stop=True),
                                reads=[src_b], writes=[pb])
                        if it < 4:
                            evac_copy(xd_t[:, h0:h0 + 8, :, :], pt[0:RN, 0:1024].rearrange("p (h a x) -> p h a x", h=8, a=2),
                                      [pb], [xd_b])
                        else:
                            evac_copy(xd_t[:, h0:h0 + 8, 1, :],
                                      pt[0:RN, 0:1024].rearrange("p (h a x) -> p h a x", h=8, a=2)[:, :, 1, :], [pb], [xd_b])
                    pt, pb = pstile()
                    for h in range(RH):
                        P.op("pe", lambda e, pt=pt, h=h, xd_t=xd_t: e.matmul(pt[0:RN, hs_(h)], xd_t[:, h, 1, :], Rb[:, h, :],
                                                                           start=True, stop=True),
                             reads=[xd_b, Rb_b], writes=[pb])
                    P.op("dve", lambda e, pt=pt: e.tensor_tensor(out=Rf[:, :, :].rearrange("p h x -> p (h x)"),
                                                                 in0=Rf[:, :, :].rearrange("p h x -> p (h x)"),
                                                                 in1=pt[0:RN, 0:HW], op=ALU.add),
                         reads=[pb, Rf_b], writes=[Rf_b])
                    P.op("act", lambda e: e.copy(out=Rb[:, :, :], in_=Rf[:, :, :]), reads=[Rf_b], writes=[Rb_b])
                if _RWB_STOP == "inv":
                    return
                pt, pb = pstile()
                for h in range(RH):
                    P.op("pe", lambda e, pt=pt, h=h: e.matmul(pt[0:RN, hs_(h)], At[:, h, tc], Sb[:, h, :], start=True, stop=False),
                         reads=[At_b, Sb_b], writes=[pb])
                    P.op("pe", lambda e, pt=pt, h=h: e.matmul(pt[0:RN, hs_(h)], Mm[:, h, 0:64], Vt[:, hs_(h)], start=False,
                                                            stop=True), reads=[Mm_b, Vt_b], writes=[pb])
                evac_copy(Wt[:, :], pt[0:RN, 0:HW], [pb], [Wt_b])
                if _RWB_STOP == "W":
                    return
                pt, pb = pstile()
                for h in range(RH):
                    P.op("pe", lambda e, pt=pt, h=h: e.matmul(pt[0:RN, hs_(h)], Rb[:, h, :], Wt[:, hs_(h)], start=True, stop=True),
                         reads=[Rb_b, Wt_b], writes=[pb])
                evac_copy(Ut[:, :], pt[0:RN, 0:HW], [pb], [Ut_b])
                if _RWB_STOP == "U":
                    return
                py, pyb = pstile()
                for h in range(RH):
                    P.op("pe", lambda e, py=py, h=h: e.matmul(py[0:RN, hs_(h)], Rt[:, h, tc], Sb[:, h, :], start=True, stop=False),
                         reads=[Rt_b, Sb_b], writes=[pyb])
                    P.op("pe", lambda e, py=py, h=h: e.matmul(py[0:RN, hs_(h)], Mm[:, h, 192:256], Ut[:, hs_(h)], start=False,
                                                            stop=False), reads=[Mm_b, Ut_b], writes=[pyb])
                    P.op("pe", lambda e, py=py, h=h: e.matmul(py[0:RN, hs_(h)], Mm[:, h, 64:128], Vt[:, hs_(h)], start=False,
                                                            stop=True), reads=[Mm_b, Vt_b], writes=[pyb])
                if _RWB_STOP == "Y":
                    return
                pS, pSb = pstile()
                for h in range(RH):
                    P.op("pe", lambda e, pS=pS, h=h: e.matmul(pS[0:RN, hs_(h)], Bh[:, hs_(h)], Ut[:, hs_(h)], start=True,
                                                            stop=False), reads=[Bh_b, Ut_b], writes=[pSb])
                    P.op("pe", lambda e, pS=pS, h=h: e.matmul(pS[0:RN, hs_(h)], Kh[:, hs_(h)], Vt[:, hs_(h)], start=False,
                                                            stop=True), reads=[Kh_b, Vt_b], writes=[pSb])
                if _RWB_STOP == "S":
                    return
                y3 = py[0:RN, 0:HW].rearrange("p (h x) -> p h x", h=RH)
                P.op("act", lambda e, py=py: e.activation(out=ysq[:, :], in_=py[0:RN, 0:HW], func=AF.Square), reads=[pyb],
                     writes=[ysq_b])
                if _RWB_STOP == "g1":
                    return
                P.op("act", lambda e, py=py: e.copy(out=yn[:, :], in_=py[0:RN, 0:HW]), reads=[pyb], writes=[yn_b])
                P.op("dve", lambda e: e.tensor_reduce(out=st1[:, :], in_=yn[:, :].rearrange("p (h x) -> p h x", h=RH), axis=AX.X,
                                                      op=ALU.add), reads=[yn_b], writes=[st1_b])
                if _RWB_STOP == "g2":
                    return
                P.op("dve", lambda e: e.tensor_reduce(out=st2[:, :], in_=ysq[:, :].rearrange("p (h x) -> p h x", h=RH), axis=AX.X,
                                                      op=ALU.add), reads=[ysq_b], writes=[st2_b])
                if _RWB_STOP == "g3":
                    return
                P.op("dve", lambda e: e.tensor_scalar(out=st1[:, :], in0=st1[:, :], scalar1=1.0 / RN, scalar2=None, op0=ALU.mult),
                     reads=[st1_b], writes=[st1_b])
                P.op("dve", lambda e: e.tensor_tensor(out=st3[:, :], in0=st1[:, :], in1=st1[:, :], op=ALU.mult),
                     reads=[st1_b], writes=[st3_b])
                P.op("dve", lambda e: e.scalar_tensor_tensor(out=st2[:, :], in0=st2[:, :], scalar=1.0 / RN, in1=st3[:, :],
                                                             op0=ALU.mult, op1=ALU.subtract), reads=[st2_b, st3_b], writes=[st2_b])
                P.op("act", lambda e: e.activation(out=st2[:, :], in_=st2[:, :], func=AF.Sqrt, bias=epsc[:, :]),
                     reads=[st2_b, eps_b], writes=[st2_b])
                P.op("dve", lambda e: e.reciprocal(out=st2[:, :], in_=st2[:, :]), reads=[st2_b], writes=[st2_b])
                if _RWB_STOP == "g4":
                    return
                yn3 = yn[:, :].rearrange("p (h x) -> p h x", h=RH)
                P.op("dve", lambda e: e.tensor_tensor(out=yn3, in0=yn3, in1=st1[:, :].unsqueeze(2).to_broadcast([RN, RH, RN]),
                                                      op=ALU.subtract), reads=[yn_b, st1_b], writes=[yn_b])
                P.op("dve", lambda e: e.tensor_tensor(out=yn3, in0=yn3, in1=st2[:, :].unsqueeze(2).to_broadcast([RN, RH, RN]),
                                                      op=ALU.mult), reads=[yn_b, st2_b], writes=[yn_b])
                if _RWB_STOP == "gn":
                    return
                P.op("dve", lambda e, gc=gc: e.tensor_tensor(out=Sf[:, :, :], in0=Sf[:, :, :],
                                                             in1=PC[:, :, gc:gc + 1].to_broadcast([RN, RH, RN]), op=ALU.mult),
                     reads=[Sf_b, PC_b], writes=[Sf_b])
                P.op("dve", lambda e, pS=pS: e.tensor_tensor(out=Sf[:, :, :].rearrange("p h x -> p (h x)"),
                                                             in0=Sf[:, :, :].rearrange("p h x -> p (h x)"), in1=pS[0:RN, 0:HW],
                                                             op=ALU.add), reads=[pSb, Sf_b], writes=[Sf_b])
                P.op("act", lambda e: e.copy(out=Sb[:, :, :], in_=Sf[:, :, :]), reads=[Sf_b], writes=[Sb_b])
                if _RWB_STOP == "st":
                    return
                po, pob = pstile()
                for h in range(RH):
                    P.op("pe", lambda e, po=po, h=h: e.transpose(out=po[0:RN, hs_(h)], in_=yn[:, hs_(h)], identity=ident[:, :]),
                         reads=[yn_b, ident_b], writes=[pob])
                po3 = po[0:RN, 0:HW].rearrange("p (h x) -> p h x", h=RH)
                P.op("dve", lambda e, po3=po3, og_t=og_t: e.tensor_tensor(out=og_t[:, :, tc], in0=po3, in1=T2[:, :, tc], op=ALU.mult),
                     reads=[pob, T2_b], writes=[og_b])
                P.op("dve", lambda e, og_t=og_t: e.tensor_tensor(out=og_t[:, :, tc], in0=og_t[:, :, tc], in1=T1[:, :, tc], op=ALU.add),
                     reads=[og_b, T1_b], writes=[og_b])
            P.dma("sp", out3[:, :, t0:t0 + RB_TN], og_t[:, :, :], reads=[og_b])
        if env is None:
            P.finish([b for _, b in ostg])
            P.emit()
    return nc


def rwkv_consts():
    bones = np.zeros((128, 128), np.float32)
    bones[0:64, 0:64] = 1.0
    bones[64:128, 64:128] = 1.0
    rmask = np.ones((128, NT), np.float32)
    rmask[:, ::CH] = 0.0
    s = np.arange(64)[:, None]
    t = np.arange(64)[None, :]
    su = (s < t).astype(np.float32)
    iu = (s <= t).astype(np.float32)
    sl = (t < s).astype(np.float32)
    mask5 = np.concatenate([su, iu, su, iu, sl], axis=1).astype(np.float32)
    ident = np.eye(64, dtype=np.float32)
    return bones, rmask, mask5, ident


def run_rwkv(x, g1, p):
    nca = _get_nc(("rwa",), build_rwkv_a)
    bones, rmask, mask5, ident = rwkv_consts()
    in_maps = []
    for c in range(NCORES):
        b, hg = c // 2, c % 2
        own = slice(hg * RF, (hg + 1) * RF)
        vec = np.zeros((128, 114 + 56), np.float32)
        vec[:, 0:16] = pack_vec(g1)
        vec[:, 16] = RMS_EPS
        for i in range(6):
            vec[:, 18 + 16 * i:18 + 16 * (i + 1)] = pack_vec(p["mu"][i])
        ownv = [p["w0"], p["a0"], p["k_k"], p["k_a"], p["gn_g"], p["gn_b"], np.asarray(p["r_k"]).reshape(-1)]
        for i, v in enumerate(ownv):
            vec[:, 114 + 8 * i:114 + 8 * (i + 1)] = pack_vec(np.asarray(v)[own])
        in_maps.append({
            "xT": _f32(x[b].T),
            "wrkv": _f32(np.concatenate([p["w_rkv"][0][:, own], p["w_rkv"][1][:, own], p["w_rkv"][2][:, own]], axis=1)),
            "w1": _f32(p["w1"]), "w2": _f32(p["w2"][:, own]), "a1": _f32(p["a1"]), "a2": _f32(p["a2"][:, own]),
            "g1": _f32(p["g1"]), "g2": _f32(p["g2"][:, own]), "vec": vec, "bones": bones, "rmask": rmask})
    resa = run_bass_kernel_spmd(nca, in_maps, core_ids=list(range(NCORES))).results
    ncb = _get_nc(("rwb",), build_rwkv_b)
    in_maps = []
    for c in range(NCORES):
        r = resa[c]

        def fm(a):
            return _f32(a.reshape(RH, RN, -1).transpose(1, 0, 2).reshape(RN, -1))
        m = {n: fm(r[n]) for n in ("At", "Kt", "Bt", "Rt", "T1", "T2", "Vb", "Kh", "Bh")}
        m["PC"] = fm(r["PC"])
        m["mask5"] = mask5
        m["ident"] = ident
        m["eps"] = np.full((RN, 1), GN_EPS, np.float32)
        in_maps.append(m)
    resb = run_bass_kernel_spmd(ncb, in_maps, core_ids=list(range(NCORES))).results
    y = np.empty((B, T, D), np.float32)
    for c in range(NCORES):
        b, hg = c // 2, c % 2
        o = resb[c]["out"].reshape(RN, RH, T)
        y[b, :, hg * RF:(hg + 1) * RF] = o.transpose(2, 1, 0).reshape(T, RF)
    return y


def kernel_unfused(x, norm1_g, norm2_g, mlp_w1, mlp_w2,
           cc_w_in, cc_b_in, cc_dw, cc_dw_b, cc_ln_g, cc_ln_b, cc_w_out, cc_b_out,
           rw_mu, rw_w_rkv, rw_w0, rw_w1, rw_w2, rw_a0, rw_a1, rw_a2, rw_g1, rw_g2,
           rw_k_k, rw_k_a, rw_r_k, rw_gn_g, rw_gn_b, rw_w_o,
           sc_w_in, sc_conv_w, sc_w_out,
           fx_w_qkvf, fx_b_f, fx_w_o, final_g):
    A = lambda a: np.asarray(a, dtype=np.float32)
    x = A(x)
    n1, n2, gf = A(norm1_g), A(norm2_g), A(final_g)
    w1, w2 = A(mlp_w1), A(mlp_w2)
    mp = dict(w_in=_f32(cc_w_in[0]), b_in=A(cc_b_in[0]), dw=A(cc_dw[0]), dw_b=A(cc_dw_b[0]), ln_g=A(cc_ln_g[0]),
              ln_b=A(cc_ln_b[0]), w_out=_f32(cc_w_out[0]), b_out=A(cc_b_out[0]))
    x = run_tok("cc", False, x, n1[0], n2[0], gf, _f32(w1[0]), _f32(w2[0]), mp)
    p = dict(mu=A(rw_mu[0]), w_rkv=A(rw_w_rkv[0]), w0=A(rw_w0[0]), w1=A(rw_w1[0]), w2=A(rw_w2[0]), a0=A(rw_a0[0]),
             a1=A(rw_a1[0]), a2=A(rw_a2[0]), g1=A(rw_g1[0]), g2=A(rw_g2[0]), k_k=A(rw_k_k[0]), k_a=A(rw_k_a[0]),
             r_k=A(rw_r_k[0]), gn_g=A(rw_gn_g[0]), gn_b=A(rw_gn_b[0]))
    m = run_rwkv(x, n1[1], p)
    x = run_tok("post", False, x, n1[1], n2[1], gf, _f32(w1[1]), _f32(w2[1]), dict(w_out=_f32(rw_w_o[0])), m_in=m)
    mp = dict(w_in=_f32(sc_w_in[0]), conv_w=A(sc_conv_w[0]), w_out=_f32(sc_w_out[0]))
    x = run_tok("sc", False, x, n1[2], n2[2], gf, _f32(w1[2]), _f32(w2[2]), mp)
    m = run_fox(x, n1[3], A(fx_w_qkvf[0]), A(fx_b_f[0]))
    x = run_tok("post", True, x, n1[3], n2[3], gf, _f32(w1[3]), _f32(w2[3]), dict(w_out=_f32(fx_w_o[0])), m_in=m)
    return x


def build_fused(T=T):
    nc = bass.Bass("TRN2", target_bir_lowering=False)
    ext = lambda n, s: nc.dram_tensor(n, list(s), F32, kind="ExternalInput").ap()
    internal = lambda n, s: nc.dram_tensor(n, list(s), F32).ap()
    xT = ext("xT", [D, T])
    zh = ext("zh", [D, NH])
    out = nc.dram_tensor("out", [D, T], F32, kind="ExternalOutput").ap()
    w1 = [ext("w1_%d" % l, [D, DFF]) for l in range(4)]
    w2 = [ext("w2_%d" % l, [DFF, D]) for l in range(4)]
    vec = [ext("vec_%d" % l, [128, nv]) for l, nv in enumerate((50 + 593, 50, 50 + 49, 50))]
    cc_w_in = ext("cc_w_in", [D, 2 * D]); cc_w_out = ext("cc_w_out", [D, D])
    sc_w_in = ext("sc_w_in", [D, 3 * D]); sc_w_out = ext("sc_w_out", [D, D])
    rw_w_o = ext("rw_w_o", [D, D]); fx_w_o = ext("fx_w_o", [D, D])
    rw = []
    for h in range(2):
        rw.append(dict(wrkv=ext("rw_wrkv_%d" % h, [D, 3 * RF]), w2=ext("rw_w2_%d" % h, [96, RF]), a2=ext("rw_a2_%d" % h, [96, RF]),
                       g2=ext("rw_g2_%d" % h, [256, RF]), vec=ext("rw_vec_%d" % h, [128, 170])))
    rw_w1 = ext("rw_w1", [D, 96]); rw_a1 = ext("rw_a1", [D, 96]); rw_g1 = ext("rw_g1", [D, 256])
    bones = ext("bones", [128, 128]); rmask = ext("rmask", [128, NT]); mask5 = ext("mask5", [RN, 320])
    ident64 = ext("ident64", [RN, RN]); gneps = ext("gneps", [RN, 1])
    fx = []
    for h in range(2):
        fx.append(dict(wall=ext("fx_wall_%d" % h, [D, 3 * FH * FDH]), wf=ext("fx_wf_%d" % h, [D, FH]), vec=ext("fx_vec_%d" % h, [128, 20])))
    ident128 = ext("ident128", [128, 128]); fmask = ext("fmask", [128, 896]); sel = ext("sel", [FH, FH * 128])
    X1 = internal("X1", [D, T]); X2 = internal("X2", [D, T]); X3 = internal("X3", [D, T])
    YG = internal("YG", [D, T]); OO = internal("OO", [D, T])
    RA = {n: internal("RA_" + n, [RF, T]) for n in RA_OUTS}
    RA_PC = internal("RA_PC", [RF, T // CH])

    with contextlib.ExitStack() as st:
        env = Env(nc, st)
        env.io = dict(xT=xT, out=X1, w1=w1[0], w2=w2[0], xh=zh, w_in=cc_w_in, w_out=cc_w_out, vec=vec[0])
        build_tok("cc", False, ntok=T, env=env)
        for h in range(2):
            env.io = dict(xT=X1, wrkv=rw[h]["wrkv"], w1=rw_w1, w2=rw[h]["w2"], a1=rw_a1, a2=rw[h]["a2"], g1=rw_g1, g2=rw[h]["g2"],
                          vec=rw[h]["vec"], bones=bones, rmask=rmask, PC=RA_PC, **{n: RA[n] for n in RA_OUTS})
            build_rwkv_a(T, env=env)
            io = {n: RA[n].rearrange("(h j) t -> j h t", j=RN) for n in RA_OUTS}
            io.update(PC=RA_PC.rearrange("(h j) c -> j h c", j=RN), mask5=mask5, ident=ident64, eps=gneps,
                      out=YG[h * RF:(h + 1) * RF, :].rearrange("(h i) t -> i h t", i=RN))
            env.io = io
            build_rwkv_b(T, env=env)
        env.io = dict(xT=X1, out=X2, w1=w1[1], w2=w2[1], mT=YG, w_out=rw_w_o, vec=vec[1])
        build_tok("post", False, ntok=T, env=env)
        env.io = dict(xT=X2, out=X3, w1=w1[2], w2=w2[2], xh=zh, w_in=sc_w_in, w_out=sc_w_out, vec=vec[2])
        build_tok("sc", False, ntok=T, env=env)
        for h in range(2):
            env.io = dict(xT=X3, wall=fx[h]["wall"], wf=fx[h]["wf"], vec=fx[h]["vec"], ident=ident128, mask=fmask, sel=sel,
                          out=OO[h * FH * FDH:(h + 1) * FH * FDH, :])
            build_fox(T, env=env)
        env.io = dict(xT=X3, out=out, w1=w1[3], w2=w2[3], mT=OO, w_out=fx_w_o, vec=vec[3])
        build_tok("post", True, ntok=T, env=env)
        env.cx.P.barrier()
        env.cx.P.emit()
    return nc


def fused_inputs(x_bT, p):
    A = lambda a: np.asarray(a, dtype=np.float32)
    Tn = x_bT.shape[0]
    m = {"xT": _f32(x_bT.T), "zh": np.zeros((D, NH), np.float32)}
    eps2 = [np.full((128, 1), RMS_EPS, np.float32), np.full((128, 1), LN_EPS, np.float32)]
    zero1 = np.zeros((128, 1), np.float32)
    gf = pack_vec(p["final_g"])
    for l in range(4):
        m["w1_%d" % l] = _f32(p["mlp_w1"][l])
        m["w2_%d" % l] = _f32(p["mlp_w2"][l])
    base = lambda l: [pack_vec(p["norm1_g"][l]), pack_vec(p["norm2_g"][l]), gf] + eps2
    m["vec_0"] = _f32(np.concatenate(base(0) + [pack_vec(p["cc_b_in"][0])] + [pack_vec(p["cc_dw"][0][k]) for k in range(31)] +
                                     [pack_vec(p["cc_dw_b"][0]), pack_vec(p["cc_ln_g"][0]), pack_vec(p["cc_ln_b"][0]),
                                      pack_vec(p["cc_b_out"][0]), zero1], axis=1))
    m["vec_1"] = _f32(np.concatenate(base(1), axis=1))
    m["vec_2"] = _f32(np.concatenate(base(2) + [pack_vec(p["sc_conv_w"][0][k]) for k in range(3)] + [zero1], axis=1))
    m["vec_3"] = _f32(np.concatenate(base(3), axis=1))
    m["cc_w_in"] = _f32(p["cc_w_in"][0]); m["cc_w_out"] = _f32(p["cc_w_out"][0])
    m["sc_w_in"] = _f32(p["sc_w_in"][0]); m["sc_w_out"] = _f32(p["sc_w_out"][0])
    m["rw_w_o"] = _f32(p["rw_w_o"][0]); m["fx_w_o"] = _f32(p["fx_w_o"][0])
    bones, rmask, mask5, ident64 = rwkv_consts()
    m.update(bones=bones, rmask=rmask, mask5=mask5, ident64=ident64, gneps=np.full((RN, 1), GN_EPS, np.float32))
    m["rw_w1"] = _f32(p["rw_w1"][0]); m["rw_a1"] = _f32(p["rw_a1"][0]); m["rw_g1"] = _f32(p["rw_g1"][0])
    wr = A(p["rw_w_rkv"][0])
    for h in range(2):
        own = slice(h * RF, (h + 1) * RF)
        m["rw_wrkv_%d" % h] = _f32(np.concatenate([wr[0][:, own], wr[1][:, own], wr[2][:, own]], axis=1))
        m["rw_w2_%d" % h] = _f32(p["rw_w2"][0][:, own]); m["rw_a2_%d" % h] = _f32(p["rw_a2"][0][:, own])
        m["rw_g2_%d" % h] = _f32(p["rw_g2"][0][:, own])
        vec = np.zeros((128, 170), np.float32)
        vec[:, 0:16] = pack_vec(p["norm1_g"][1]); vec[:, 16] = RMS_EPS
        for i in range(6):
            vec[:, 18 + 16 * i:18 + 16 * (i + 1)] = pack_vec(p["rw_mu"][0][i])
        ownv = [p["rw_w0"][0], p["rw_a0"][0], p["rw_k_k"][0], p["rw_k_a"][0], p["rw_gn_g"][0], p["rw_gn_b"][0],
                A(p["rw_r_k"][0]).reshape(-1)]
        for i, v in enumerate(ownv):
            vec[:, 114 + 8 * i:114 + 8 * (i + 1)] = pack_vec(A(v)[own])
        m["rw_vec_%d" % h] = vec
    ident128, fmask, sel = fox_consts()
    m.update(ident128=ident128, fmask=fmask, sel=sel)
    wq = A(p["fx_w_qkvf"][0])
    for h in range(2):
        h0 = h * FH
        cols = slice(h0 * FDH, (h0 + FH) * FDH)
        m["fx_wall_%d" % h] = _f32(np.concatenate([wq[:, 0:D][:, cols], wq[:, D:2 * D][:, cols], wq[:, 2 * D:3 * D][:, cols]], axis=1))
        m["fx_wf_%d" % h] = _f32(wq[:, 3 * D + h0:3 * D + h0 + FH])
        vec = np.zeros((128, 20), np.float32)
        vec[:, 0:16] = pack_vec(p["norm1_g"][3]); vec[:, 16] = RMS_EPS; vec[:, 17] = 1.0
        vec[0:FH, 18] = A(p["fx_b_f"][0])[h0:h0 + FH]
        m["fx_vec_%d" % h] = vec
    return m


def kernel_fused(**p):
    x = np.asarray(p["x"], dtype=np.float32)
    nc = _get_nc(("fused",), build_fused)
    shared = None
    in_maps = []
    for c in range(NCORES):
        b = c // 2
        if shared is None:
            shared = fused_inputs(x[b], p)
            m = shared
        else:
            m = dict(shared)
            m["xT"] = _f32(x[b].T)
        in_maps.append(m)
    res = run_bass_kernel_spmd(nc, in_maps, core_ids=list(range(NCORES))).results
    y = np.empty((B, T, D), np.float32)
    for b in range(B):
        y[b] = res[2 * b]["out"].T
    return y


def kernel(**inputs):
    return kernel_fused(**inputs)
```

```python
import contextlib
import numpy as np
import concourse.bass as bass
import concourse.mybir as mybir
from concourse.bass_utils import run_bass_kernel_spmd

F32 = mybir.dt.float32
BF16 = mybir.dt.bfloat16
AF = mybir.ActivationFunctionType
ALU = mybir.AluOpType
AX = mybir.AxisListType

D = 2048
DC = 16
DFF = 8192
B = 4
T = 4096
NCORES = 8


class Buf:
    __slots__ = ("name", "w", "r", "dsem")

    def __init__(self, name):
        self.name = name
        self.w = None
        self.r = {}
        self.dsem = None


class Prog:
    ENGS = ("pe", "act", "dve", "pool", "sp")

    def __init__(self, nc):
        self.nc = nc
        self.ops = {e: [] for e in self.ENGS}
        self.seen = {e: {} for e in self.ENGS}
        self.ndsem = 0
        self.dcount = []
        self.free_dsems = []
        self.dkind = []

    def _dsem(self, buf, eng="sp"):
        if buf.dsem is None:
            kind = "sw" if eng == "pool" else "hw"
            fl = [i for i in self.free_dsems if self.dkind[i] == kind]
            if fl:
                buf.dsem = fl[-1]
                self.free_dsems.remove(fl[-1])
            else:
                buf.dsem = self.ndsem
                self.ndsem += 1
                self.dcount.append(0)
                self.dkind.append(kind)
        return buf.dsem

    def barrier(self):
        last = {}
        for e in self.ENGS:
            for i in range(len(self.ops[e]) - 1, -1, -1):
                o = self.ops[e][i]
                if o["fn"] is not None and o["dma"] is None:
                    last[e] = ("eng", e, i)
                    break
        for e in self.ENGS:
            deps = [tok for e2, tok in last.items() if e2 != e]
            deps += [("dma", s_, c) for s_, c in enumerate(self.dcount) if c > 0]
            idx = len(self.ops[e])
            waits = self._waits(e, idx, deps)
            self.ops[e].append({"waits": waits, "fn": None, "sig": False, "dma": None})
        self.free_dsems = list(range(self.ndsem))

    def _waits(self, eng, idx, deps):
        waits = []
        seen = self.seen[eng]
        for d in deps:
            if d[0] == "eng":
                _, e2, i2 = d
                if e2 == eng:
                    if eng in ("pe", "sp"):
                        continue
                key = ("eng", e2)
                if seen.get(key, -1) >= i2:
                    continue
                seen[key] = i2
                self.ops[e2][i2]["sig"] = True
                waits.append(d)
            else:
                _, s, c = d
                key = ("dma", s)
                if seen.get(key, -1) >= c:
                    continue
                seen[key] = c
                waits.append(d)
        return waits

    def _deps(self, reads, writes):
        deps = []
        for b in reads:
            if b.w is not None:
                deps.append(b.w)
        for b in writes:
            if b.w is not None:
                deps.append(b.w)
            deps.extend(b.r.values())
        return deps

    def op(self, eng, fn, reads=(), writes=()):
        idx = len(self.ops[eng])
        waits = self._waits(eng, idx, self._deps(reads, writes))
        self.ops[eng].append({"waits": waits, "fn": fn, "sig": False, "dma": None})
        tok = ("eng", eng, idx)
        for b in reads:
            b.r[("eng", eng)] = tok
        for b in writes:
            b.w = tok
            b.r = {}
        return tok

    def dma(self, eng, out, in_, reads=(), writes=(), sembuf=None, **kw):
        idx = len(self.ops[eng])
        waits = self._waits(eng, idx, self._deps(reads, writes))
        sb = sembuf or (writes[0] if writes else reads[0])
        s = self._dsem(sb, eng)
        self.dcount[s] += 16
        tok = ("dma", s, self.dcount[s])
        self.ops[eng].append({"waits": waits, "fn": lambda e: e.dma_start(out=out, in_=in_, **kw),
                              "sig": False, "dma": s})
        for b in reads:
            b.r[("dma", s)] = tok
        for b in writes:
            b.w = tok
            b.r = {}
        return tok

    def finish(self, bufs):
        deps = []
        for b in bufs:
            if b.w is not None:
                deps.append(b.w)
            deps.extend(b.r.values())
        idx = len(self.ops["sp"])
        waits = self._waits("sp", idx, deps)
        self.ops["sp"].append({"waits": waits, "fn": None, "sig": False, "dma": None})

    def emit(self):
        nc = self.nc
        with contextlib.ExitStack() as st:
            esem = {e: st.enter_context(nc.semaphore("s_" + e)) for e in self.ENGS}
            dsem = [st.enter_context(nc.semaphore("d%d" % i)) for i in range(self.ndsem)]
            cnt = {}
            for e in self.ENGS:
                c = 0
                arr = []
                for o in self.ops[e]:
                    if o["sig"]:
                        c += 1
                    arr.append(c)
                cnt[e] = arr
            block = st.enter_context(nc.Block())

            def run(e, eng):
                for o in self.ops[e]:
                    for w in o["waits"]:
                        if w[0] == "eng":
                            eng.wait_ge(esem[w[1]], cnt[w[1]][w[2]])
                        else:
                            eng.wait_ge(dsem[w[1]], w[2])
                    if o["fn"] is None:
                        continue
                    ins = o["fn"](eng)
                    if o["dma"] is not None:
                        ins.then_inc(dsem[o["dma"]], 16)
                    elif o["sig"]:
                        ins.then_inc(esem[e], 1)

            @block.tensor
            def _(eng):
                run("pe", eng)

            @block.scalar
            def _(eng):
                run("act", eng)

            @block.vector
            def _(eng):
                run("dve", eng)

            @block.gpsimd
            def _(eng):
                run("pool", eng)

            @block.sync
            def _(eng):
                run("sp", eng)


def _each(rng):
    def deco(fn):
        for x in rng:
            fn(x)
        return fn
    return deco


class Ctx:
    def __init__(self, nc, st):
        self.nc = nc
        self.st = st
        self.P = Prog(nc)
        self.n = 0
        self.psum = []
        self.pi = 0

    def sb(self, shape, dt, name=None):
        self.n += 1
        arena = getattr(self, "arena", None)
        if arena is None:
            return self.st.enter_context(self.nc.sbuf_tensor("sb_" + (name or ("t%d" % self.n)), list(shape), dt))
        shape = list(shape)
        elems = 1
        for d_ in shape[1:]:
            elems *= d_
        esz = 2 if dt == BF16 else 4
        words = (elems * esz + 3) // 4
        words = (words + 7) // 8 * 8
        assert self.off + words <= self.arena_words, ("arena overflow", name, self.off, words)
        v = arena[0:shape[0], self.off:self.off + words]
        self.off += words
        if dt == BF16:
            v = v.bitcast(BF16)
        v = v[:, 0:elems]
        if len(shape) == 3:
            v = v.rearrange("p (a b) -> p a b", a=shape[1])
        elif len(shape) == 4:
            v = v.rearrange("p (a b c) -> p a b c", a=shape[1], b=shape[2])
        return v

    def init_psum(self, nbanks=8):
        for i in range(nbanks):
            t = self.st.enter_context(self.nc.psum_tensor("ps%d" % i, [128, 512], F32))
            self.psum.append((t, Buf("ps%d" % i)))

    def ps(self):
        n = getattr(self, "rot_n", len(self.psum))
        r = self.psum[self.pi % n]
        self.pi += 1
        return r


class Env:
    def __init__(self, nc, st):
        self.nc = nc
        self.st = st
        self.cx = Ctx(nc, st)
        cx = self.cx
        cx.arena_words = 53000
        cx.arena = st.enter_context(nc.sbuf_tensor("arena", [128, cx.arena_words], F32))
        cx.off = 0
        self.psd = [st.enter_context(nc.psum_tensor("psd%d" % i, [128, 1024], F32)) for i in range(4)]
        self.io = {}
        self.first = True

    def dram(self, n, s, k="ExternalInput"):
        ap = self.io[n]
        assert list(ap.shape) == list(s), (n, ap.shape, s)
        return ap

    @contextlib.contextmanager
    def scope(self):
        cx = self.cx
        if not self.first:
            cx.P.barrier()
        self.first = False
        cx.off = 0
        cx.rot_n = 8
        cx.pi = 0
        cx.psum = [(self.psd[i // 2][:, (i % 2) * 512:(i % 2 + 1) * 512], Buf("ps%d" % i)) for i in range(8)]
        yield self.st


class WStream:
    def __init__(self, cx, nslots=3, kc=16, cols=512):
        self.cx = cx
        self.slots = [(cx.sb([128, kc, cols], BF16, "wsl%d" % i), Buf("wsl%d" % i)) for i in range(nslots)]
        self.i = 0
        self.kc = kc
        self.cols = cols

    def load(self, W, k0, kp, nk, c0, ncols):
        t, b = self.slots[self.i % len(self.slots)]
        self.i += 1
        src = W[k0:k0 + nk * kp, c0:c0 + ncols].rearrange("(kc p) m -> p kc m", p=kp)
        self.cx.P.dma("pool", t[0:kp, 0:nk, 0:ncols], src, writes=[b])
        return t, b


def gemm(cx, ws, W, K, M, act, N, epi, kp=128, mgrp=3, mp=128):
    P = cx.P
    nk = K // kp
    nm = M // mp
    kblk = ws.kc
    nkb = (nk + kblk - 1) // kblk
    for m0 in range(0, nm, mgrp):
        mg = min(mgrp, nm - m0)
        pss = [cx.ps() for _ in range(mg)]
        for kb in range(nkb):
            nkc = min(kblk, nk - kb * kblk)
            wt, wb = ws.load(W, kb * kblk * kp, kp, nkc, m0 * mp, mg * mp)
            for mi in range(mg):
                pt, pb = pss[mi]
                for kc in range(nkc):
                    a_ap, a_buf = act(kb * kblk + kc)
                    first = (kb == 0 and kc == 0)
                    last = (kb == nkb - 1 and kc == nkc - 1)
                    P.op("pe", (lambda e, pt=pt, wt=wt, kc=kc, mi=mi, a_ap=a_ap, first=first, last=last:
                                e.matmul(pt[0:mp, 0:N], wt[0:kp, kc, mi * mp:(mi + 1) * mp], a_ap,
                                         start=first, stop=last)),
                         reads=[wb, a_buf], writes=[pb])
        for mi in range(mg):
            pt, pb = pss[mi]
            epi(m0 + mi, pt[0:mp, 0:N], pb)


NTOK = 2048
NT = 512
NH = 32
RMS_EPS = 1e-6
LN_EPS = 1e-5


def pack_vec(v):
    v = np.ascontiguousarray(np.asarray(v, dtype=np.float32).reshape(-1))
    return np.ascontiguousarray(v.reshape(-1, 128).T)


def build_tok(mixer, final, ntok=NTOK, env=None, split=False):
    nc = env.nc if env else bass.Bass("TRN2", target_bir_lowering=False)
    dram = env.dram if env else (lambda n, s, k="ExternalInput": nc.dram_tensor(n, list(s), F32, kind=k).ap())
    xT = dram("xT", [D, ntok])
    nout = ntok // 2 if split else ntok
    out = dram("out", [D, nout], "ExternalOutput")
    w1 = dram("w1", [D, DFF])
    w2 = dram("w2", [DFF, D])
    o_eps = 48
    VB = 50
    if mixer == "cc":
        xh = dram("xh", [D, NH])
        w_in = dram("w_in", [D, 2 * D])
        w_out = dram("w_out", [D, D])
        o_bin, o_dw, o_dwb, o_lng, o_lnb, o_bout = VB, VB + 32, VB + 32 + 496, VB + 544, VB + 560, VB + 576
        o_mask = VB + 592
        NV = VB + 593
        CARRY = 30
    elif mixer == "sc":
        xh = dram("xh", [D, NH])
        w_in = dram("w_in", [D, 3 * D])
        w_out = dram("w_out", [D, D])
        o_cw = VB
        o_mask = VB + 48
        NV = VB + 49
        CARRY = 2
    else:
        mT = dram("mT", [D, ntok])
        w_out = dram("w_out", [D, D])
        o_m = VB
        NV = VB + 2 if split else VB
    vec_d = dram("vec", [128, NV])

    with (env.scope() if env else contextlib.ExitStack()) as st:
        cx = env.cx if env else Ctx(nc, st)
        P = cx.P
        if env is None:
            cx.init_psum(8)
        ws = WStream(cx, nslots=2, cols=384)
        vec = cx.sb([128, NV], F32, "vec")
        vec_b = Buf("vec")
        ones = cx.sb([128, 128], BF16, "ones")
        ones_b = Buf("ones")
        xs = [(cx.sb([128, NT], F32, "x%d" % c), Buf("x%d" % c)) for c in range(DC)]
        hs = [(cx.sb([128, NT], BF16, "h%d" % c), Buf("h%d" % c)) for c in range(DC)]
        qs = [(cx.sb([128, NT], BF16, "q%d" % c), Buf("q%d" % c)) for c in range(DC)]
        BIGW = 1568
        bigs = [(cx.sb([128, BIGW], F32, "big%d" % c), Buf("big%d" % c)) for c in range(DC)]
        sq = [(cx.sb([128, NT], BF16, "sq%d" % i), Buf("sq%d" % i)) for i in range(2)]
        tmpb = [(cx.sb([128, NT], BF16, "tb%d" % i), Buf("tb%d" % i)) for i in range(2)]
        tmpf = [(cx.sb([128, NT], F32, "tf%d" % i), Buf("tf%d" % i)) for i in range(3)]
        rstd = (cx.sb([128, NT], F32, "rstd"), Buf("rstd"))
        mean = (cx.sb([128, NT], F32, "mean"), Buf("mean"))
        cnt = {"sq": 0, "tb": 0, "tf": 0, "ev": 0}

        def rot(lst, key):
            r = lst[cnt[key] % len(lst)]
            cnt[key] += 1
            return r

        def a_view(k):
            t, b = bigs[k // 4]
            return t[:, 0:1024].bitcast(BF16)[:, (k % 4) * NT:(k % 4 + 1) * NT], b

        P.dma("sp", vec[:, :], vec_d[:, :], writes=[vec_b])
        P.op("dve", lambda e: e.memset(ones[:, :], 1.0), writes=[ones_b])

        def vcol(o, n=1):
            return vec[:, o:o + n]

        def rmsnorm(src, gcol, dst, N, dst_is_x=False):
            pt, pb = cx.ps()
            for c in range(DC):
                s_t, s_b = rot(sq, "sq")
                x_ap, x_b = src[c]
                P.op("act", lambda e, s_t=s_t, x_ap=x_ap: e.activation(out=s_t[:, 0:N], in_=x_ap, func=AF.Square),
                     reads=[x_b], writes=[s_b])
                P.op("pe", lambda e, s_t=s_t, c=c: e.matmul(pt[:, 0:N], ones[:, :], s_t[:, 0:N],
                                                          start=(c == 0), stop=(c == DC - 1)),
                     reads=[s_b, ones_b], writes=[pb])
            r_t, r_b = rstd
            P.op("act", lambda e: e.activation(out=r_t[:, 0:N], in_=pt[:, 0:N], func=AF.Sqrt, scale=1.0 / D,
                                               bias=vcol(o_eps)), reads=[pb, vec_b], writes=[r_b])
            P.op("dve", lambda e: e.reciprocal(out=r_t[:, 0:N], in_=r_t[:, 0:N]), reads=[r_b], writes=[r_b])
            for c in range(DC):
                x_ap, x_b = src[c]
                d_ap, d_b = dst[c]
                eng = "dve"
                P.op(eng, lambda e, x_ap=x_ap, d_ap=d_ap, c=c: e.scalar_tensor_tensor(
                    out=d_ap, in0=x_ap, scalar=vcol(gcol + c), in1=r_t[:, 0:N], op0=ALU.mult, op1=ALU.mult),
                    reads=[x_b, r_b, vec_b], writes=[d_b])

        def evac_engine():
            cnt["ev"] += 1
            return "act" if cnt["ev"] % 2 else "dve"

        def mixer_sc(N, halo):
            def gb(c): return bigs[c][0][:, 0:NT]
            def gc(c): return bigs[c][0][:, 512:512 + NT]
            def pp(c): return bigs[c][0][:, 1024:1024 + CARRY + NT]

            def epi(mc, ps, pb):
                if mc < 16:
                    if halo:
                        return
                    t, b = bigs[mc]
                    P.op("act", lambda e: e.copy(out=gb(mc)[:, 0:N], in_=ps), reads=[pb], writes=[b])
                elif mc < 32:
                    c = mc - 16
                    t, b = bigs[c]
                    P.op("act", lambda e: e.copy(out=gc(c)[:, 0:N], in_=ps), reads=[pb], writes=[b])
                else:
                    c = mc - 32
                    t, b = bigs[c]
                    if halo:
                        f_t, f_b = rot(tmpf, "tf")
                        P.op("dve", lambda e: e.tensor_tensor(out=f_t[:, 0:N], in0=gc(c)[:, 0:N], in1=ps, op=ALU.mult),
                             reads=[pb, b], writes=[f_b])
                        P.op("dve", lambda e: e.tensor_scalar(out=pp(c)[:, 0:CARRY], in0=f_t[:, N - CARRY:N],
                                                              scalar1=vcol(o_mask), scalar2=None, op0=ALU.mult),
                             reads=[f_b, vec_b], writes=[b])
                        return
                    P.op("dve", lambda e: e.tensor_tensor(out=pp(c)[:, CARRY:CARRY + N], in0=gc(c)[:, 0:N], in1=ps,
                                                          op=ALU.mult), reads=[pb, b], writes=[b])
                    z_t, z_b = rot(tmpf, "tf")
                    P.op("dve", lambda e: e.tensor_scalar(out=z_t[:, 0:N], in0=pp(c)[:, 0:N], scalar1=vcol(o_cw + c),
                                                          scalar2=None, op0=ALU.mult), reads=[b, vec_b], writes=[z_b])
                    for k in (1, 2):
                        P.op("dve", lambda e, k=k: e.scalar_tensor_tensor(
                            out=z_t[:, 0:N], in0=pp(c)[:, k:k + N], scalar=vcol(o_cw + 16 * k + c), in1=z_t[:, 0:N],
                            op0=ALU.mult, op1=ALU.add), reads=[b, vec_b, z_b], writes=[z_b])
                    q_t, q_b = qs[c]
                    P.op("dve", lambda e: e.tensor_tensor(out=q_t[:, 0:N], in0=gb(c)[:, 0:N], in1=z_t[:, 0:N],
                                                          op=ALU.mult), reads=[b, z_b], writes=[q_b])
                    P.op("act", lambda e: e.copy(out=pp(c)[:, 0:CARRY], in_=pp(c)[:, N:N + CARRY]),
                         reads=[b], writes=[b])

            gemm(cx, ws, w_in, D, 3 * D, lambda kc: (hs[kc][0][:, 0:N], hs[kc][1]), N, epi)
            if halo:
                return

            def epi_o(mc, ps, pb):
                x_t, x_b = xs[mc]
                P.op("dve", lambda e: e.tensor_tensor(out=x_t[:, 0:N], in0=x_t[:, 0:N], in1=ps, op=ALU.add),
                     reads=[pb, x_b], writes=[x_b])

            gemm(cx, ws, w_out, D, D, lambda kc: (qs[kc][0][:, 0:N], qs[kc][1]), N, epi_o)

        def mixer_cc(N, halo):
            def cv(c): return bigs[c][0][:, 0:NT]
            def uu(c): return bigs[c][0][:, 1024:1024 + CARRY + NT]
            KW = 31

            def epi(mc, ps, pb):
                if mc < 16:
                    t, b = bigs[mc]
                    P.op("act", lambda e: e.activation(out=cv(mc)[:, 0:N], in_=ps, func=AF.Identity,
                                                       bias=vcol(o_bin + mc)),
                         reads=[pb, vec_b], writes=[b])
                else:
                    c = mc - 16
                    t, b = bigs[c]
                    f_t, f_b = rot(tmpf, "tf")
                    P.op("act", lambda e: e.activation(out=f_t[:, 0:N], in_=ps, func=AF.Sigmoid,
                                                       bias=vcol(o_bin + 16 + c)),
                         reads=[pb, vec_b], writes=[f_b])
                    if halo:
                        P.op("dve", lambda e: e.tensor_tensor(out=f_t[:, 0:N], in0=cv(c)[:, 0:N], in1=f_t[:, 0:N],
                                                              op=ALU.mult), reads=[b, f_b], writes=[f_b])
                        P.op("dve", lambda e: e.tensor_scalar(out=uu(c)[:, 0:CARRY], in0=f_t[:, N - CARRY:N],
                                                              scalar1=vcol(o_mask), scalar2=None, op0=ALU.mult),
                             reads=[f_b, vec_b], writes=[b])
                        return
                    P.op("dve", lambda e: e.tensor_tensor(out=uu(c)[:, CARRY:CARRY + N], in0=cv(c)[:, 0:N],
                                                          in1=f_t[:, 0:N], op=ALU.mult), reads=[b, f_b], writes=[b])
                    eng = "dve"
                    P.op(eng, lambda e: e.tensor_scalar(out=cv(c)[:, 0:N], in0=uu(c)[:, 0:N], scalar1=vcol(o_dw + c),
                                                        scalar2=vcol(o_dwb + c), op0=ALU.mult, op1=ALU.add),
                         reads=[b, vec_b], writes=[b])
                    for k in range(1, KW):
                        P.op(eng, lambda e, k=k: e.scalar_tensor_tensor(
                            out=cv(c)[:, 0:N], in0=uu(c)[:, k:k + N], scalar=vcol(o_dw + 16 * k + c), in1=cv(c)[:, 0:N],
                            op0=ALU.mult, op1=ALU.add), reads=[b, vec_b], writes=[b])
                    P.op("act", lambda e: e.copy(out=uu(c)[:, 0:CARRY], in_=uu(c)[:, N:N + CARRY]),
                         reads=[b], writes=[b])

            gemm(cx, ws, w_in, D, 2 * D, lambda kc: (hs[kc][0][:, 0:N], hs[kc][1]), N, epi)
            if halo:
                return
            p1, p1b = cx.ps()
            p2, p2b = cx.ps()
            for c in range(DC):
                t, b = bigs[c]
                a_t, a_b = rot(tmpb, "tb")
                s_t, s_b = rot(sq, "sq")
                P.op("act", lambda e, a_t=a_t, c=c: e.copy(out=a_t[:, 0:N], in_=cv(c)[:, 0:N]), reads=[b], writes=[a_b])
                P.op("act", lambda e, s_t=s_t, c=c: e.activation(out=s_t[:, 0:N], in_=cv(c)[:, 0:N], func=AF.Square),
                     reads=[b], writes=[s_b])
                P.op("pe", lambda e, a_t=a_t, c=c: e.matmul(p1[:, 0:N], ones[:, :], a_t[:, 0:N], start=(c == 0),
                                                          stop=(c == DC - 1)), reads=[a_b, ones_b], writes=[p1b])
                P.op("pe", lambda e, s_t=s_t, c=c: e.matmul(p2[:, 0:N], ones[:, :], s_t[:, 0:N], start=(c == 0),
                                                          stop=(c == DC - 1)), reads=[s_b, ones_b], writes=[p2b])
            m_t, m_b = mean
            r_t, r_b = rstd
            P.op("dve", lambda e: e.tensor_scalar(out=m_t[:, 0:N], in0=p1[:, 0:N], scalar1=1.0 / D, scalar2=None,
                                                  op0=ALU.mult), reads=[p1b], writes=[m_b])
            f_t, f_b = rot(tmpf, "tf")
            P.op("dve", lambda e: e.tensor_tensor(out=f_t[:, 0:N], in0=m_t[:, 0:N], in1=m_t[:, 0:N], op=ALU.mult),
                 reads=[m_b], writes=[f_b])
            P.op("dve", lambda e: e.scalar_tensor_tensor(out=r_t[:, 0:N], in0=p2[:, 0:N], scalar=1.0 / D,
                                                         in1=f_t[:, 0:N], op0=ALU.mult, op1=ALU.subtract),
                 reads=[p2b, f_b], writes=[r_b])
            P.op("act", lambda e: e.activation(out=r_t[:, 0:N], in_=r_t[:, 0:N], func=AF.Sqrt,
                                               bias=vcol(o_eps + 1)), reads=[r_b, vec_b], writes=[r_b])
            P.op("dve", lambda e: e.reciprocal(out=r_t[:, 0:N], in_=r_t[:, 0:N]), reads=[r_b], writes=[r_b])
            for c in range(DC):
                t, b = bigs[c]
                g_t, g_b = rot(tmpf, "tf")
                P.op("dve", lambda e, c=c, g_t=g_t: e.tensor_tensor(out=g_t[:, 0:N], in0=cv(c)[:, 0:N], in1=m_t[:, 0:N],
                                                                  op=ALU.subtract), reads=[b, m_b], writes=[g_b])
                P.op("dve", lambda e, c=c, g_t=g_t: e.scalar_tensor_tensor(
                    out=g_t[:, 0:N], in0=g_t[:, 0:N], scalar=vcol(o_lng + c), in1=r_t[:, 0:N], op0=ALU.mult,
                    op1=ALU.mult), reads=[g_b, r_b, vec_b], writes=[g_b])
                q_t, q_b = qs[c]
                P.op("act", lambda e, c=c, g_t=g_t, q_t=q_t: e.activation(out=q_t[:, 0:N], in_=g_t[:, 0:N], func=AF.Silu,
                                                                        bias=vcol(o_lnb + c)),
                     reads=[g_b, vec_b], writes=[q_b])

            def epi_o(mc, ps, pb):
                x_t, x_b = xs[mc]
                P.op("dve", lambda e: e.scalar_tensor_tensor(out=x_t[:, 0:N], in0=ps, scalar=vcol(o_bout + mc),
                                                             in1=x_t[:, 0:N], op0=ALU.add, op1=ALU.add),
                     reads=[pb, x_b, vec_b], writes=[x_b])

            gemm(cx, ws, w_out, D, D, lambda kc: (qs[kc][0][:, 0:N], qs[kc][1]), N, epi_o)

        def mixer_post(N, t0):
            for c in range(DC):
                f_t, f_b = rot(tmpf, "tf")
                P.dma("sp", f_t[:, 0:N], mT[c * 128:(c + 1) * 128, t0:t0 + N], writes=[f_b])
                q_t, q_b = qs[c]
                if split:
                    g_t, g_b = rot(tmpf, "tf")
                    P.dma("sp", g_t[:, 0:N], mT[c * 128:(c + 1) * 128, nout + t0:nout + t0 + N], writes=[g_b])
                    P.op("dve", lambda e, f_t=f_t: e.tensor_scalar(out=f_t[:, 0:N], in0=f_t[:, 0:N], scalar1=vcol(o_m),
                                                                  scalar2=None, op0=ALU.mult), reads=[f_b, vec_b], writes=[f_b])
                    P.op("dve", lambda e, f_t=f_t, g_t=g_t, q_t=q_t: e.scalar_tensor_tensor(
                        out=q_t[:, 0:N], in0=g_t[:, 0:N], scalar=vcol(o_m + 1), in1=f_t[:, 0:N], op0=ALU.mult, op1=ALU.add),
                        reads=[g_b, f_b, vec_b], writes=[q_b])
                else:
                    P.op("act", lambda e, f_t=f_t, q_t=q_t: e.copy(out=q_t[:, 0:N], in_=f_t[:, 0:N]), reads=[f_b], writes=[q_b])

            def epi_o(mc, ps, pb):
                x_t, x_b = xs[mc]
                P.op("dve", lambda e: e.tensor_tensor(out=x_t[:, 0:N], in0=x_t[:, 0:N], in1=ps, op=ALU.add),
                     reads=[pb, x_b], writes=[x_b])

            gemm(cx, ws, w_out, D, D, lambda kc: (qs[kc][0][:, 0:N], qs[kc][1]), N, epi_o)

        def mlp(N):
            def epi1(mc, ps, pb):
                r_t, r_b = rot(tmpb, "tb")
                P.op("act", lambda e: e.activation(out=r_t[:, 0:N], in_=ps, func=AF.Relu), reads=[pb], writes=[r_b])
                a_ap, a_b = a_view(mc)
                P.op("dve", lambda e: e.tensor_tensor(out=a_ap[:, 0:N], in0=r_t[:, 0:N], in1=r_t[:, 0:N], op=ALU.mult),
                     reads=[r_b], writes=[a_b])

            gemm(cx, ws, w1, D, DFF, lambda kc: (hs[kc][0][:, 0:N], hs[kc][1]), N, epi1)

            def epi2(mc, ps, pb):
                x_t, x_b = xs[mc]
                P.op("dve", lambda e: e.tensor_tensor(out=x_t[:, 0:N], in0=x_t[:, 0:N], in1=ps, op=ALU.add),
                     reads=[pb, x_b], writes=[x_b])

            def actf(kc):
                ap, b = a_view(kc)
                return ap[:, 0:N], b

            gemm(cx, ws, w2, DFF, D, actf, N, epi2)

        if mixer in ("cc", "sc"):
            for c in range(DC):
                x_t, x_b = xs[c]
                P.dma("sp", x_t[:, 0:NH], xh[c * 128:(c + 1) * 128, :], writes=[x_b])
            rmsnorm([(xs[c][0][:, 0:NH], xs[c][1]) for c in range(DC)], 0,
                    [(hs[c][0][:, 0:NH], hs[c][1]) for c in range(DC)], NH)
            (mixer_cc if mixer == "cc" else mixer_sc)(NH, True)

        for ti in range(nout // NT):
            t0 = ti * NT
            N = NT
            for c in range(DC):
                x_t, x_b = xs[c]
                P.dma("sp", x_t[:, 0:N], xT[c * 128:(c + 1) * 128, t0:t0 + N], writes=[x_b])
                if split:
                    g_t, g_b = rot(tmpf, "tf")
                    P.dma("sp", g_t[:, 0:N], xT[c * 128:(c + 1) * 128, nout + t0:nout + t0 + N], writes=[g_b])
                    P.op("dve", lambda e, x_t=x_t: e.tensor_scalar(out=x_t[:, 0:N], in0=x_t[:, 0:N], scalar1=vcol(o_m),
                                                                  scalar2=None, op0=ALU.mult), reads=[x_b, vec_b], writes=[x_b])
                    P.op("dve", lambda e, x_t=x_t, g_t=g_t: e.scalar_tensor_tensor(
                        out=x_t[:, 0:N], in0=g_t[:, 0:N], scalar=vcol(o_m + 1), in1=x_t[:, 0:N], op0=ALU.mult, op1=ALU.add),
                        reads=[g_b, x_b, vec_b], writes=[x_b])
            xsrc = [(xs[c][0][:, 0:N], xs[c][1]) for c in range(DC)]
            hdst = [(hs[c][0][:, 0:N], hs[c][1]) for c in range(DC)]
            if mixer == "post":
                mixer_post(N, t0)
            else:
                rmsnorm(xsrc, 0, hdst, N)
                (mixer_cc if mixer == "cc" else mixer_sc)(N, False)
            rmsnorm(xsrc, 16, hdst, N)
            mlp(N)
            if final:
                rmsnorm(xsrc, 32, xsrc, N)
            for c in range(DC):
                x_t, x_b = xs[c]
                P.dma("sp", out[c * 128:(c + 1) * 128, t0:t0 + N], x_t[:, 0:N], reads=[x_b])
        if env is None:
            P.finish([b for _, b in xs])
            P.emit()
    return nc


_NC_CACHE = {}


def _get_nc(key, builder):
    if key not in _NC_CACHE:
        _NC_CACHE[key] = builder()
    return _NC_CACHE[key]


def _f32(a):
    return np.ascontiguousarray(np.asarray(a, dtype=np.float32))


def run_tok(mixer, final, x, g1, g2, gf, w1, w2, mp, m_in=None):
    nc = _get_nc(("tok", mixer, final), lambda: build_tok(mixer, final))
    base = [pack_vec(g1), pack_vec(g2), pack_vec(gf), np.full((128, 1), RMS_EPS, np.float32),
            np.full((128, 1), LN_EPS, np.float32)]
    in_maps = []
    for c in range(NCORES):
        b, half = c // 2, c % 2
        t0 = half * NTOK
        m = {"xT": _f32(x[b, t0:t0 + NTOK, :].T), "w1": w1, "w2": w2}
        mask = np.full((128, 1), 1.0 if half == 1 else 0.0, np.float32)
        if mixer in ("cc", "sc"):
            if half == 1:
                m["xh"] = _f32(x[b, t0 - NH:t0, :].T)
            else:
                m["xh"] = np.zeros((D, NH), np.float32)
        if mixer == "cc":
            vecs = base + [pack_vec(mp["b_in"]), ] + [pack_vec(mp["dw"][k]) for k in range(31)] + \
                [pack_vec(mp["dw_b"]), pack_vec(mp["ln_g"]), pack_vec(mp["ln_b"]), pack_vec(mp["b_out"]), mask]
            m["w_in"] = mp["w_in"]
            m["w_out"] = mp["w_out"]
        elif mixer == "sc":
            vecs = base + [pack_vec(mp["conv_w"][k]) for k in range(3)] + [mask]
            m["w_in"] = mp["w_in"]
            m["w_out"] = mp["w_out"]
        else:
            vecs = base
            m["mT"] = _f32(m_in[b, t0:t0 + NTOK, :].T)
            m["w_out"] = mp["w_out"]
        m["vec"] = _f32(np.concatenate(vecs, axis=1))
        in_maps.append(m)
    res = run_bass_kernel_spmd(nc, in_maps, core_ids=list(range(NCORES)))
    y = np.empty((B, T, D), np.float32)
    for c in range(NCORES):
        b, half = c // 2, c % 2
        t0 = half * NTOK
        y[b, t0:t0 + NTOK, :] = res.results[c]["out"].T
    return y


FH = 8
FDH = 128
FOX_NEG = -30000.0


def build_fox(T=T, env=None):
    nc = env.nc if env else bass.Bass("TRN2", target_bir_lowering=False)
    dram = env.dram if env else (lambda n, s, k="ExternalInput": nc.dram_tensor(n, list(s), F32, kind=k).ap())
    xT = dram("xT", [D, T])
    wall = dram("wall", [D, 3 * FH * FDH])
    wf_d = dram("wf", [D, FH])
    vec_d = dram("vec", [128, 20])
    ident_d = dram("ident", [128, 128])
    mask_d = dram("mask", [128, 896])
    sel_d = dram("sel", [FH, FH * 128])
    out = dram("out", [FH * FDH, T], "ExternalOutput")
    NTI = T // NT
    scale = 1.0 / float(np.sqrt(FDH))

    with (env.scope() if env else contextlib.ExitStack()) as st:
        cx = env.cx if env else Ctx(nc, st)
        P = cx.P
        if env is None:
            cx.init_psum(8)
        cx.rot_n = 4
        ws = WStream(cx, nslots=2, cols=256)
        vec = cx.sb([128, 20], F32, "vec"); vec_b = Buf("vec")
        ident = cx.sb([128, 128], F32, "ident"); ident_b = Buf("ident")
        mask = cx.sb([128, 896], F32, "mask"); mask_b = Buf("mask")
        sel = cx.sb([FH, FH * 128], F32, "sel"); sel_b = Buf("sel")
        ones = cx.sb([128, 128], BF16, "ones"); ones_b = Buf("ones")
        onesf = cx.sb([FH, NT], F32, "onesf"); onesf_b = Buf("onesf")
        wf = cx.sb([128, DC, FH], BF16, "wfs"); wf_b = Buf("wfs")
        negbf = cx.sb([FH, 1], F32, "negbf"); negbf_b = Buf("negbf")
        kT = cx.sb([128, FH, T], BF16, "kT"); kT_b = [Buf("kT%d" % j) for j in range(NTI)]
        vtm = cx.sb([128, T // 128, FH * FDH], BF16, "vtm"); vtm_b = [Buf("vtm%d" % j) for j in range(NTI)]
        qT = cx.sb([128, FH, NT], BF16, "qT"); qT_b = [Buf("qT%d" % h) for h in range(FH)]
        hs = [(cx.sb([128, NT], BF16, "h%d" % c), Buf("h%d" % c)) for c in range(DC)]
        xst = [(cx.sb([128, NT], F32, "xs%d" % i), Buf("xs%d" % i)) for i in range(2)]
        sq = [(cx.sb([128, NT], BF16, "sq%d" % i), Buf("sq%d" % i)) for i in range(2)]
        tmpf = [(cx.sb([128, NT], F32, "tf%d" % i), Buf("tf%d" % i)) for i in range(3)]
        ptb = [(cx.sb([128, NT], BF16, "pt%d" % i), Buf("pt%d" % i)) for i in range(3)]
        cqb = [(cx.sb([128, NT], F32, "cqb%d" % i), Buf("cqb%d" % i)) for i in range(1)]
        rstd = (cx.sb([128, NT], F32, "rstd"), Buf("rstd"))
        cfm = [(cx.sb([FH, NT], F32, "cfm%d" % i), Buf("cfm%d" % i)) for i in range(2)]
        lfm = (cx.sb([FH, NT], F32, "lfm"), Buf("lfm"))
        negc = cx.sb([128, T // 128, FH], F32, "negc"); negc_b = [Buf("negc%d" % j) for j in range(NTI)]
        KK = cx.sb([128, FH], F32, "KK"); KK_b = Buf("KK")
        kmax = (cx.sb([128, 1], F32, "kmax"), Buf("kmax"))
        ost = [(cx.sb([128, NT], F32, "ost%d" % i), Buf("ost%d" % i)) for i in range(1)]
        cnt = {}

        def rot(lst, key):
            cnt[key] = cnt.get(key, 0) + 1
            return lst[(cnt[key] - 1) % len(lst)]

        def vcol(o, n=1):
            return vec[:, o:o + n]

        P.dma("sp", vec[:, :], vec_d[:, :], writes=[vec_b])
        P.dma("sp", ident[:, :], ident_d[:, :], writes=[ident_b])
        P.dma("sp", mask[:, :], mask_d[:, :], writes=[mask_b])
        P.dma("sp", sel[:, :], sel_d[:, :], writes=[sel_b])
        P.dma("pool", wf[:, :, :], wf_d.rearrange("(kc p) m -> p kc m", p=128), writes=[wf_b])
        P.op("dve", lambda e: e.memset(ones[:, :], 1.0), writes=[ones_b])
        P.op("dve", lambda e: e.memset(onesf[:, :], 1.0), writes=[onesf_b])
        P.op("dve", lambda e: e.memset(KK[:, :], 0.0), writes=[KK_b])
        P.op("dve", lambda e: e.tensor_scalar(out=negbf[:, :], in0=vec[0:FH, 18:19], scalar1=-1.0, scalar2=None,
                                              op0=ALU.mult), reads=[vec_b], writes=[negbf_b])

        @_each(range(NTI))
        def _body_j(j):
            t0 = j * NT
            N = NT
            pt, pb = cx.ps()
            for c in range(DC):
                x_t, x_b = rot(xst, "xs")
                s_t, s_b = rot(sq, "sq")
                P.dma("sp", x_t[:, :], xT[c * 128:(c + 1) * 128, t0:t0 + N], writes=[x_b])
                P.op("act", lambda e, s_t=s_t, x_t=x_t: e.activation(out=s_t[:, :], in_=x_t[:, :], func=AF.Square),
                     reads=[x_b], writes=[s_b])
                P.op("pe", lambda e, s_t=s_t, c=c, pt=pt: e.matmul(pt[:, 0:N], ones[:, :], s_t[:, :], start=(c == 0),
                                                                 stop=(c == DC - 1)), reads=[s_b, ones_b], writes=[pb])
            r_t, r_b = rstd
            P.op("act", lambda e, pt=pt: e.activation(out=r_t[:, :], in_=pt[:, 0:N], func=AF.Sqrt, scale=1.0 / D,
                                                      bias=vcol(16)), reads=[pb, vec_b], writes=[r_b])
            P.op("dve", lambda e: e.reciprocal(out=r_t[:, :], in_=r_t[:, :]), reads=[r_b], writes=[r_b])
            for c in range(DC):
                x_t, x_b = rot(xst, "xs")
                P.dma("sp", x_t[:, :], xT[c * 128:(c + 1) * 128, t0:t0 + N], writes=[x_b])
                h_t, h_b = hs[c]
                P.op("dve", lambda e, x_t=x_t, h_t=h_t, c=c: e.scalar_tensor_tensor(
                    out=h_t[:, :], in0=x_t[:, :], scalar=vcol(c), in1=r_t[:, :], op0=ALU.mult, op1=ALU.mult),
                    reads=[x_b, r_b, vec_b], writes=[h_b])

            def epi_qk(mc, ps, pb, j=j, t0=t0):
                if mc < FH:
                    P.op("act", lambda e: e.mul(out=qT[:, mc, :], in_=ps, mul=scale), reads=[pb], writes=[qT_b[mc]])
                else:
                    hh = mc - FH
                    P.op("dve", lambda e: e.tensor_copy(out=kT[:, hh, t0:t0 + N], in_=ps), reads=[pb], writes=[kT_b[j]])

            gemm(cx, ws, wall[:, 0:2 * FH * FDH], D, 2 * FH * FDH, lambda kc: (hs[kc][0][:, :], hs[kc][1]), N, epi_qk, mgrp=2)

            for hh in range(FH):
                s_t, s_b = rot(sq, "sq")
                P.op("act", lambda e, s_t=s_t, hh=hh: e.activation(out=s_t[:, :], in_=kT[:, hh, t0:t0 + N], func=AF.Square),
                     reads=[kT_b[j]], writes=[s_b])
                p2, p2b = cx.ps()
                P.op("pe", lambda e, s_t=s_t, p2=p2: e.matmul(p2[:, 0:N], ones[:, :], s_t[:, :], start=True, stop=True),
                     reads=[s_b, ones_b], writes=[p2b])
                km_t, km_b = kmax
                P.op("dve", lambda e, p2=p2: e.tensor_reduce(out=km_t[:, :], in_=p2[:, 0:N], axis=AX.X, op=ALU.max),
                     reads=[p2b], writes=[km_b])
                P.op("dve", lambda e, hh=hh: e.tensor_tensor(out=KK[:, hh:hh + 1], in0=KK[:, hh:hh + 1], in1=km_t[:, :],
                                                            op=ALU.max), reads=[km_b, KK_b], writes=[KK_b])

            vc0 = 2 * FH * FDH
            for c0 in range(0, FH * FDH, 256):
                ncols = min(256, FH * FDH - c0)
                wt, wb = ws.load(wall, 0, 128, DC, vc0 + c0, ncols)
                for tb in range(4):
                    pv, pvb = cx.ps()
                    for kc in range(DC):
                        P.op("pe", lambda e, pv=pv, wt=wt, kc=kc, tb=tb, ncols=ncols: e.matmul(
                            pv[:, 0:ncols], hs[kc][0][:, tb * 128:(tb + 1) * 128], wt[:, kc, 0:ncols],
                            start=(kc == 0), stop=(kc == DC - 1)), reads=[wb, hs[kc][1]], writes=[pvb])
                    eng = "act" if tb % 2 == 0 else "dve"
                    if eng == "act":
                        P.op("act", lambda e, pv=pv, tb=tb, c0=c0, ncols=ncols: e.copy(
                            out=vtm[:, j * 4 + tb, c0:c0 + ncols], in_=pv[:, 0:ncols]), reads=[pvb], writes=[vtm_b[j]])
                    else:
                        P.op("dve", lambda e, pv=pv, tb=tb, c0=c0, ncols=ncols: e.tensor_copy(
                            out=vtm[:, j * 4 + tb, c0:c0 + ncols], in_=pv[:, 0:ncols]), reads=[pvb], writes=[vtm_b[j]])

            pf, pfb = cx.ps()
            for kc in range(DC):
                P.op("pe", lambda e, pf=pf, kc=kc: e.matmul(pf[0:FH, 0:N], wf[:, kc, :], hs[kc][0][:, :], start=(kc == 0),
                                                          stop=(kc == DC - 1)), reads=[wf_b, hs[kc][1]], writes=[pfb])
            l_t, l_b = lfm
            P.op("act", lambda e, pf=pf: e.activation(out=l_t[:, :], in_=pf[0:FH, 0:N], func=AF.Exp, scale=-1.0,
                                                      bias=negbf[:, :]), reads=[pfb, negbf_b], writes=[l_b])
            P.op("act", lambda e: e.activation(out=l_t[:, :], in_=l_t[:, :], func=AF.Ln, bias=vec[0:FH, 17:18]),
                 reads=[l_b, vec_b], writes=[l_b])
            c_t, c_b = cfm[j % 2]
            cp_t, cp_b = cfm[(j + 1) % 2]
            if j == 0:
                P.op("dve", lambda e, c_t=c_t: e.tensor_tensor_scan(out=c_t[:, :], data0=onesf[:, :], data1=l_t[:, :],
                                                                    initial=0.0, op0=ALU.mult, op1=ALU.subtract),
                     reads=[onesf_b, l_b], writes=[c_b])
            else:
                P.op("dve", lambda e, c_t=c_t, cp_t=cp_t: e.tensor_tensor_scan(
                    out=c_t[:, :], data0=onesf[:, :], data1=l_t[:, :], initial=cp_t[:, N - 1:N], op0=ALU.mult,
                    op1=ALU.subtract), reads=[onesf_b, l_b, cp_b], writes=[c_b])
            for tb in range(4):
                ptr, ptrb = cx.ps()
                P.op("pe", lambda e, ptr=ptr, tb=tb, c_t=c_t: e.transpose(out=ptr[:, 0:FH], in_=c_t[:, tb * 128:(tb + 1) * 128],
                                                                        identity=ident[0:FH, 0:FH]),
                     reads=[c_b, ident_b], writes=[ptrb])
                P.op("act", lambda e, ptr=ptr, tb=tb: e.mul(out=negc[:, j * 4 + tb, :], in_=ptr[:, 0:FH], mul=-1.0),
                     reads=[ptrb], writes=[negc_b[j]])

            for hh in range(FH):
                s_t, s_b = rot(sq, "sq")
                P.op("act", lambda e, s_t=s_t, hh=hh: e.activation(out=s_t[:, :], in_=qT[:, hh, :], func=AF.Square),
                     reads=[qT_b[hh]], writes=[s_b])
                pq, pqb = cx.ps()
                P.op("pe", lambda e, s_t=s_t, pq=pq: e.matmul(pq[:, 0:N], ones[:, :], s_t[:, :], start=True, stop=True),
                     reads=[s_b, ones_b], writes=[pqb])
                m_t, m_b = rot(tmpf, "tf")
                P.op("act", lambda e, pq=pq, m_t=m_t, hh=hh: e.activation(out=m_t[:, :], in_=pq[:, 0:N], func=AF.Sqrt,
                                                                         scale=KK[:, hh:hh + 1]),
                     reads=[pqb, KK_b], writes=[m_b])
                pc, pcb = cx.ps()
                P.op("pe", lambda e, pc=pc, hh=hh, c_t=c_t: e.matmul(pc[:, 0:N], sel[:, hh * 128:(hh + 1) * 128], c_t[:, :],
                                                                   start=True, stop=True), reads=[sel_b, c_b], writes=[pcb])
                q_t, q_b = rot(cqb, "cqb")
                P.op("dve", lambda e, q_t=q_t, m_t=m_t, pc=pc: e.scalar_tensor_tensor(
                    out=q_t[:, :], in0=m_t[:, :], scalar=-1.02, in1=pc[:, 0:N], op0=ALU.mult, op1=ALU.add),
                    reads=[m_b, pcb], writes=[q_b])
                accO, accOb = cx.psum[4 + 2 * (hh % 2)]
                accD, accDb = cx.psum[5 + 2 * (hh % 2)]
                nblk = 4 * j + 4
                LA = 2
                pend = {}

                def stage1(i):
                    pS, pSb = cx.ps()
                    P.op("pe", lambda e, pS=pS, i=i, hh=hh: e.matmul(pS[:, 0:N], kT[:, hh, i * 128:(i + 1) * 128], qT[:, hh, :],
                                                                   start=True, stop=True),
                         reads=[kT_b[i // 4], qT_b[hh]], writes=[pSb])
                    f_t, f_b = rot(tmpf, "tf")
                    P.op("dve", lambda e, f_t=f_t, pS=pS, q_t=q_t: e.tensor_tensor(out=f_t[:, :], in0=pS[:, 0:N], in1=q_t[:, :],
                                                                                 op=ALU.add), reads=[pSb, q_b], writes=[f_b])
                    if i >= 4 * j:
                        r = i - 4 * j
                        P.op("dve", lambda e, f_t=f_t, r=r: e.tensor_tensor(
                            out=f_t[:, :], in0=f_t[:, :], in1=mask[:, 384 - 128 * r:384 - 128 * r + NT], op=ALU.add),
                            reads=[f_b, mask_b], writes=[f_b])
                    p_t, p_b = rot(ptb, "pt")
                    P.op("act", lambda e, p_t=p_t, f_t=f_t, i=i, hh=hh: e.activation(
                        out=p_t[:, :], in_=f_t[:, :], func=AF.Exp, bias=negc[:, i, hh:hh + 1]),
                        reads=[f_b, negc_b[i // 4]], writes=[p_b])
                    pend[i] = (p_t, p_b)

                def stage2(i):
                    p_t, p_b = pend.pop(i)
                    P.op("pe", lambda e, accO=accO, p_t=p_t, i=i, hh=hh, nblk=nblk: e.matmul(
                        accO[:, 0:N], vtm[:, i, hh * FDH:(hh + 1) * FDH], p_t[:, :], start=(i == 0), stop=(i == nblk - 1)),
                        reads=[vtm_b[i // 4], p_b], writes=[accOb])
                    P.op("pe", lambda e, accD=accD, p_t=p_t, i=i, nblk=nblk: e.matmul(
                        accD[:, 0:N], ones[:, :], p_t[:, :], start=(i == 0), stop=(i == nblk - 1)),
                        reads=[ones_b, p_b], writes=[accDb])

                for i in range(nblk + LA):
                    if i < nblk:
                        stage1(i)
                    if i >= LA:
                        stage2(i - LA)
                rc_t, rc_b = rot(tmpf, "tf")
                P.op("dve", lambda e, rc_t=rc_t, accD=accD: e.reciprocal(out=rc_t[:, :], in_=accD[:, 0:N]),
                     reads=[accDb], writes=[rc_b])
                o_t, o_b = rot(ost, "ost")
                P.op("dve", lambda e, o_t=o_t, rc_t=rc_t, accO=accO: e.tensor_tensor(out=o_t[:, :], in0=accO[:, 0:N],
                                                                                     in1=rc_t[:, :], op=ALU.mult),
                     reads=[accOb, rc_b], writes=[o_b])
                P.dma("sp", out[hh * FDH:(hh + 1) * FDH, t0:t0 + N], o_t[:, :], reads=[o_b])
        if env is None:
            P.finish([b for _, b in ost])
            P.emit()
    return nc


def fox_consts():
    ident = np.eye(128, dtype=np.float32)
    p = np.arange(128)[:, None]
    xx = np.arange(896)[None, :]
    mask = np.where(xx - 384 >= p, 0.0, FOX_NEG).astype(np.float32)
    sel = np.zeros((FH, FH * 128), np.float32)
    for h in range(FH):
        sel[h, h * 128:(h + 1) * 128] = 1.0
    return ident, mask, sel


def run_fox(x, g1, w_qkvf, b_f):
    nc = _get_nc(("fox",), build_fox)
    ident, mask, sel = fox_consts()
    in_maps = []
    for c in range(NCORES):
        b, hg = c // 2, c % 2
        h0 = hg * FH
        cols = slice(h0 * FDH, (h0 + FH) * FDH)
        wall = np.concatenate([w_qkvf[:, 0:D][:, cols], w_qkvf[:, D:2 * D][:, cols], w_qkvf[:, 2 * D:3 * D][:, cols]], axis=1)
        vec = np.zeros((128, 20), np.float32)
        vec[:, 0:16] = pack_vec(g1)
        vec[:, 16] = RMS_EPS
        vec[:, 17] = 1.0
        vec[0:FH, 18] = np.asarray(b_f)[h0:h0 + FH]
        in_maps.append({"xT": _f32(x[b].T), "wall": _f32(wall), "wf": _f32(w_qkvf[:, 3 * D + h0:3 * D + h0 + FH]),
                        "vec": vec, "ident": ident, "mask": mask, "sel": sel})
    res = run_bass_kernel_spmd(nc, in_maps, core_ids=list(range(NCORES)))
    o = np.empty((B, T, D), np.float32)
    for c in range(NCORES):
        b, hg = c // 2, c % 2
        o[b, :, hg * FH * FDH:(hg + 1) * FH * FDH] = res.results[c]["out"].T
    return o


RH = 16
RN = 64
RF = RH * RN
RFC = RF // 128
CH = 64
C0 = float(np.exp(-0.5))
GN_EPS = 64e-5
RA_OUTS = ("At", "Kt", "Bt", "Rt", "Kh", "Bh", "Vb", "T1", "T2")


def build_rwkv_a(T=T, env=None):
    nc = env.nc if env else bass.Bass("TRN2", target_bir_lowering=False)
    dram = env.dram if env else (lambda n, s, k="ExternalInput": nc.dram_tensor(n, list(s), F32, kind=k).ap())
    xT = dram("xT", [D, T])
    wrkv = dram("wrkv", [D, 3 * RF])
    w1_d = dram("w1", [D, 96]); w2_d = dram("w2", [96, RF])
    a1_d = dram("a1", [D, 96]); a2_d = dram("a2", [96, RF])
    g1_d = dram("g1", [D, 256]); g2_d = dram("g2", [256, RF])
    o_mu = 18
    o_own = 114
    o_w0, o_a0, o_kk, o_ka, o_gng, o_gnb, o_rk = [o_own + 8 * i for i in range(7)]
    NV = o_own + 56
    vec_d = dram("vec", [128, NV])
    bones_d = dram("bones", [128, 128])
    rmask_d = dram("rmask", [128, NT])
    outs = {n: dram(n, [RF, T], "ExternalOutput") for n in RA_OUTS}
    pc_d = dram("PC", [RF, T // CH], "ExternalOutput")
    NTI = T // NT
    NCH = NT // CH

    with (env.scope() if env else contextlib.ExitStack()) as st:
        cx = env.cx if env else Ctx(nc, st)
        P = cx.P
        if env is None:
            cx.init_psum(8)
        ws = WStream(cx, nslots=2, cols=256)
        vec = cx.sb([128, NV], F32, "vec"); vec_b = Buf("vec")
        omm = cx.sb([128, 96], F32, "omm"); omm_b = Buf("omm")
        bonesf = cx.sb([128, 128], F32, "bonesf"); bonesf_b = Buf("bonesf")
        bones = cx.sb([128, 128], BF16, "bones"); bones_b = Buf("bones")
        ones = cx.sb([128, 128], BF16, "ones"); ones_b = Buf("ones")
        rmask = cx.sb([128, NT], F32, "rmask"); rmask_b = Buf("rmask")
        hf = [(cx.sb([128, NT + 1], F32, "hf%d" % c), Buf("hf%d" % c)) for c in range(DC)]
        mixs = [[(cx.sb([128, NT], BF16, "mx%d_%d" % (s_, c)), Buf("mx%d_%d" % (s_, c))) for c in range(DC)] for s_ in range(2)]
        G = {}
        for nm in ("r", "k", "v", "s", "a", "g"):
            G[nm] = [(cx.sb([128, NT], F32, "G%s%d" % (nm, f)), Buf("G%s%d" % (nm, f))) for f in range(RFC)]
        th = (cx.sb([96, NT], BF16, "th"), Buf("th"))
        ah = (cx.sb([96, NT], BF16, "ah"), Buf("ah"))
        gh = [(cx.sb([128, NT], BF16, "gh%d" % i), Buf("gh%d" % i)) for i in range(2)]
        tf = [(cx.sb([128, NT], F32, "tf%d" % i), Buf("tf%d" % i)) for i in range(6)]
        tb = [(cx.sb([128, NT], BF16, "tb%d" % i), Buf("tb%d" % i)) for i in range(2)]
        og = [(cx.sb([128, NT], F32, "og%d" % i), Buf("og%d" % i)) for i in range(3)]
        pcs = cx.sb([128, RFC, T // CH], F32, "pcs"); pcs_b = Buf("pcs")
        cnt = {}

        def rot(lst, key):
            cnt[key] = cnt.get(key, 0) + 1
            return lst[(cnt[key] - 1) % len(lst)]

        def vcol(o, n=1):
            return vec[:, o:o + n]

        P.dma("sp", vec[:, :], vec_d[:, :], writes=[vec_b])
        P.dma("sp", bonesf[:, :], bones_d[:, :], writes=[bonesf_b])
        P.dma("sp", rmask[:, :], rmask_d[:, :], writes=[rmask_b])
        P.op("dve", lambda e: e.memset(ones[:, :], 1.0), writes=[ones_b])
        P.op("dve", lambda e: e.tensor_copy(out=bones[:, :], in_=bonesf[:, :]), reads=[bonesf_b], writes=[bones_b])
        P.op("dve", lambda e: e.tensor_scalar(out=omm[:, :], in0=vec[:, o_mu:o_mu + 96], scalar1=-1.0, scalar2=1.0,
                                              op0=ALU.mult, op1=ALU.add), reads=[vec_b], writes=[omm_b])
        for c in range(DC):
            P.op("dve", lambda e, c=c: e.memset(hf[c][0][:, 0:1], 0.0), writes=[hf[c][1]])

        @_each(range(NTI))
        def _body_j(j):
            t0 = j * NT
            N = NT
            if j > 0:
                for c in range(DC):
                    P.op("act", lambda e, c=c: e.copy(out=hf[c][0][:, 0:1], in_=hf[c][0][:, N:N + 1]),
                         reads=[hf[c][1]], writes=[hf[c][1]])
            pt, pb = cx.ps()
            for c in range(DC):
                h_t, h_b = hf[c]
                s_t, s_b = rot(tb, "tb")
                P.dma("sp", h_t[:, 1:N + 1], xT[c * 128:(c + 1) * 128, t0:t0 + N], writes=[h_b])
                P.op("act", lambda e, s_t=s_t, h_t=h_t: e.activation(out=s_t[:, :], in_=h_t[:, 1:N + 1], func=AF.Square),
                     reads=[h_b], writes=[s_b])
                P.op("pe", lambda e, s_t=s_t, c=c, pt=pt: e.matmul(pt[:, 0:N], ones[:, :], s_t[:, :], start=(c == 0),
                                                                 stop=(c == DC - 1)), reads=[s_b, ones_b], writes=[pb])
            r_t, r_b = rot(tf, "tf")
            P.op("act", lambda e, pt=pt: e.activation(out=r_t[:, :], in_=pt[:, 0:N], func=AF.Sqrt, scale=1.0 / D,
                                                      bias=vcol(16)), reads=[pb, vec_b], writes=[r_b])
            P.op("dve", lambda e: e.reciprocal(out=r_t[:, :], in_=r_t[:, :]), reads=[r_b], writes=[r_b])
            for c in range(DC):
                h_t, h_b = hf[c]
                P.op("dve", lambda e, h_t=h_t, c=c: e.scalar_tensor_tensor(
                    out=h_t[:, 1:N + 1], in0=h_t[:, 1:N + 1], scalar=vcol(c), in1=r_t[:, :], op0=ALU.mult, op1=ALU.mult),
                    reads=[h_b, r_b, vec_b], writes=[h_b])

            def mix(i, s_):
                for c in range(DC):
                    h_t, h_b = hf[c]
                    m_t, m_b = mixs[s_][c]
                    f_t, f_b = rot(tf, "tf")
                    P.op("dve", lambda e, f_t=f_t, h_t=h_t, c=c: e.tensor_scalar(
                        out=f_t[:, :], in0=h_t[:, 1:N + 1], scalar1=omm[:, i * 16 + c:i * 16 + c + 1], scalar2=None,
                        op0=ALU.mult), reads=[h_b, omm_b], writes=[f_b])
                    P.op("dve", lambda e, f_t=f_t, h_t=h_t, m_t=m_t, c=c: e.scalar_tensor_tensor(
                        out=m_t[:, :], in0=h_t[:, 0:N], scalar=vcol(o_mu + i * 16 + c), in1=f_t[:, :], op0=ALU.mult,
                        op1=ALU.add), reads=[h_b, f_b, vec_b], writes=[m_b])

            actfs = [(lambda kc, s_=s_: (mixs[s_][kc][0][:, :], mixs[s_][kc][1])) for s_ in range(2)]

            def epi_store(nm, func=None, bias_o=None):
                def epi(mc, ps, pb):
                    g_t, g_b = G[nm][mc]
                    if func is None:
                        eng = "act" if mc % 2 == 0 else "dve"
                        if eng == "act":
                            P.op("act", lambda e: e.copy(out=g_t[:, :], in_=ps), reads=[pb], writes=[g_b])
                        else:
                            P.op("dve", lambda e: e.tensor_copy(out=g_t[:, :], in_=ps), reads=[pb], writes=[g_b])
                    else:
                        P.op("act", lambda e: e.activation(out=g_t[:, :], in_=ps, func=func, bias=vcol(bias_o + mc)),
                             reads=[pb, vec_b], writes=[g_b])
                return epi

            def epi_th(mc, ps, pb):
                P.op("act", lambda e: e.activation(out=th[0][:, :], in_=ps, func=AF.Tanh), reads=[pb], writes=[th[1]])

            def epi_ah(mc, ps, pb):
                P.op("act", lambda e: e.copy(out=ah[0][:, :], in_=ps), reads=[pb], writes=[ah[1]])

            def epi_gh(mc, ps, pb):
                P.op("act", lambda e: e.activation(out=gh[mc][0][:, :], in_=ps, func=AF.Sigmoid), reads=[pb],
                     writes=[gh[mc][1]])

            mix(0, 0)
            mix(2, 1)
            gemm(cx, ws, wrkv[:, 0:RF], D, RF, actfs[0], N, epi_store("r"), mgrp=2)
            mix(3, 0)
            gemm(cx, ws, wrkv[:, RF:2 * RF], D, RF, actfs[1], N, epi_store("k"), mgrp=2)
            mix(1, 1)
            gemm(cx, ws, wrkv[:, 2 * RF:3 * RF], D, RF, actfs[0], N, epi_store("v"), mgrp=2)
            mix(4, 0)
            gemm(cx, ws, w1_d, D, 96, actfs[1], N, epi_th, mgrp=1, mp=96)
            gemm(cx, ws, w2_d, 96, RF, lambda kc: (th[0][:, :], th[1]), N, epi_store("s", AF.Sigmoid, o_w0), kp=96, mgrp=2)
            mix(5, 1)
            gemm(cx, ws, a1_d, D, 96, actfs[0], N, epi_ah, mgrp=1, mp=96)
            gemm(cx, ws, a2_d, 96, RF, lambda kc: (ah[0][:, :], ah[1]), N, epi_store("a", AF.Sigmoid, o_a0), kp=96, mgrp=2)
            gemm(cx, ws, g1_d, D, 256, actfs[1], N, epi_gh, mgrp=2)
            gemm(cx, ws, g2_d, 256, RF, lambda kc: (gh[kc][0][:, :], gh[kc][1]), N, epi_store("g"), mgrp=2)

            @_each(range(RFC))
            def _body_f(f):
                r_t_, r_b_ = G["r"][f]; k_t, k_b = G["k"][f]; v_t, v_b = G["v"][f]
                s_t, s_b = G["s"][f]; a_t, a_b = G["a"][f]; g_t, g_b = G["g"][f]
                rows = slice(f * 128, (f + 1) * 128)

                def store(nm, o_t, o_b):
                    P.dma("sp", outs[nm][rows, t0:t0 + N], o_t[:, :], reads=[o_b])

                kk_t, kk_b = rot(tf, "tf")
                P.op("dve", lambda e: e.tensor_scalar(out=kk_t[:, :], in0=k_t[:, :], scalar1=vcol(o_kk + f), scalar2=None,
                                                      op0=ALU.mult), reads=[k_b, vec_b], writes=[kk_b])
                q_t, q_b = rot(tb, "tb")
                P.op("act", lambda e: e.activation(out=q_t[:, :], in_=kk_t[:, :], func=AF.Square), reads=[kk_b], writes=[q_b])
                p1, p1b = cx.ps()
                P.op("pe", lambda e: e.matmul(p1[:, 0:N], bones[:, :], q_t[:, :], start=True, stop=True),
                     reads=[bones_b, q_b], writes=[p1b])
                n_t, n_b = rot(tf, "tf")
                P.op("act", lambda e: e.activation(out=n_t[:, :], in_=p1[:, 0:N], func=AF.Sqrt), reads=[p1b], writes=[n_b])
                P.op("dve", lambda e: e.tensor_scalar(out=n_t[:, :], in0=n_t[:, :], scalar1=1e-12, scalar2=None, op0=ALU.max),
                     reads=[n_b], writes=[n_b])
                P.op("dve", lambda e: e.reciprocal(out=n_t[:, :], in_=n_t[:, :]), reads=[n_b], writes=[n_b])
                P.op("dve", lambda e: e.tensor_tensor(out=kk_t[:, :], in0=kk_t[:, :], in1=n_t[:, :], op=ALU.mult),
                     reads=[kk_b, n_b], writes=[kk_b])
                u_t, u_b = rot(tf, "tf")
                P.op("dve", lambda e: e.tensor_scalar(out=u_t[:, :], in0=a_t[:, :], scalar1=-1.0, scalar2=vcol(o_ka + f),
                                                      op0=ALU.add, op1=ALU.mult), reads=[a_b, vec_b], writes=[u_b])
                P.op("dve", lambda e: e.scalar_tensor_tensor(out=k_t[:, :], in0=u_t[:, :], scalar=1.0, in1=k_t[:, :],
                                                             op0=ALU.add, op1=ALU.mult), reads=[u_b, k_b], writes=[k_b])
                P.op("dve", lambda e: e.tensor_tensor(out=a_t[:, :], in0=kk_t[:, :], in1=a_t[:, :], op=ALU.mult),
                     reads=[kk_b, a_b], writes=[a_b])
                P.op("dve", lambda e: e.tensor_tensor(out=u_t[:, :], in0=r_t_[:, :], in1=k_t[:, :], op=ALU.mult),
                     reads=[r_b_, k_b], writes=[u_b])
                q2_t, q2_b = rot(tb, "tb")
                P.op("dve", lambda e: e.tensor_scalar(out=q2_t[:, :], in0=u_t[:, :], scalar1=vcol(o_rk + f), scalar2=None,
                                                      op0=ALU.mult), reads=[u_b, vec_b], writes=[q2_b])
                p2, p2b = cx.ps()
                P.op("pe", lambda e: e.matmul(p2[:, 0:N], bones[:, :], q2_t[:, :], start=True, stop=True),
                     reads=[bones_b, q2_b], writes=[p2b])
                o1_t, o1_b = rot(og, "og")
                P.op("dve", lambda e: e.tensor_tensor(out=o1_t[:, :], in0=p2[:, 0:N], in1=v_t[:, :], op=ALU.mult),
                     reads=[p2b, v_b], writes=[o1_b])
                P.op("dve", lambda e: e.scalar_tensor_tensor(out=o1_t[:, :], in0=o1_t[:, :], scalar=vcol(o_gnb + f),
                                                             in1=g_t[:, :], op0=ALU.add, op1=ALU.mult),
                     reads=[o1_b, g_b, vec_b], writes=[o1_b])
                store("T1", o1_t, o1_b)
                o2_t, o2_b = rot(og, "og")
                P.op("dve", lambda e: e.tensor_scalar(out=o2_t[:, :], in0=g_t[:, :], scalar1=vcol(o_gng + f), scalar2=None,
                                                      op0=ALU.mult), reads=[g_b, vec_b], writes=[o2_b])
                store("T2", o2_t, o2_b)
                store("Vb", v_t, v_b)
                cs_t, cs_b = rot(tf, "tf")
                P.op("dve", lambda e: e.tensor_tensor_scan(out=cs_t[:, :], data0=rmask[:, :], data1=s_t[:, :], initial=0.0,
                                                           op0=ALU.mult, op1=ALU.add), reads=[rmask_b, s_b], writes=[cs_b])
                e_t, e_b = rot(tf, "tf")
                P.op("act", lambda e: e.activation(out=e_t[:, :], in_=cs_t[:, :], func=AF.Exp, scale=-C0), reads=[cs_b], writes=[e_b])
                o_t, o_b = rot(og, "og")
                P.op("dve", lambda e, o_t=o_t: e.tensor_tensor(out=o_t[:, :], in0=r_t_[:, :], in1=e_t[:, :], op=ALU.mult),
                     reads=[r_b_, e_b], writes=[o_b])
                store("Rt", o_t, o_b)
                e2_t, e2_b = rot(tf, "tf")
                P.op("act", lambda e: e.activation(out=e2_t[:, :], in_=cs_t[:, :], func=AF.Exp, scale=C0), reads=[cs_b], writes=[e2_b])
                o_t, o_b = rot(og, "og")
                P.op("dve", lambda e, o_t=o_t: e.tensor_tensor(out=o_t[:, :], in0=k_t[:, :], in1=e2_t[:, :], op=ALU.mult),
                     reads=[k_b, e2_b], writes=[o_b])
                store("Kt", o_t, o_b)
                o_t, o_b = rot(og, "og")
                P.op("dve", lambda e, o_t=o_t: e.tensor_tensor(out=o_t[:, :], in0=a_t[:, :], in1=e2_t[:, :], op=ALU.mult),
                     reads=[a_b, e2_b], writes=[o_b])
                store("Bt", o_t, o_b)
                P.op("dve", lambda e: e.tensor_tensor(out=e_t[:, :], in0=cs_t[:, :], in1=s_t[:, :], op=ALU.subtract),
                     reads=[cs_b, s_b], writes=[e_b])
                P.op("act", lambda e: e.activation(out=e_t[:, :], in_=e_t[:, :], func=AF.Exp, scale=-C0), reads=[e_b], writes=[e_b])
                o_t, o_b = rot(og, "og")
                P.op("dve", lambda e, o_t=o_t: e.scalar_tensor_tensor(out=o_t[:, :], in0=kk_t[:, :], scalar=-1.0, in1=e_t[:, :],
                                                                      op0=ALU.mult, op1=ALU.mult),
                     reads=[kk_b, e_b], writes=[o_b])
                store("At", o_t, o_b)
                cs3 = cs_t[:, :].rearrange("p (c t) -> p c t", t=CH)
                P.op("dve", lambda e: e.tensor_tensor(out=e2_t[:, :].rearrange("p (c t) -> p c t", t=CH),
                                                      in0=cs3[:, :, CH - 1:CH].to_broadcast([128, NCH, CH]), in1=cs3,
                                                      op=ALU.subtract), reads=[cs_b], writes=[e2_b])
                P.op("act", lambda e: e.activation(out=e2_t[:, :], in_=e2_t[:, :], func=AF.Exp, scale=-C0), reads=[e2_b], writes=[e2_b])
                o_t, o_b = rot(og, "og")
                P.op("dve", lambda e, o_t=o_t: e.tensor_tensor(out=o_t[:, :], in0=k_t[:, :], in1=e2_t[:, :], op=ALU.mult),
                     reads=[k_b, e2_b], writes=[o_b])
                store("Kh", o_t, o_b)
                o_t, o_b = rot(og, "og")
                P.op("dve", lambda e, o_t=o_t: e.tensor_tensor(out=o_t[:, :], in0=a_t[:, :], in1=e2_t[:, :], op=ALU.mult),
                     reads=[a_b, e2_b], writes=[o_b])
                store("Bh", o_t, o_b)
                P.op("act", lambda e: e.activation(out=pcs[:, f, j * NCH:(j + 1) * NCH], in_=cs3[:, :, CH - 1], func=AF.Exp,
                                                   scale=-C0), reads=[cs_b], writes=[pcs_b])
        for f in range(RFC):
            P.dma("sp", pc_d[f * 128:(f + 1) * 128, :], pcs[:, f, :], reads=[pcs_b])
        if env is None:
            P.finish([b for _, b in og] + [pcs_b] + [b for _, b in G["v"]])
            P.emit()
    return nc


RB_TN = 128
_RWB_STOP = None


def build_rwkv_b(T=T, env=None):
    nc = env.nc if env else bass.Bass("TRN2", target_bir_lowering=False)
    dram = env.dram if env else (lambda n, s, k="ExternalInput": nc.dram_tensor(n, list(s), F32, kind=k).ap())
    FMN = ("At", "Kt", "Bt", "Rt", "T1", "T2", "Vb", "Kh", "Bh")
    if env:
        fm3 = {n: env.io[n] for n in FMN}
        pc3 = env.io["PC"]
        out3 = env.io["out"]
    else:
        fm3 = {n: dram(n, [RN, RH * T]).rearrange("j (h t) -> j h t", h=RH) for n in FMN}
        pc3 = dram("PC", [RN, RH * (T // CH)]).rearrange("j (h c) -> j h c", h=RH)
    mask_d = dram("mask5", [RN, 320])
    ident_d = dram("ident", [RN, RN])
    eps_d = dram("eps", [RN, 1])
    if not env:
        out3 = dram("out", [RN, RH * T], "ExternalOutput").rearrange("j (h t) -> j h t", h=RH)
    NTI = T // RB_TN
    NCC = RB_TN // CH
    HW = RH * RN

    def fmv(ap):
        return ap.rearrange("j (h t) -> j h t", h=RH)

    with (env.scope() if env else contextlib.ExitStack()) as st:
        cx = env.cx if env else Ctx(nc, st)
        P = cx.P
        pst = []
        for i in range(4):
            t = env.psd[i] if env else st.enter_context(nc.psum_tensor("psd%d" % i, [128, 1024], F32))
            pst.append((t, Buf("psd%d" % i)))
        pcount = [0]

        def pstile():
            r = pst[pcount[0] % 4]
            pcount[0] += 1
            return r

        mask5 = cx.sb([RN, 320], F32, "mask5"); mask_b = Buf("mask5")
        ident = cx.sb([RN, RN], F32, "ident"); ident_b = Buf("ident")
        epsc = cx.sb([RN, 1], F32, "eps"); eps_b = Buf("eps")
        PC = cx.sb([RN, RH, T // CH], F32, "PC"); PC_b = Buf("PC")
        fm = {}
        for n in ("At", "Kt", "Bt", "Rt"):
            fm[n] = [(cx.sb([RN, RH, RB_TN], BF16, "%s%d" % (n, i)), Buf("%s%d" % (n, i))) for i in range(2)]
        for n in ("T1", "T2"):
            fm[n] = [(cx.sb([RN, RH, RB_TN], F32, "%s%d" % (n, i)), Buf("%s%d" % (n, i))) for i in range(1)]
        for n in ("Vb", "Kh", "Bh"):
            fm[n] = [(cx.sb([RN, RH, RB_TN], BF16, "%s%d" % (n, i)), Buf("%s%d" % (n, i))) for i in range(2)]
        tmt = {n: (cx.sb([CH, HW], BF16, "tm%s" % n), Buf("tm%s" % n)) for n in ("Vb", "Kh", "Bh")}
        identb = cx.sb([RN, RN], BF16, "identb"); identb_b = Buf("identb")
        Mm = cx.sb([RN, RH, 320], BF16, "Mm"); Mm_b = Buf("Mm")
        XX = [(cx.sb([RN, RH, 2, RN], BF16, "XX%d" % i), Buf("XX%d" % i)) for i in range(2)]
        Rf = cx.sb([RN, RH, RN], F32, "Rf"); Rf_b = Buf("Rf")
        Rb = cx.sb([RN, RH, RN], BF16, "Rb"); Rb_b = Buf("Rb")
        Wt = cx.sb([RN, HW], BF16, "Wt"); Wt_b = Buf("Wt")
        Ut = cx.sb([RN, HW], BF16, "Ut"); Ut_b = Buf("Ut")
        Sf = cx.sb([RN, RH, RN], F32, "Sf"); Sf_b = Buf("Sf")
        Sb = cx.sb([RN, RH, RN], BF16, "Sb"); Sb_b = Buf("Sb")
        ysq = cx.sb([RN, HW], F32, "ysq"); ysq_b = Buf("ysq")
        yn = cx.sb([RN, HW], F32, "yn"); yn_b = Buf("yn")
        st1 = cx.sb([RN, RH], F32, "st1"); st1_b = Buf("st1")
        st2 = cx.sb([RN, RH], F32, "st2"); st2_b = Buf("st2")
        st3 = cx.sb([RN, RH], F32, "st3"); st3_b = Buf("st3")
        ostg = [(cx.sb([RN, RH, RB_TN], F32, "ostg%d" % i), Buf("ostg%d" % i)) for i in range(2)]

        P.dma("sp", mask5[:, :], mask_d[:, :], writes=[mask_b])
        P.dma("sp", ident[:, :], ident_d[:, :], writes=[ident_b])
        P.dma("sp", epsc[:, :], eps_d[:, :], writes=[eps_b])
        P.dma("sp", PC[:, :, :], pc3, writes=[PC_b])
        P.op("dve", lambda e: e.tensor_copy(out=identb[:, :], in_=ident[:, :]), reads=[ident_b], writes=[identb_b])
        P.op("dve", lambda e: e.memset(Sf[:, :, :], 0.0), writes=[Sf_b])
        P.op("dve", lambda e: e.memset(Sb[:, :, :], 0.0), writes=[Sb_b])
        ev = [0]

        def evac_copy(out_ap, in_ap, reads, writes):
            ev[0] += 1
            if ev[0] % 2:
                P.op("act", lambda e: e.copy(out=out_ap, in_=in_ap), reads=reads, writes=writes)
            else:
                P.op("dve", lambda e: e.tensor_copy(out=out_ap, in_=in_ap), reads=reads, writes=writes)

        @_each(range(NTI))
        def _body_ti(ti):
            t0 = ti * RB_TN
            cur = {}
            for n in ("At", "Kt", "Bt", "Rt", "Vb", "Kh", "Bh"):
                t_, b_ = fm[n][ti % 2]
                P.dma("pool", t_[:, :, :], fm3[n][:, :, t0:t0 + RB_TN], writes=[b_])
                cur[n] = (t_, b_)
            for n in ("T1", "T2"):
                t_, b_ = fm[n][0]
                P.dma("sp", t_[:, :, :], fm3[n][:, :, t0:t0 + RB_TN], writes=[b_])
                cur[n] = (t_, b_)
            og_t, og_b = ostg[ti % 2]
            At, At_b = cur["At"]; Kt, Kt_b = cur["Kt"]; Bt, Bt_b = cur["Bt"]; Rt, Rt_b = cur["Rt"]
            Vf, Vf_b = cur["Vb"]; Khf, Khf_b = cur["Kh"]; Bhf, Bhf_b = cur["Bh"]
            Vt, Vt_b = tmt["Vb"]; Kh, Kh_b = tmt["Kh"]; Bh, Bh_b = tmt["Bh"]
            T1, T1_b = cur["T1"]; T2, T2_b = cur["T2"]
            @_each(range(NCC))
            def _body_cc(cc):
                gc = ti * NCC + cc
                tc = slice(cc * CH, (cc + 1) * CH)
                hs_ = lambda h: slice(h * RN, (h + 1) * RN)
                for (src, src_b, dst, dst_b) in ((Vf, Vf_b, Vt, Vt_b), (Khf, Khf_b, Kh, Kh_b), (Bhf, Bhf_b, Bh, Bh_b)):
                    ptt, ptb_ = pstile()
                    pv16 = ptt[0:RN, 0:512].bitcast(BF16)
                    for h in range(RH):
                        P.op("pe", lambda e, pv16=pv16, h=h, src=src: e.transpose(out=pv16[:, hs_(h)], in_=src[:, h, tc],
                                                                                 identity=identb[:, :]),
                             reads=[src_b, identb_b], writes=[ptb_])
                    evac_copy(dst[:, :], pv16[:, 0:HW], [ptb_], [dst_b])
                for h0 in range(0, RH, 3):
                    nh = min(3, RH - h0)
                    pt, pb = pstile()
                    for hh in range(nh):
                        h = h0 + hh
                        o = hh * 320
                        for bi, (L, Lb, R_, R_b) in enumerate(((Kt, Kt_b, At, At_b), (Kt, Kt_b, Rt, Rt_b),
                                                              (Bt, Bt_b, At, At_b), (Bt, Bt_b, Rt, Rt_b),
                                                              (At, At_b, Bt, Bt_b))):
                            P.op("pe", lambda e, pt=pt, o=o, bi=bi, L=L, R_=R_, h=h: e.matmul(
                                pt[0:RN, o + bi * 64:o + (bi + 1) * 64], L[:, h, tc], R_[:, h, tc], start=True, stop=True),
                                reads=[Lb, R_b], writes=[pb])
                    P.op("dve", lambda e, pt=pt, h0=h0, nh=nh: e.tensor_tensor(
                        out=Mm[:, h0:h0 + nh, :], in0=pt[0:RN, 0:nh * 320].rearrange("p (h x) -> p h x", h=nh),
                        in1=mask5[:, :].unsqueeze(1).to_broadcast([RN, nh, 320]), op=ALU.mult),
                        reads=[pb, mask_b], writes=[Mm_b])
                if _RWB_STOP == "M":
                    return
                P.op("dve", lambda e: e.tensor_tensor(out=Rf[:, :, :], in0=Mm[:, :, 128:192],
                                                      in1=ident[:, :].unsqueeze(1).to_broadcast([RN, RH, RN]), op=ALU.add),
                     reads=[Mm_b, ident_b], writes=[Rf_b])
                P.op("act", lambda e: e.copy(out=Rb[:, :, :], in_=Rf[:, :, :]), reads=[Rf_b], writes=[Rb_b])
                for it in range(5):
                    if it == 0:
                        Xf = lambda h: Mm[:, h, 128:192]
                        XTf = lambda h: Mm[:, h, 256:320]
                        src_b = Mm_b
                    else:
                        xs_t, xs_b = XX[(it - 1) % 2]
                        Xf = lambda h, xs_t=xs_t: xs_t[:, h, 0, :]
                        XTf = lambda h, xs_t=xs_t: xs_t[:, h, 1, :]
                        src_b = xs_b
                    xd_t, xd_b = XX[it % 2]
                    for h0 in range(0, RH, 8):
                        pt, pb = pstile()
                        for hh in range(8):
                            h = h0 + hh
                            if it < 4:
                                P.op("pe", lambda e, pt=pt, hh=hh, h=h, Xf=Xf, XTf=XTf: e.matmul(
                                    pt[0:RN, hh * 128:hh * 128 + 64], XTf(h), Xf(h), start=True, stop=True),
                                    reads=[src_b], writes=[pb])
                            P.op("pe", lambda e, pt=pt, hh=hh, h=h, Xf=Xf, XTf=XTf: e.matmul(
                                pt[0:RN, hh * 128 + 64:hh * 128 + 128], Xf(h), XTf(h), start=True, stop=True),
                                reads=[src_b], writes=[pb])
                        if it < 4:
                            evac_copy(xd_t[:, h0:h0 + 8, :, :], pt[0:RN, 0:1024].rearrange("p (h a x) -> p h a x", h=8, a=2),
                                      [pb], [xd_b])
                        else:
                            evac_copy(xd_t[:, h0:h0 + 8, 1, :],
                                      pt[0:RN, 0:1024].rearrange("p (h a x) -> p h a x", h=8, a=2)[:, :, 1, :], [pb], [xd_b])
                    pt, pb = pstile()
                    for h in range(RH):
                        P.op("pe", lambda e, pt=pt, h=h, xd_t=xd_t: e.matmul(pt[0:RN, hs_(h)], xd_t[:, h, 1, :], Rb[:, h, :],
                                                                           start=True, stop=True),
                             reads=[xd_b, Rb_b], writes=[pb])
                    P.op("dve", lambda e, pt=pt: e.tensor_tensor(out=Rf[:, :, :].rearrange("p h x -> p (h x)"),
                                                                 in0=Rf[:, :, :].rearrange("p h x -> p (h x)"),
                                                                 in1=pt[0:RN, 0:HW], op=ALU.add),
                         reads=[pb, Rf_b], writes=[Rf_b])
                    P.op("act", lambda e: e.copy(out=Rb[:, :, :], in_=Rf[:, :, :]), reads=[Rf_b], writes=[Rb_b])
                if _RWB_STOP == "inv":
                    return
                pt, pb = pstile()
                for h in range(RH):
                    P.op("pe", lambda e, pt=pt, h=h: e.matmul(pt[0:RN, hs_(h)], At[:, h, tc], Sb[:, h, :], start=True, stop=False),
                         reads=[At_b, Sb_b], writes=[pb])
                    P.op("pe", lambda e, pt=pt, h=h: e.matmul(pt[0:RN, hs_(h)], Mm[:, h, 0:64], Vt[:, hs_(h)], start=False,
                                                            stop=True), reads=[Mm_b, Vt_b], writes=[pb])
                evac_copy(Wt[:, :], pt[0:RN, 0:HW], [pb], [Wt_b])
                if _RWB_STOP == "W":
                    return
                pt, pb = pstile()
                for h in range(RH):
                    P.op("pe", lambda e, pt=pt, h=h: e.matmul(pt[0:RN, hs_(h)], Rb[:, h, :], Wt[:, hs_(h)], start=True, stop=True),
                         reads=[Rb_b, Wt_b], writes=[pb])
                evac_copy(Ut[:, :], pt[0:RN, 0:HW], [pb], [Ut_b])
                if _RWB_STOP == "U":
                    return
                py, pyb = pstile()
                for h in range(RH):
                    P.op("pe", lambda e, py=py, h=h: e.matmul(py[0:RN, hs_(h)], Rt[:, h, tc], Sb[:, h, :], start=True, stop=False),
                         reads=[Rt_b, Sb_b], writes=[pyb])
                    P.op("pe", lambda e, py=py, h=h: e.matmul(py[0:RN, hs_(h)], Mm[:, h, 192:256], Ut[:, hs_(h)], start=False,
                                                            stop=False), reads=[Mm_b, Ut_b], writes=[pyb])
                    P.op("pe", lambda e, py=py, h=h: e.matmul(py[0:RN, hs_(h)], Mm[:, h, 64:128], Vt[:, hs_(h)], start=False,
                                                            stop=True), reads=[Mm_b, Vt_b], writes=[pyb])
                if _RWB_STOP == "Y":
                    return
                pS, pSb = pstile()
                for h in range(RH):
                    P.op("pe", lambda e, pS=pS, h=h: e.matmul(pS[0:RN, hs_(h)], Bh[:, hs_(h)], Ut[:, hs_(h)], start=True,
                                                            stop=False), reads=[Bh_b, Ut_b], writes=[pSb])
                    P.op("pe", lambda e, pS=pS, h=h: e.matmul(pS[0:RN, hs_(h)], Kh[:, hs_(h)], Vt[:, hs_(h)], start=False,
                                                            stop=True), reads=[Kh_b, Vt_b], writes=[pSb])
                if _RWB_STOP == "S":
                    return
                y3 = py[0:RN, 0:HW].rearrange("p (h x) -> p h x", h=RH)
                P.op("act", lambda e, py=py: e.activation(out=ysq[:, :], in_=py[0:RN, 0:HW], func=AF.Square), reads=[pyb],
                     writes=[ysq_b])
                if _RWB_STOP == "g1":
                    return
                P.op("act", lambda e, py=py: e.copy(out=yn[:, :], in_=py[0:RN, 0:HW]), reads=[pyb], writes=[yn_b])
                P.op("dve", lambda e: e.tensor_reduce(out=st1[:, :], in_=yn[:, :].rearrange("p (h x) -> p h x", h=RH), axis=AX.X,
                                                      op=ALU.add), reads=[yn_b], writes=[st1_b])
                if _RWB_STOP == "g2":
                    return
                P.op("dve", lambda e: e.tensor_reduce(out=st2[:, :], in_=ysq[:, :].rearrange("p (h x) -> p h x", h=RH), axis=AX.X,
                                                      op=ALU.add), reads=[ysq_b], writes=[st2_b])
                if _RWB_STOP == "g3":
                    return
                P.op("dve", lambda e: e.tensor_scalar(out=st1[:, :], in0=st1[:, :], scalar1=1.0 / RN, scalar2=None, op0=ALU.mult),
                     reads=[st1_b], writes=[st1_b])
                P.op("dve", lambda e: e.tensor_tensor(out=st3[:, :], in0=st1[:, :], in1=st1[:, :], op=ALU.mult),
                     reads=[st1_b], writes=[st3_b])
                P.op("dve", lambda e: e.scalar_tensor_tensor(out=st2[:, :], in0=st2[:, :], scalar=1.0 / RN, in1=st3[:, :],
                                                             op0=ALU.mult, op1=ALU.subtract), reads=[st2_b, st3_b], writes=[st2_b])
                P.op("act", lambda e: e.activation(out=st2[:, :], in_=st2[:, :], func=AF.Sqrt, bias=epsc[:, :]),
                     reads=[st2_b, eps_b], writes=[st2_b])
                P.op("dve", lambda e: e.reciprocal(out=st2[:, :], in_=st2[:, :]), reads=[st2_b], writes=[st2_b])
                if _RWB_STOP == "g4":
                    return
                yn3 = yn[:, :].rearrange("p (h x) -> p h x", h=RH)
                P.op("dve", lambda e: e.tensor_tensor(out=yn3, in0=yn3, in1=st1[:, :].unsqueeze(2).to_broadcast([RN, RH, RN]),
                                                      op=ALU.subtract), reads=[yn_b, st1_b], writes=[yn_b])
                P.op("dve", lambda e: e.tensor_tensor(out=yn3, in0=yn3, in1=st2[:, :].unsqueeze(2).to_broadcast([RN, RH, RN]),
                                                      op=ALU.mult), reads=[yn_b, st2_b], writes=[yn_b])
                if _RWB_STOP == "gn":
                    return
                P.op("dve", lambda e, gc=gc: e.tensor_tensor(out=Sf[:, :, :], in0=Sf[:, :, :],
                                                             in1=PC[:, :, gc:gc + 1].to_broadcast([RN, RH, RN]), op=ALU.mult),
                     reads=[Sf_b, PC_b], writes=[Sf_b])
                P.op("dve", lambda e, pS=pS: e.tensor_tensor(out=Sf[:, :, :].rearrange("p h x -> p (h x)"),
                                                             in0=Sf[:, :, :].rearrange("p h x -> p (h x)"), in1=pS[0:RN, 0:HW],
                                                             op=ALU.add), reads=[pSb, Sf_b], writes=[Sf_b])
                P.op("act", lambda e: e.copy(out=Sb[:, :, :], in_=Sf[:, :, :]), reads=[Sf_b], writes=[Sb_b])
                if _RWB_STOP == "st":
                    return
                po, pob = pstile()
                for h in range(RH):
                    P.op("pe", lambda e, po=po, h=h: e.transpose(out=po[0:RN, hs_(h)], in_=yn[:, hs_(h)], identity=ident[:, :]),
                         reads=[yn_b, ident_b], writes=[pob])
                po3 = po[0:RN, 0:HW].rearrange("p (h x) -> p h x", h=RH)
                P.op("dve", lambda e, po3=po3, og_t=og_t: e.tensor_tensor(out=og_t[:, :, tc], in0=po3, in1=T2[:, :, tc], op=ALU.mult),
                     reads=[pob, T2_b], writes=[og_b])
                P.op("dve", lambda e, og_t=og_t: e.tensor_tensor(out=og_t[:, :, tc], in0=og_t[:, :, tc], in1=T1[:, :, tc], op=ALU.add),
                     reads=[og_b, T1_b], writes=[og_b])
            P.dma("sp", out3[:, :, t0:t0 + RB_TN], og_t[:, :, :], reads=[og_b])
        if env is None:
            P.finish([b for _, b in ostg])
            P.emit()
    return nc


def rwkv_consts():
    bones = np.zeros((128, 128), np.float32)
    bones[0:64, 0:64] = 1.0
    bones[64:128, 64:128] = 1.0
    rmask = np.ones((128, NT), np.float32)
    rmask[:, ::CH] = 0.0
    s = np.arange(64)[:, None]
    t = np.arange(64)[None, :]
    su = (s < t).astype(np.float32)
    iu = (s <= t).astype(np.float32)
    sl = (t < s).astype(np.float32)
    mask5 = np.concatenate([su, iu, su, iu, sl], axis=1).astype(np.float32)
    ident = np.eye(64, dtype=np.float32)
    return bones, rmask, mask5, ident


def run_rwkv(x, g1, p):
    nca = _get_nc(("rwa",), build_rwkv_a)
    bones, rmask, mask5, ident = rwkv_consts()
    in_maps = []
    for c in range(NCORES):
        b, hg = c // 2, c % 2
        own = slice(hg * RF, (hg + 1) * RF)
        vec = np.zeros((128, 114 + 56), np.float32)
        vec[:, 0:16] = pack_vec(g1)
        vec[:, 16] = RMS_EPS
        for i in range(6):
            vec[:, 18 + 16 * i:18 + 16 * (i + 1)] = pack_vec(p["mu"][i])
        ownv = [p["w0"], p["a0"], p["k_k"], p["k_a"], p["gn_g"], p["gn_b"], np.asarray(p["r_k"]).reshape(-1)]
        for i, v in enumerate(ownv):
            vec[:, 114 + 8 * i:114 + 8 * (i + 1)] = pack_vec(np.asarray(v)[own])
        in_maps.append({
            "xT": _f32(x[b].T),
            "wrkv": _f32(np.concatenate([p["w_rkv"][0][:, own], p["w_rkv"][1][:, own], p["w_rkv"][2][:, own]], axis=1)),
            "w1": _f32(p["w1"]), "w2": _f32(p["w2"][:, own]), "a1": _f32(p["a1"]), "a2": _f32(p["a2"][:, own]),
            "g1": _f32(p["g1"]), "g2": _f32(p["g2"][:, own]), "vec": vec, "bones": bones, "rmask": rmask})
    resa = run_bass_kernel_spmd(nca, in_maps, core_ids=list(range(NCORES))).results
    ncb = _get_nc(("rwb",), build_rwkv_b)
    in_maps = []
    for c in range(NCORES):
        r = resa[c]

        def fm(a):
            return _f32(a.reshape(RH, RN, -1).transpose(1, 0, 2).reshape(RN, -1))
        m = {n: fm(r[n]) for n in ("At", "Kt", "Bt", "Rt", "T1", "T2", "Vb", "Kh", "Bh")}
        m["PC"] = fm(r["PC"])
        m["mask5"] = mask5
        m["ident"] = ident
        m["eps"] = np.full((RN, 1), GN_EPS, np.float32)
        in_maps.append(m)
    resb = run_bass_kernel_spmd(ncb, in_maps, core_ids=list(range(NCORES))).results
    y = np.empty((B, T, D), np.float32)
    for c in range(NCORES):
        b, hg = c // 2, c % 2
        o = resb[c]["out"].reshape(RN, RH, T)
        y[b, :, hg * RF:(hg + 1) * RF] = o.transpose(2, 1, 0).reshape(T, RF)
    return y


def kernel_unfused(x, norm1_g, norm2_g, mlp_w1, mlp_w2,
           cc_w_in, cc_b_in, cc_dw, cc_dw_b, cc_ln_g, cc_ln_b, cc_w_out, cc_b_out,
           rw_mu, rw_w_rkv, rw_w0, rw_w1, rw_w2, rw_a0, rw_a1, rw_a2, rw_g1, rw_g2,
           rw_k_k, rw_k_a, rw_r_k, rw_gn_g, rw_gn_b, rw_w_o,
           sc_w_in, sc_conv_w, sc_w_out,
           fx_w_qkvf, fx_b_f, fx_w_o, final_g):
    A = lambda a: np.asarray(a, dtype=np.float32)
    x = A(x)
    n1, n2, gf = A(norm1_g), A(norm2_g), A(final_g)
    w1, w2 = A(mlp_w1), A(mlp_w2)
    mp = dict(w_in=_f32(cc_w_in[0]), b_in=A(cc_b_in[0]), dw=A(cc_dw[0]), dw_b=A(cc_dw_b[0]), ln_g=A(cc_ln_g[0]),
              ln_b=A(cc_ln_b[0]), w_out=_f32(cc_w_out[0]), b_out=A(cc_b_out[0]))
    x = run_tok("cc", False, x, n1[0], n2[0], gf, _f32(w1[0]), _f32(w2[0]), mp)
    p = dict(mu=A(rw_mu[0]), w_rkv=A(rw_w_rkv[0]), w0=A(rw_w0[0]), w1=A(rw_w1[0]), w2=A(rw_w2[0]), a0=A(rw_a0[0]),
             a1=A(rw_a1[0]), a2=A(rw_a2[0]), g1=A(rw_g1[0]), g2=A(rw_g2[0]), k_k=A(rw_k_k[0]), k_a=A(rw_k_a[0]),
             r_k=A(rw_r_k[0]), gn_g=A(rw_gn_g[0]), gn_b=A(rw_gn_b[0]))
    m = run_rwkv(x, n1[1], p)
    x = run_tok("post", False, x, n1[1], n2[1], gf, _f32(w1[1]), _f32(w2[1]), dict(w_out=_f32(rw_w_o[0])), m_in=m)
    mp = dict(w_in=_f32(sc_w_in[0]), conv_w=A(sc_conv_w[0]), w_out=_f32(sc_w_out[0]))
    x = run_tok("sc", False, x, n1[2], n2[2], gf, _f32(w1[2]), _f32(w2[2]), mp)
    m = run_fox(x, n1[3], A(fx_w_qkvf[0]), A(fx_b_f[0]))
    x = run_tok("post", True, x, n1[3], n2[3], gf, _f32(w1[3]), _f32(w2[3]), dict(w_out=_f32(fx_w_o[0])), m_in=m)
    return x


def build_fused(T=T):
    nc = bass.Bass("TRN2", target_bir_lowering=False)
    ext = lambda n, s: nc.dram_tensor(n, list(s), F32, kind="ExternalInput").ap()
    internal = lambda n, s: nc.dram_tensor(n, list(s), F32).ap()
    xT = ext("xT", [D, T])
    zh = ext("zh", [D, NH])
    out = nc.dram_tensor("out", [D, T // 2], F32, kind="ExternalOutput").ap()
    w1 = [ext("w1_%d" % l, [D, DFF]) for l in range(4)]
    w2 = [ext("w2_%d" % l, [DFF, D]) for l in range(4)]
    vec = [ext("vec_%d" % l, [128, nv]) for l, nv in enumerate((50 + 593, 50, 50 + 49, 52))]
    cc_w_in = ext("cc_w_in", [D, 2 * D]); cc_w_out = ext("cc_w_out", [D, D])
    sc_w_in = ext("sc_w_in", [D, 3 * D]); sc_w_out = ext("sc_w_out", [D, D])
    rw_w_o = ext("rw_w_o", [D, D]); fx_w_o = ext("fx_w_o", [D, D])
    rw = []
    for h in range(2):
        rw.append(dict(wrkv=ext("rw_wrkv_%d" % h, [D, 3 * RF]), w2=ext("rw_w2_%d" % h, [96, RF]), a2=ext("rw_a2_%d" % h, [96, RF]),
                       g2=ext("rw_g2_%d" % h, [256, RF]), vec=ext("rw_vec_%d" % h, [128, 170])))
    rw_w1 = ext("rw_w1", [D, 96]); rw_a1 = ext("rw_a1", [D, 96]); rw_g1 = ext("rw_g1", [D, 256])
    bones = ext("bones", [128, 128]); rmask = ext("rmask", [128, NT]); mask5 = ext("mask5", [RN, 320])
    ident64 = ext("ident64", [RN, RN]); gneps = ext("gneps", [RN, 1])
    fx = []
    for h in range(2):
        fx.append(dict(wall=ext("fx_wall_%d" % h, [D, 3 * FH * FDH]), wf=ext("fx_wf_%d" % h, [D, FH]), vec=ext("fx_vec_%d" % h, [128, 20])))
    ident128 = ext("ident128", [128, 128]); fmask = ext("fmask", [128, 896]); sel = ext("sel", [FH, FH * 128])
    X1 = internal("X1", [D, T]); X2 = internal("X2", [D, T]); X3 = internal("X3", [D, T])
    YG = internal("YG", [D, T]); OO = internal("OO", [D, T])
    RA = {n: internal("RA_" + n, [RF, T]) for n in RA_OUTS}
    RA_PC = internal("RA_PC", [RF, T // CH])

    with contextlib.ExitStack() as st:
        env = Env(nc, st)
        env.io = dict(xT=xT, out=X1, w1=w1[0], w2=w2[0], xh=zh, w_in=cc_w_in, w_out=cc_w_out, vec=vec[0])
        build_tok("cc", False, ntok=T, env=env)
        for h in range(2):
            env.io = dict(xT=X1, wrkv=rw[h]["wrkv"], w1=rw_w1, w2=rw[h]["w2"], a1=rw_a1, a2=rw[h]["a2"], g1=rw_g1, g2=rw[h]["g2"],
                          vec=rw[h]["vec"], bones=bones, rmask=rmask, PC=RA_PC, **{n: RA[n] for n in RA_OUTS})
            build_rwkv_a(T, env=env)
            io = {n: RA[n].rearrange("(h j) t -> j h t", j=RN) for n in RA_OUTS}
            io.update(PC=RA_PC.rearrange("(h j) c -> j h c", j=RN), mask5=mask5, ident=ident64, eps=gneps,
                      out=YG[h * RF:(h + 1) * RF, :].rearrange("(h i) t -> i h t", i=RN))
            env.io = io
            build_rwkv_b(T, env=env)
        env.io = dict(xT=X1, out=X2, w1=w1[1], w2=w2[1], mT=YG, w_out=rw_w_o, vec=vec[1])
        build_tok("post", False, ntok=T, env=env)
        env.io = dict(xT=X2, out=X3, w1=w1[2], w2=w2[2], xh=zh, w_in=sc_w_in, w_out=sc_w_out, vec=vec[2])
        build_tok("sc", False, ntok=T, env=env)
        for h in range(2):
            env.io = dict(xT=X3, wall=fx[h]["wall"], wf=fx[h]["wf"], vec=fx[h]["vec"], ident=ident128, mask=fmask, sel=sel,
                          out=OO[h * FH * FDH:(h + 1) * FH * FDH, :])
            build_fox(T, env=env)
        env.io = dict(xT=X3, out=out, w1=w1[3], w2=w2[3], mT=OO, w_out=fx_w_o, vec=vec[3])
        build_tok("post", True, ntok=T, env=env, split=True)
        env.cx.P.barrier()
        env.cx.P.emit()
    return nc


def fused_inputs(x_bT, p, half=0):
    A = lambda a: np.asarray(a, dtype=np.float32)
    Tn = x_bT.shape[0]
    m = {"xT": _f32(x_bT.T), "zh": np.zeros((D, NH), np.float32)}
    eps2 = [np.full((128, 1), RMS_EPS, np.float32), np.full((128, 1), LN_EPS, np.float32)]
    zero1 = np.zeros((128, 1), np.float32)
    gf = pack_vec(p["final_g"])
    for l in range(4):
        m["w1_%d" % l] = _f32(p["mlp_w1"][l])
        m["w2_%d" % l] = _f32(p["mlp_w2"][l])
    base = lambda l: [pack_vec(p["norm1_g"][l]), pack_vec(p["norm2_g"][l]), gf] + eps2
    m["vec_0"] = _f32(np.concatenate(base(0) + [pack_vec(p["cc_b_in"][0])] + [pack_vec(p["cc_dw"][0][k]) for k in range(31)] +
                                     [pack_vec(p["cc_dw_b"][0]), pack_vec(p["cc_ln_g"][0]), pack_vec(p["cc_ln_b"][0]),
                                      pack_vec(p["cc_b_out"][0]), zero1], axis=1))
    m["vec_1"] = _f32(np.concatenate(base(1), axis=1))
    m["vec_2"] = _f32(np.concatenate(base(2) + [pack_vec(p["sc_conv_w"][0][k]) for k in range(3)] + [zero1], axis=1))
    m["vec_3"] = _f32(np.concatenate(base(3) + [np.full((128, 1), 1.0 - half, np.float32), np.full((128, 1), float(half), np.float32)],
                                     axis=1))
    m["cc_w_in"] = _f32(p["cc_w_in"][0]); m["cc_w_out"] = _f32(p["cc_w_out"][0])
    m["sc_w_in"] = _f32(p["sc_w_in"][0]); m["sc_w_out"] = _f32(p["sc_w_out"][0])
    m["rw_w_o"] = _f32(p["rw_w_o"][0]); m["fx_w_o"] = _f32(p["fx_w_o"][0])
    bones, rmask, mask5, ident64 = rwkv_consts()
    m.update(bones=bones, rmask=rmask, mask5=mask5, ident64=ident64, gneps=np.full((RN, 1), GN_EPS, np.float32))
    m["rw_w1"] = _f32(p["rw_w1"][0]); m["rw_a1"] = _f32(p["rw_a1"][0]); m["rw_g1"] = _f32(p["rw_g1"][0])
    wr = A(p["rw_w_rkv"][0])
    for h in range(2):
        own = slice(h * RF, (h + 1) * RF)
        m["rw_wrkv_%d" % h] = _f32(np.concatenate([wr[0][:, own], wr[1][:, own], wr[2][:, own]], axis=1))
        m["rw_w2_%d" % h] = _f32(p["rw_w2"][0][:, own]); m["rw_a2_%d" % h] = _f32(p["rw_a2"][0][:, own])
        m["rw_g2_%d" % h] = _f32(p["rw_g2"][0][:, own])
        vec = np.zeros((128, 170), np.float32)
        vec[:, 0:16] = pack_vec(p["norm1_g"][1]); vec[:, 16] = RMS_EPS
        for i in range(6):
            vec[:, 18 + 16 * i:18 + 16 * (i + 1)] = pack_vec(p["rw_mu"][0][i])
        ownv = [p["rw_w0"][0], p["rw_a0"][0], p["rw_k_k"][0], p["rw_k_a"][0], p["rw_gn_g"][0], p["rw_gn_b"][0],
                A(p["rw_r_k"][0]).reshape(-1)]
        for i, v in enumerate(ownv):
            vec[:, 114 + 8 * i:114 + 8 * (i + 1)] = pack_vec(A(v)[own])
        m["rw_vec_%d" % h] = vec
    ident128, fmask, sel = fox_consts()
    m.update(ident128=ident128, fmask=fmask, sel=sel)
    wq = A(p["fx_w_qkvf"][0])
    for h in range(2):
        h0 = h * FH
        cols = slice(h0 * FDH, (h0 + FH) * FDH)
        m["fx_wall_%d" % h] = _f32(np.concatenate([wq[:, 0:D][:, cols], wq[:, D:2 * D][:, cols], wq[:, 2 * D:3 * D][:, cols]], axis=1))
        m["fx_wf_%d" % h] = _f32(wq[:, 3 * D + h0:3 * D + h0 + FH])
        vec = np.zeros((128, 20), np.float32)
        vec[:, 0:16] = pack_vec(p["norm1_g"][3]); vec[:, 16] = RMS_EPS; vec[:, 17] = 1.0
        vec[0:FH, 18] = A(p["fx_b_f"][0])[h0:h0 + FH]
        m["fx_vec_%d" % h] = vec
    return m


def kernel_fused(**p):
    x = np.asarray(p["x"], dtype=np.float32)
    nc = _get_nc(("fused",), build_fused)
    shared = None
    in_maps = []
    for c in range(NCORES):
        b = c // 2
        if shared is None:
            shared = fused_inputs(x[b], p)
            m = shared
        else:
            m = dict(shared)
            m["xT"] = _f32(x[b].T)
        if c % 2 == 1:
            v3 = m["vec_3"].copy()
            v3[:, 50] = 0.0
            v3[:, 51] = 1.0
            m["vec_3"] = v3
        in_maps.append(m)
    res = run_bass_kernel_spmd(nc, in_maps, core_ids=list(range(NCORES))).results
    y = np.empty((B, T, D), np.float32)
    for c in range(NCORES):
        b, half = c // 2, c % 2
        y[b, half * (T // 2):(half + 1) * (T // 2)] = res[c]["out"].T
    return y


def kernel(**inputs):
    return kernel_fused(**inputs)
```
